# Optimizing a Trainium2 kernel written in Bass

```python
import jax, jax.numpy as jnp
from jax import lax
import numpy as np


D_MODEL = 1024
BATCH = 8
SEQ = 2048
DEPTH = 2

N_MEM = 256
RWKV_HEADS = 8
RWKV_HEAD_DIM = 64
RWKV_W = RWKV_HEADS * RWKV_HEAD_DIM
W_LORA = 64
A_LORA = 64
G_LORA = 128
RWKV_IN = 3 * RWKV_W + W_LORA + A_LORA + G_LORA
RWKV_LN_EPS = RWKV_HEAD_DIM * 1e-5
LRU_BLOCKS = 8
LRU_W = 512
LRU_BLOCK_DIM = LRU_W // LRU_BLOCKS
CONV_WIDTH = 4
LRU_C = 8.0
MLA_HEADS = 8
MLA_NOPE = 64
MLA_ROPE = 32
MLA_QK = MLA_NOPE + MLA_ROPE
MLA_V = 64
Q_RANK = 256
KV_RANK = 128
ROPE_THETA = 10000.0
Q_BLOCK = 128
N_BRANCH = 3
BRANCH_W = 512
XA_HEADS = 4
XA_HEAD_DIM = 128
XA_W = XA_HEADS * XA_HEAD_DIM
D_FF = -(-8 * D_MODEL // (3 * 256)) * 256
LRU_OFF = RWKV_IN
MLA_OFF = LRU_OFF + 2 * LRU_W
GATE_OFF = MLA_OFF + Q_RANK + KV_RANK + MLA_ROPE
D_IN = GATE_OFF + N_BRANCH * D_MODEL

kernel_name = 'hybrid_rwkv7_rglru_mla_gated_block'


def rms_norm(x, g, eps=1e-6):
    xf = x.astype(jnp.float32)
    y = xf * lax.rsqrt(jnp.mean(xf * xf, axis=-1, keepdims=True) + eps)
    return (y * g.astype(jnp.float32)).astype(x.dtype)


def rope(x, cos, sin):
    x1, x2 = jnp.split(x, 2, axis=-1)
    return jnp.concatenate([x1 * cos - x2 * sin, x2 * cos + x1 * sin], axis=-1)


def token_shift(p):
    return jnp.pad(p, ((0, 0), (1, 0), (0, 0)))[:, :-1]


def rwkv7_scan(r, w, k, v, kk, kka):
    B, S, H, N = r.shape

    def step(state, inp):
        r_t, w_t, k_t, v_t, kk_t, kka_t = inp
        sa = jnp.einsum('bhvk,bhk->bhv', state, kk_t)
        state = (state * w_t[:, :, None, :] - sa[..., None] * kka_t[:, :, None, :]
                 + v_t[..., None] * k_t[:, :, None, :])
        return state, jnp.einsum('bhvk,bhk->bhv', state, r_t)

    xs = tuple(jnp.swapaxes(t, 0, 1) for t in (r, w, k, v, kk, kka))
    s0 = jnp.zeros((B, H, N, N), jnp.float32)
    _, ys = lax.scan(step, s0, xs)
    return jnp.swapaxes(ys, 0, 1)


def rwkv7_mix(p, mu, w0, w_up, a0, a_up, g_up, k_k, k_a, r_k, ln_g, ln_b):
    B, S, _ = p.shape
    p = p + (token_shift(p) - p) * mu
    o1, o2, o3 = RWKV_W, 2 * RWKV_W, 3 * RWKV_W
    r, k, v, wd, ad, gd = jnp.split(p, [o1, o2, o3, o3 + W_LORA, o3 + W_LORA + A_LORA], axis=-1)
    w_raw = -jax.nn.softplus(-(w0 + jnp.tanh(wd) @ w_up).astype(jnp.float32)) - 0.5
    decay = jnp.exp(-jnp.exp(w_raw))
    a = jax.nn.sigmoid(a0 + ad @ a_up)
    g = jax.nn.sigmoid(gd) @ g_up
    heads = lambda t: t.reshape(B, S, RWKV_HEADS, RWKV_HEAD_DIM)
    kk = heads(k * k_k).astype(jnp.float32)
    kk = kk / jnp.maximum(jnp.sqrt(jnp.sum(kk * kk, axis=-1, keepdims=True)), 1e-12)
    k = k * (1.0 + (a - 1.0) * k_a)
    r, k, v, decay, a = heads(r), heads(k), heads(v), heads(decay), heads(a)
    y = rwkv7_scan(r, decay, k, v, kk, kk * a)
    mean = jnp.mean(y, axis=-1, keepdims=True)
    var = jnp.mean(jnp.square(y - mean), axis=-1, keepdims=True)
    y = ((y - mean) * lax.rsqrt(var + RWKV_LN_EPS)).reshape(B, S, RWKV_W) * ln_g + ln_b
    bonus = jnp.sum(r * k * r_k, axis=-1, keepdims=True) * v
    y = y + bonus.reshape(B, S, RWKV_W)
    return (y * g).astype(p.dtype)


def _lin_combine(e1, e2):
    a1, b1 = e1
    a2, b2 = e2
    return a1 * a2, a2 * b1 + b2


def rglru_mix(xb, gb, conv_w, conv_b, wa, ba, wx, bx, lam):
    B, S, _ = xb.shape
    xc = lax.conv_general_dilated(xb, conv_w[:, None, :].astype(xb.dtype), window_strides=(1,),
                                  padding=[(CONV_WIDTH - 1, 0)],
                                  dimension_numbers=('NWC', 'WIO', 'NWC'),
                                  feature_group_count=LRU_W) + conv_b
    xh = xc.reshape(B, S, LRU_BLOCKS, LRU_BLOCK_DIM)
    r = jax.nn.sigmoid(jnp.einsum('bsni,nij->bsnj', xh, wa).reshape(B, S, LRU_W) + ba)
    i = jax.nn.sigmoid(jnp.einsum('bsni,nij->bsnj', xh, wx).reshape(B, S, LRU_W) + bx)
    log_a = -LRU_C * r.astype(jnp.float32) * jax.nn.softplus(-lam.astype(jnp.float32))
    a = jnp.exp(log_a)
    u = jnp.sqrt(-jnp.expm1(2.0 * log_a)) * (i * xc)
    _, h = lax.associative_scan(_lin_combine, (a, u), axis=1)
    return (h * jax.nn.gelu(gb)).astype(xb.dtype)


def causal_attention(q, k, v):
    B, S, H, Dk = q.shape
    nb = S // Q_BLOCK
    scale = Dk ** -0.5
    qb = q.reshape(B, nb, Q_BLOCK, H, Dk).transpose(1, 0, 2, 3, 4)
    kpos = jnp.arange(S)

    def block(args):
        qi, bi = args
        qpos = bi * Q_BLOCK + jnp.arange(Q_BLOCK)
        s = jnp.einsum('bqhd,bkhd->bhqk', qi, k, preferred_element_type=jnp.float32) * scale
        s = jnp.where(kpos[None, :] <= qpos[:, None], s, -jnp.inf)
        pr = jax.nn.softmax(s, axis=-1)
        return jnp.einsum('bhqk,bkhd->bqhd', pr.astype(v.dtype), v)

    o = lax.map(block, (qb, jnp.arange(nb)))
    return o.transpose(1, 0, 2, 3, 4).reshape(B, S, H, v.shape[-1])


def mla_mix(cq, ckv, kr, cos, sin, q_norm, w_uq, kv_norm, w_ukv, q_gain, k_gain):
    B, S, _ = cq.shape
    q = (rms_norm(cq, q_norm) @ w_uq).reshape(B, S, MLA_HEADS, MLA_QK)
    kv = (rms_norm(ckv, kv_norm) @ w_ukv).reshape(B, S, MLA_HEADS, MLA_NOPE + MLA_V)
    k_nope, v = jnp.split(kv, [MLA_NOPE], axis=-1)
    k = jnp.concatenate([k_nope, jnp.broadcast_to(kr[:, :, None, :], (B, S, MLA_HEADS, MLA_ROPE))], axis=-1)
    q = rms_norm(q, q_gain)
    k = rms_norm(k, k_gain)
    q = jnp.concatenate([q[..., :MLA_NOPE], rope(q[..., MLA_NOPE:], cos, sin).astype(q.dtype)], axis=-1)
    k = jnp.concatenate([k[..., :MLA_NOPE], rope(k[..., MLA_NOPE:], cos, sin).astype(k.dtype)], axis=-1)
    return causal_attention(q, k, v).reshape(B, S, MLA_HEADS * MLA_V)


def memory_xattn(h, m, w_q, w_kv, q_gain, k_gain, w_o):
    B, S, _ = h.shape
    M = m.shape[1]
    q = rms_norm((h @ w_q).reshape(B, S, XA_HEADS, XA_HEAD_DIM), q_gain)
    kv = (m @ w_kv).reshape(B, M, XA_HEADS, 2 * XA_HEAD_DIM)
    k, v = jnp.split(kv, 2, axis=-1)
    k = rms_norm(k, k_gain)
    s = jnp.einsum('bqhd,bkhd->bhqk', q, k, preferred_element_type=jnp.float32) * XA_HEAD_DIM ** -0.5
    pr = jax.nn.softmax(s, axis=-1)
    o = jnp.einsum('bhqk,bkhd->bqhd', pr.astype(v.dtype), v).reshape(B, S, XA_W)
    return o @ w_o


def swiglu(h, w1, w3, w2):
    return (jax.nn.silu(h @ w1) * (h @ w3)) @ w2


def setup_inputs(seed: int = 0) -> dict:
    key = jax.random.key(seed)
    ks = iter(jax.random.split(key, 64))

    def nrm(shape, scale):
        return jax.random.normal(next(ks), shape, jnp.float32) * scale

    def gain(shape):
        return 1.0 + nrm(shape, 0.1)

    L, D = DEPTH, D_MODEL
    x = nrm((BATCH, SEQ, D), 1.0)
    mem = nrm((BATCH, N_MEM, D), 1.0)
    positions = (jnp.arange(SEQ, dtype=jnp.int32)[None, :]
                 + jax.random.randint(next(ks), (BATCH, 1), 0, 4096, dtype=jnp.int32))
    a_c = jax.random.uniform(next(ks), (L, LRU_W), jnp.float32, 0.9, 0.999)
    a_base = a_c ** (1.0 / LRU_C)
    lru_lambda = jnp.log(a_base) - jnp.log1p(-a_base)
    return {
        'x': x,
        'mem': mem,
        'positions': positions,
        'norm_mix': gain((L, D)),
        'norm_xattn': gain((L, D)),
        'norm_mem': gain((L, D)),
        'norm_ffn': gain((L, D)),
        'w_in': nrm((L, D, D_IN), D ** -0.5),
        'b_gate': nrm((L, N_BRANCH * D), 0.1),
        'rwkv_mu': jax.random.uniform(next(ks), (L, RWKV_IN), jnp.float32),
        'rwkv_w0': nrm((L, RWKV_W), 0.5) - 0.5,
        'rwkv_w_up': nrm((L, W_LORA, RWKV_W), 0.1),
        'rwkv_a0': nrm((L, RWKV_W), 0.5),
        'rwkv_a_up': nrm((L, A_LORA, RWKV_W), A_LORA ** -0.5),
        'rwkv_g_up': nrm((L, G_LORA, RWKV_W), G_LORA ** -0.5),
        'rwkv_k_k': gain((L, RWKV_W)),
        'rwkv_k_a': gain((L, RWKV_W)),
        'rwkv_r_k': nrm((L, RWKV_HEADS, RWKV_HEAD_DIM), 0.1),
        'rwkv_ln_g': gain((L, RWKV_W)),
        'rwkv_ln_b': nrm((L, RWKV_W), 0.01),
        'lru_conv_w': nrm((L, CONV_WIDTH, LRU_W), CONV_WIDTH ** -0.5),
        'lru_conv_b': nrm((L, LRU_W), 0.01),
        'lru_wa': nrm((L, LRU_BLOCKS, LRU_BLOCK_DIM, LRU_BLOCK_DIM), LRU_BLOCK_DIM ** -0.5),
        'lru_ba': nrm((L, LRU_W), 0.01),
        'lru_wx': nrm((L, LRU_BLOCKS, LRU_BLOCK_DIM, LRU_BLOCK_DIM), LRU_BLOCK_DIM ** -0.5),
        'lru_bx': nrm((L, LRU_W), 0.01),
        'lru_lambda': lru_lambda,
        'mla_q_norm': gain((L, Q_RANK)),
        'mla_w_uq': nrm((L, Q_RANK, MLA_HEADS * MLA_QK), Q_RANK ** -0.5),
        'mla_kv_norm': gain((L, KV_RANK)),
        'mla_w_ukv': nrm((L, KV_RANK, MLA_HEADS * (MLA_NOPE + MLA_V)), KV_RANK ** -0.5),
        'mla_q_gain': gain((L, MLA_QK)),
        'mla_k_gain': gain((L, MLA_QK)),
        'w_branch': nrm((L, N_BRANCH, BRANCH_W, D), BRANCH_W ** -0.5),
        'w_out': nrm((L, D, D), D ** -0.5),
        'xa_w_q': nrm((L, D, XA_W), D ** -0.5),
        'xa_w_kv': nrm((L, D, 2 * XA_W), D ** -0.5),
        'xa_q_gain': gain((L, XA_HEAD_DIM)),
        'xa_k_gain': gain((L, XA_HEAD_DIM)),
        'xa_w_o': nrm((L, XA_W, D), XA_W ** -0.5),
        'ffn_w1': nrm((L, D, D_FF), D ** -0.5),
        'ffn_w3': nrm((L, D, D_FF), D ** -0.5),
        'ffn_w2': nrm((L, D_FF, D), D_FF ** -0.5),
    }


def reference(x, mem, positions, norm_mix, norm_xattn, norm_mem, norm_ffn, w_in, b_gate,
              rwkv_mu, rwkv_w0, rwkv_w_up, rwkv_a0, rwkv_a_up, rwkv_g_up, rwkv_k_k, rwkv_k_a,
              rwkv_r_k, rwkv_ln_g, rwkv_ln_b, lru_conv_w, lru_conv_b, lru_wa, lru_ba, lru_wx,
              lru_bx, lru_lambda, mla_q_norm, mla_w_uq, mla_kv_norm, mla_w_ukv, mla_q_gain,
              mla_k_gain, w_branch, w_out, xa_w_q, xa_w_kv, xa_q_gain, xa_k_gain, xa_w_o,
              ffn_w1, ffn_w3, ffn_w2):
    B, S, _ = x.shape
    inv_freq = ROPE_THETA ** (-jnp.arange(0, MLA_ROPE, 2, dtype=jnp.float32) / MLA_ROPE)
    ang = positions.astype(jnp.float32)[..., None] * inv_freq
    cos = jnp.cos(ang)[:, :, None, :]
    sin = jnp.sin(ang)[:, :, None, :]
    for l in range(DEPTH):
        h = rms_norm(x, norm_mix[l])
        p = h @ w_in[l]
        y_a = rwkv7_mix(p[..., :RWKV_IN], rwkv_mu[l], rwkv_w0[l], rwkv_w_up[l], rwkv_a0[l],
                        rwkv_a_up[l], rwkv_g_up[l], rwkv_k_k[l], rwkv_k_a[l], rwkv_r_k[l],
                        rwkv_ln_g[l], rwkv_ln_b[l])
        y_b = rglru_mix(p[..., LRU_OFF:LRU_OFF + LRU_W], p[..., LRU_OFF + LRU_W:MLA_OFF],
                        lru_conv_w[l], lru_conv_b[l], lru_wa[l], lru_ba[l], lru_wx[l], lru_bx[l],
                        lru_lambda[l])
        y_c = mla_mix(p[..., MLA_OFF:MLA_OFF + Q_RANK],
                      p[..., MLA_OFF + Q_RANK:MLA_OFF + Q_RANK + KV_RANK],
                      p[..., MLA_OFF + Q_RANK + KV_RANK:GATE_OFF], cos, sin,
                      mla_q_norm[l], mla_w_uq[l], mla_kv_norm[l], mla_w_ukv[l],
                      mla_q_gain[l], mla_k_gain[l])
        gates = jax.nn.sigmoid(p[..., GATE_OFF:] + b_gate[l]).reshape(B, S, N_BRANCH, D_MODEL)
        branches = jnp.stack([y_a, y_b, y_c], axis=2)
        proj = jnp.einsum('bsnc,ncd->bsnd', branches, w_branch[l])
        merged = jnp.sum(gates * proj, axis=2)
        x = x + merged @ w_out[l]
        x = x + memory_xattn(rms_norm(x, norm_xattn[l]), rms_norm(mem, norm_mem[l]),
                             xa_w_q[l], xa_w_kv[l], xa_q_gain[l], xa_k_gain[l], xa_w_o[l])
        x = x + swiglu(rms_norm(x, norm_ffn[l]), ffn_w1[l], ffn_w3[l], ffn_w2[l])
    return x
```

```python
import contextlib
import os
import sys
import numpy as np
import concourse.bass as bass
import concourse.mybir as mybir
from concourse.bass_utils import run_bass_kernel_spmd

F32 = mybir.dt.float32
BF16 = mybir.dt.bfloat16
I32 = mybir.dt.int32
AF = mybir.ActivationFunctionType
ALU = mybir.AluOpType
AX = mybir.AxisListType

_ESZ = {F32: 4, BF16: 2, I32: 4}

D = 1024
SEQ = 2048
NT = 16
KD = 8
DEPTH = 2
N_MEM = 256
RW = 512
RWKV_IN = 1792
LRU_OFF = 1792
MLA_OFF = 2816
GATE_OFF = 3232
D_IN = 6304
D_FF = 2816
EPS = 1e-6
TWO_PI = 6.283185307179586

PV = {}
_o = 0
for _n, _w in [('g_mix', 8), ('g_xa', 8), ('g_mem', 8), ('g_ffn', 8), ('b_gate', 24), ('mu', 14),
               ('w0', 4), ('a0', 4), ('k_k', 4), ('k_a', 4), ('r_k', 4), ('conv_w', 16),
               ('conv_b', 4), ('ba', 4), ('bx', 4), ('lam', 4), ('q_norm', 2), ('kv_norm', 1),
               ('xa_qg', 1), ('xa_kg', 1)]:
    PV[_n] = (_o, _w)
    _o += _w
NPV = _o
PR = {'ln_g': (0, 512), 'ln_b': (512, 512), 'q_gain': (1024, 96), 'k_gain': (1120, 96)}
NPR = 1216
CS = {}
_o = 0
for _n, _w in [('ident', 128), ('identblk', 64), ('mask_s', 64), ('maskT_s', 64), ('cmask', 128),
               ('blockones', 128), ('headsel', 2), ('chunkmask', 256), ('invfreq', 16), ('ones', 128)]:
    CS[_n] = (_o, _w)
    _o += _w
NCS = _o


def _box(ap):
    t = ap.tensor
    esz = _ESZ[ap.dtype]
    apl = ap.ap
    off = int(ap.offset) * esz
    sp = str(ap.space)
    if sp == 'PSUM':
        return (t.name, 0, 127, 0, 1 << 20)
    if sp != 'DRAM':
        row = esz
        for s in t.shape[1:]:
            row *= int(s)
        pstep, pcnt = apl[0]
        p0 = off // row
        f0 = off % row
        pn = (pstep * esz) // row if pcnt > 1 else 1
        p1 = p0 + (pcnt - 1) * max(pn, 0)
        ext = 0
        for s, c in apl[1:]:
            ext += (c - 1) * abs(s)
        return (t.name, p0, p1, f0, f0 + (ext + 1) * esz)
    ext = 0
    for s, c in apl:
        ext += (c - 1) * abs(s)
    return (t.name, 0, 0, off, off + (ext + 1) * esz)


def _ovl(a, b):
    return a[1] <= b[2] and b[1] <= a[2] and a[3] < b[4] and b[3] < a[4]


def _covers(a, b):
    return a[1] <= b[1] and a[2] >= b[2] and a[3] <= b[3] and a[4] >= b[4]


class Op:
    __slots__ = ('eng', 'fn', 'deps', 'signal', 'sem', 'val', 'is_dma', 'idx', 'waits', 'key', 'clock', 'line', 'rows', 'odeps', 'cost', 'succ', 'npend')


class Sched:
    def __init__(self, nc):
        self.nc = nc
        self.ops = []
        self.recs = {}
        self.wk = 0
        self.serial_w = True
        self.wprev = None

    def add(self, eng, fn, reads=(), writes=(), dma_key=None, extra_deps=(), force=False, rows=None, cost=None):
        lim = os.environ.get('MAXOPS')
        if lim is not None and len(self.ops) >= int(lim) and not force:
            return None
        extra_deps = [d for d in extra_deps if d is not None]
        op = Op()
        op.line = (sys._getframe(2).f_lineno, sys._getframe(1).f_code.co_name)
        op.eng = eng
        op.rows = rows
        op.fn = fn
        op.is_dma = dma_key is not None
        op.key = dma_key
        op.signal = op.is_dma
        op.idx = len(self.ops)
        deps = {}
        rb = [_box(a) for a in reads]
        wb = [_box(a) for a in writes]
        for b in rb:
            psum = b[4] == (1 << 20)
            for rec in self.recs.get(b[0], ()):
                if _ovl(rec[0], b):
                    if rec[2]:
                        deps[rec[1]] = deps.get(rec[1], 0) | 1
                    elif psum:
                        deps[rec[1]] = deps.get(rec[1], 0) | 4
        for b in wb:
            for rec in self.recs.get(b[0], ()):
                if _ovl(rec[0], b):
                    deps[rec[1]] = deps.get(rec[1], 0) | 2
        for d in extra_deps:
            deps[d.idx] = 1
        need = []
        op.odeps = [self.ops[di] for di, kind in deps.items() if kind != 4]
        if cost is None:
            fd = 1
            if writes:
                for s_ in writes[0].shape[1:]:
                    fd *= int(s_)
            if op.is_dma:
                cost = 2000 + fd * 128 * 4 / 250.0
            elif eng == 'pe':
                cost = 70 + max(fd, 64) * 0.45
            elif eng == 'act':
                cost = 200 + fd * 0.85
            else:
                cost = 80 + fd * 1.05
        op.cost = cost
        for di, kind in deps.items():
            d = self.ops[di]
            if not d.is_dma and not op.is_dma and d.eng == eng:
                if kind == 4:
                    continue
                if eng == 'pe':
                    if not ((kind & 2) and rows is not None and d.rows is not None and
                            (rows[1] <= d.rows[0] or d.rows[1] <= rows[0])):
                        continue
            d.signal = True
            need.append(d)
        op.deps = need
        have = {d.idx for d in op.odeps}
        for d in need:
            if d.idx not in have:
                op.odeps.append(d)
        self.ops.append(op)
        for b in wb:
            lst = self.recs.setdefault(b[0], [])
            lst[:] = [r for r in lst if not _covers(b, r[0])]
            lst.append((b, op.idx, True))
        for b in rb:
            lst = self.recs.setdefault(b[0], [])
            lst.append((b, op.idx, False))
        return op

    def matmul(self, out, lhsT, rhs, start=True, stop=True):
        b0 = _box(lhsT)
        return self.add('pe', lambda e: e.matmul(out, lhsT, rhs, start=start, stop=stop),
                        reads=[lhsT, rhs], writes=[out], rows=(b0[1], b0[2] + 1))

    def act(self, out, in_, func, bias=None, scale=None, accum_out=None):
        kw = {}
        reads = [in_]
        writes = [out]
        if bias is not None:
            kw['bias'] = bias
            if not isinstance(bias, (int, float)):
                reads.append(bias)
        if scale is not None:
            kw['scale'] = scale
            if not isinstance(scale, (int, float)):
                reads.append(scale)
        if accum_out is not None:
            kw['accum_out'] = accum_out
            writes.append(accum_out)
        return self.add('act', lambda e: e.activation(out, in_, func, **kw), reads, writes)

    def tt(self, eng, out, in0, in1, op):
        return self.add(eng, lambda e: e.tensor_tensor(out, in0, in1, op), [in0, in1], [out])

    def ts(self, eng, out, in0, s1, s2, op0, op1=None):
        reads = [in0]
        for s in (s1, s2):
            if s is not None and not isinstance(s, (int, float)):
                reads.append(s)
        if op1 is None:
            return self.add(eng, lambda e: e.tensor_scalar(out, in0, s1, None, op0), reads, [out])
        return self.add(eng, lambda e: e.tensor_scalar(out, in0, s1, s2, op0, op1), reads, [out])

    def stt(self, eng, out, in0, scalar, in1, op0, op1):
        reads = [in0, in1]
        if not isinstance(scalar, (int, float)):
            reads.append(scalar)
        return self.add(eng, lambda e: e.scalar_tensor_tensor(out, in0, scalar, in1, op0, op1),
                        reads, [out])

    def copy(self, eng, out, in_):
        if eng == 'act':
            return self.add('act', lambda e: e.copy(out, in_), [in_], [out])
        return self.add(eng, lambda e: e.tensor_copy(out, in_), [in_], [out])

    def reduce(self, eng, out, in_, op):
        return self.add(eng, lambda e: e.tensor_reduce(out, in_, AX.X, op), [in_], [out])

    def recip(self, out, in_):
        return self.add('dve', lambda e: e.reciprocal(out, in_), [in_], [out])

    def memset(self, eng, out, val):
        return self.add(eng, lambda e: e.memset(out, val), [], [out])

    def scan(self, out, d0, d1, initial, op0, op1):
        reads = [d0, d1]
        if not isinstance(initial, (int, float)):
            reads.append(initial)
        return self.add('dve', lambda e: e.tensor_tensor_scan(out, d0, d1, initial, op0, op1),
                        reads, [out])

    def dma(self, eng, out, in_, key=None, force=False):
        NQ = 8
        self.dk = getattr(self, 'dk', 0) + 1
        k = '%s%d' % (eng, self.dk % NQ)
        if not hasattr(self, 'dprev'):
            self.dprev = {}
        prev = [self.dprev[k]] if k in self.dprev else []
        op = self.add(eng, lambda e: e.dma_start(out=out, in_=in_), [in_], [out], dma_key=k,
                      extra_deps=prev, force=force)
        if op is not None:
            self.dprev[k] = op
        return op

    def wdma(self, out, in_):
        NW = 1
        self.wk = (self.wk + 1) % NW
        k = 'w%d' % self.wk
        if not hasattr(self, 'wprevs'):
            self.wprevs = {}
        prev = [self.wprevs[k]] if k in self.wprevs else []
        op = self.add('pool', lambda e: e.dma_start(out=out, in_=in_), [in_], [out],
                      dma_key=k, extra_deps=prev)
        if op is not None:
            self.wprevs[k] = op
        return op

    def fence(self, eng, deps):
        return self.add(eng, None, extra_deps=deps, force=True)

    def schedule(self):
        import heapq
        ops = self.ops
        for op in ops:
            op.succ = []
        for op in ops:
            uniq = {d.idx: d for d in op.odeps}
            op.odeps = list(uniq.values())
            op.npend = len(op.odeps)
            for d in op.odeps:
                d.succ.append(op)
        fin = [0.0] * len(ops)
        engs = ['pe', 'act', 'dve', 'pool', 'sp']
        free = {e: 0.0 for e in engs}
        wait_h = {e: [] for e in engs}
        rdy_h = {e: [] for e in engs}
        WIN = int(os.environ.get('SCHED_WIN', '1000'))

        def push(op):
            rt = 0.0
            for d in op.odeps:
                lat = 100.0 if (d.eng == op.eng and not d.is_dma) else 250.0
                rt = max(rt, fin[d.idx] + lat)
            heapq.heappush(wait_h[op.eng], (rt, op.idx))
        for op in ops:
            if op.npend == 0:
                push(op)
        order = []
        nsched = 0
        lowest = 0
        done = [False] * len(ops)
        while nsched < len(ops):
            best = None
            for e in engs:
                wh, rh = wait_h[e], rdy_h[e]
                while wh and wh[0][0] <= free[e]:
                    heapq.heappush(rh, heapq.heappop(wh)[1])
                if rh:
                    cand = (free[e], rh[0], e, True)
                elif wh:
                    cand = (wh[0][0], wh[0][1], e, False)
                else:
                    continue
                if cand[1] > lowest + WIN:
                    cand = (cand[0] + 1e9, cand[1], e, cand[3])
                if best is None or cand[:2] < best[:2]:
                    best = cand
            st_, idx, e, from_rdy = best
            if st_ >= 1e9:
                st_ -= 1e9
            if from_rdy:
                heapq.heappop(rdy_h[e])
            else:
                heapq.heappop(wait_h[e])
            op = ops[idx]
            start = max(st_, free[e])
            fin[idx] = start + op.cost
            free[e] = fin[idx] if not op.is_dma else start + 60.0
            order.append(op)
            done[idx] = True
            nsched += 1
            while lowest < len(ops) and done[lowest]:
                lowest += 1
            for s2 in op.succ:
                s2.npend -= 1
                if s2.npend == 0:
                    push(s2)
        self.est_time = max(fin) if fin else 0.0
        return order

    def emit(self):
        nc = self.nc
        ops = self.schedule() if os.environ.get('NOSCHED') is None else self.ops
        engs = ['pe', 'act', 'dve', 'pool', 'sp']
        keys = []
        for op in ops:
            if op.is_dma and op.signal and op.key not in keys:
                keys.append(op.key)
        with contextlib.ExitStack() as st:
            sems = {}
            for e in engs:
                sems[e] = st.enter_context(nc.semaphore('s_' + e))
            for k in keys:
                sems[('dma', k)] = st.enter_context(nc.semaphore('d_' + str(k)))
            cnt = {}
            clock = {e: {} for e in engs}
            for op in ops:
                ck = clock[op.eng]
                need = {}
                for d in op.deps:
                    if ck.get(d.sem, 0) < d.val:
                        need[d.sem] = max(need.get(d.sem, 0), d.val)
                for d in op.deps:
                    for s, v in d.clock.items():
                        if ck.get(s, 0) < v:
                            ck[s] = v
                    if ck.get(d.sem, 0) < d.val:
                        ck[d.sem] = d.val
                op.waits = list(need.items())
                if op.signal:
                    if op.is_dma:
                        sk = ('dma', op.key)
                        cnt[sk] = cnt.get(sk, 0) + 16
                    else:
                        sk = op.eng
                        cnt[sk] = cnt.get(sk, 0) + 1
                    op.sem = sk
                    op.val = cnt[sk]
                    op.clock = dict(ck)
                else:
                    op.sem = None
                    op.val = 0
                    op.clock = None
            per = {e: [o for o in ops if o.eng == e] for e in engs}

            def run(e_name):
                def body(eng):
                    for op in per[e_name]:
                        for s, v in op.waits:
                            eng.wait_ge(sems[s], v)
                        if op.fn is None:
                            continue
                        ins = op.fn(eng)
                        if op.signal:
                            ins.then_inc(sems[op.sem], 16 if op.is_dma else 1)
                return body

            with nc.Block() as block:
                block.tensor(run('pe'))
                block.scalar(run('act'))
                block.vector(run('dve'))
                block.gpsimd(run('pool'))
                block.sync(run('sp'))


class Arena:
    def __init__(self, t, words):
        self.t = t
        self.off = 0
        self.words = words
        self.peak = 0

    def mark(self):
        return self.off

    def release(self, m):
        self.off = m

    def alloc(self, shape, dtype):
        n = 1
        for s in shape:
            n *= s
        esz = _ESZ[dtype]
        w = (n * esz + 3) // 4
        w = (w + 7) // 8 * 8
        assert self.off + w <= self.words, ('arena overflow', self.off, w, self.words)
        v = self.t[:, self.off:self.off + w]
        self.off += w
        self.peak = max(self.peak, self.off)
        if dtype != F32:
            v = v.bitcast(dtype)
        v = v[:, 0:n]
        if len(shape) == 2:
            v = v.rearrange("p (a b) -> p a b", b=shape[1])
        elif len(shape) == 3:
            v = v.rearrange("p (a b c) -> p a b c", b=shape[1], c=shape[2])
        return v


def bc(ap, axis, shape):
    return ap.unsqueeze(axis).to_broadcast(shape)


def build(nc, layers=DEPTH, taps=(), phases=('rwkv', 'lru', 'mla', 'merge', 'xattn', 'ffn')):
    dr = lambda n, s, d=F32, k="ExternalInput": nc.dram_tensor(n, s, d, kind=k).ap()
    x_d = dr("x", [SEQ, D])
    mem_d = dr("mem", [N_MEM, D])
    pos_d = dr("pos", [128, NT], I32)
    cst_d = dr("cst", [128, NCS])
    pvec_d = dr("pvec", [DEPTH, 128, NPV])
    prow_d = dr("prow", [DEPTH, 128, NPR])
    w_in_d = dr("w_in", [DEPTH, D, D_IN])
    w_up_d = dr("rwkv_w_up", [DEPTH, 64, RW])
    a_up_d = dr("rwkv_a_up", [DEPTH, 64, RW])
    g_up_d = dr("rwkv_g_up", [DEPTH, 128, RW])
    wa_d = dr("lru_wa", [DEPTH, 8, 64, 64])
    wx_d = dr("lru_wx", [DEPTH, 8, 64, 64])
    w_uq_d = dr("mla_w_uq", [DEPTH, 256, 768])
    w_ukv_d = dr("mla_w_ukv", [DEPTH, 128, 1024])
    w_br_d = dr("w_branch", [DEPTH, 3, 512, D])
    w_out_d = dr("w_out", [DEPTH, D, D])
    xa_wq_d = dr("xa_w_q", [DEPTH, D, 512])
    xa_wkv_d = dr("xa_w_kv", [DEPTH, D, D])
    xa_wo_d = dr("xa_w_o", [DEPTH, 512, D])
    w1_d = dr("ffn_w1", [DEPTH, D, D_FF])
    w3_d = dr("ffn_w3", [DEPTH, D, D_FF])
    w2_d = dr("ffn_w2", [DEPTH, D_FF, D])
    out_d = dr("out", [SEQ, D], F32, "ExternalOutput")
    tap_d = {}
    for name, shape in taps:
        tap_d[name] = dr("tap_" + name, shape, F32, "ExternalOutput")

    ARW = 24800
    with contextlib.ExitStack() as st:
        sb = lambda n, s, d: st.enter_context(nc.sbuf_tensor(n, s, d))
        xres = sb("xres", [128, NT, D], F32)
        hT = sb("hT", [128, KD, SEQ], BF16)
        cst = sb("cstf", [128, NCS], F32)
        cb = sb("cstb", [128, 512], BF16)
        pvec = sb("pvec_s", [128, DEPTH, NPV], F32)
        prow = sb("prow_s", [128, NPR], F32)
        omu = sb("omu", [128, 32], F32)
        npv = sb("npv", [128, DEPTH, NPV], F32)
        cos_t = sb("cos_t", [128, NT, 16], F32)
        sin_t = sb("sin_t", [128, NT, 16], F32)
        art = sb("arena", [128, ARW], F32)
        pst = [st.enter_context(nc.psum_tensor("ps%d" % i, [128, 512], F32)) for i in range(8)]
        S = Sched(nc)
        A = Arena(art, ARW)
        psi = [0]

        def PS():
            psi[0] = (psi[0] + 1) % 6
            return pst[psi[0]]

        def PSA(i):
            return pst[6 + i]

        def C(name, rows=slice(0, 128)):
            o, w = CS[name]
            return cst[rows, o:o + w]

        def PVc(l, name, j=0, n=1, rows=slice(0, 128)):
            o, w = PV[name]
            return pvec[rows, l, o + j:o + j + n]

        def NPVc(l, name, j=0, n=1):
            o, w = PV[name]
            return npv[:, l, o + j:o + j + n]

        def PRr(name, rows=slice(0, 128)):
            o, w = PR[name]
            return prow[rows, o:o + w]

        ident_b = cb[:, 0:128]
        bones_b = cb[:, 128:256]
        ones_b = cb[:, 256:384]
        hsel_b = cb[:, 384:386]
        tapn = [0]

        def tap(name, dst_ap, src_ap):
            if name not in tap_d:
                return
            tapn[0] += 1
            outs.append(S.dma('sp', dst_ap, src_ap, 'tap'))

        outs = []
        evn = [0]

        def evac(out, in_, scale_ap=None):
            evn[0] += 1
            if evn[0] % 2 == 0:
                if scale_ap is None:
                    S.copy('act', out, in_)
                else:
                    S.act(out, in_, AF.Copy, scale=scale_ap)
            else:
                if scale_ap is None:
                    S.copy('dve', out, in_)
                else:
                    S.ts('dve', out, in_, scale_ap, None, ALU.mult)

        S.dma('sp', cst[:, :], cst_d, 'c0')
        S.dma('sp', pvec[:, :, :], pvec_d.rearrange("l p c -> p l c"), 'c1')
        for t in range(NT):
            S.dma('sp', xres[:, t, :], x_d[t * 128:(t + 1) * 128, :], 'x%d' % (t % 4))
        S.ts('dve', npv[:, :, :], pvec[:, :, :], -1.0, None, ALU.mult)
        S.copy('dve', ident_b, C('ident'))
        S.copy('dve', bones_b, C('blockones'))
        S.copy('dve', ones_b, C('ones'))
        S.copy('dve', hsel_b, C('headsel'))
        m0 = A.mark()
        pi_ = A.alloc([NT], I32)
        pf = A.alloc([NT], F32)
        ang = A.alloc([NT, 16], F32)
        kf = A.alloc([NT, 16], F32)
        ki = A.alloc([NT, 16], I32)
        S.dma('sp', pi_, pos_d, 'c2')
        S.copy('dve', pf, pi_)
        S.tt('dve', ang, bc(pf, 2, [128, NT, 16]), bc(C('invfreq'), 1, [128, NT, 16]), ALU.mult)
        for (dst, shift) in ((sin_t, 0.0), (cos_t, TWO_PI / 4)):
            a2 = ang
            if shift != 0.0:
                a2 = A.alloc([NT, 16], F32)
                S.ts('dve', a2, ang, shift, None, ALU.add)
            S.ts('dve', kf, a2, 1.0 / TWO_PI, None, ALU.mult)
            S.copy('dve', ki, kf)
            S.copy('dve', kf, ki)
            S.stt('dve', kf, kf, -TWO_PI, a2, ALU.mult, ALU.add)
            S.ts('dve', kf, kf, 3.14159, -3.14159, ALU.min, ALU.max)
            S.act(dst[:, :, :], kf, AF.Sin)
        A.release(m0)

        def rsqrt_(out, in_):
            S.act(out, in_, AF.Ln)
            S.act(out, out, AF.Exp, scale=-0.5)

        def sigmoid_(out, in_, nbias=None, scale=1.0, tmp_=None):
            t_ = out if tmp_ is None else tmp_
            if nbias is None:
                S.act(t_, in_, AF.Exp, scale=-scale)
            else:
                S.act(t_, in_, AF.Exp, scale=-scale, bias=nbias)
            S.act(t_, t_, AF.Ln, bias=1.0)
            S.act(out, t_, AF.Exp, scale=-1.0)

        def rms_to_hT(l, gname):
            m = A.mark()
            junk = A.alloc([D], BF16)
            ss = A.alloc([NT], F32)
            rstd = A.alloc([NT], F32)
            xn = A.alloc([4, D], BF16)
            for t in range(NT):
                S.act(junk, xres[:, t, :], AF.Square, accum_out=ss[:, t:t + 1])
            S.ts('dve', rstd, ss, 1.0 / D, EPS, ALU.mult, ALU.add)
            rsqrt_(rstd, rstd)
            for tg in range(4):
                for j in range(4):
                    t = tg * 4 + j
                    S.ts('dve', xn[:, j, :], xres[:, t, :], rstd[:, t:t + 1], None, ALU.mult)
                for kc in range(KD):
                    p = PS()
                    for j in range(4):
                        S.matmul(p[:, j * 128:(j + 1) * 128], xn[:, j, kc * 128:(kc + 1) * 128], ident_b)
                    evac(hT[:, kc, tg * 512:(tg + 1) * 512], p[:, :], PVc(l, gname, kc))
            A.release(m)

        def merge(l, n, yT):
            m = A.mark()
            gp = A.alloc([KD, SEQ], BF16)
            wout = A.alloc([KD, D], BF16)
            wg = [A.alloc([KD, 512], BF16) for _ in range(2)]
            wb_ = [A.alloc([4, 512], BF16) for _ in range(2)]
            gt = [A.alloc([512], F32) for _ in range(2)]
            for fq in range(2):
                c0 = GATE_OFF + n * D + fq * 512
                S.wdma(wg[fq], w_in_d[l, :, c0:c0 + 512].rearrange("(a p) c -> p a c", p=128))
                S.wdma(wb_[fq], w_br_d[l, n, :, fq * 512:(fq + 1) * 512].rearrange("(a p) c -> p a c", p=128))
            for h2 in range(2):
                S.wdma(wout[:, :, h2 * 512:(h2 + 1) * 512],
                       w_out_d[l, :, h2 * 512:(h2 + 1) * 512].rearrange("(a p) c -> p a c", p=128))
            it = 0
            for fq in range(2):
                for f in range(4):
                    fo = fq * 4 + f
                    for tc in range(4):
                        ts_ = slice(tc * 512, (tc + 1) * 512)
                        pg = PS()
                        for kc in range(KD):
                            S.matmul(pg[:, :], wg[fq][:, kc, f * 128:(f + 1) * 128], hT[:, kc, ts_],
                                     start=(kc == 0), stop=(kc == KD - 1))
                        pp = PS()
                        for kc in range(4):
                            S.matmul(pp[:, :], wb_[fq][:, kc, f * 128:(f + 1) * 128], yT[:, kc, ts_],
                                     start=(kc == 0), stop=(kc == 3))
                        g = gt[it % 2]
                        it += 1
                        S.act(g, pg[:, :], AF.Sigmoid, bias=PVc(l, 'b_gate', n * 8 + fo))
                        S.tt('dve', gp[:, fo, ts_], g, pp[:, :], ALU.mult)
            for t in range(NT):
                for h2 in range(2):
                    p = PS()
                    for f in range(KD):
                        S.matmul(p[:, :], gp[:, f, t * 128:(t + 1) * 128], wout[:, f, h2 * 512:(h2 + 1) * 512],
                                 start=(f == 0), stop=(f == KD - 1))
                    S.tt('dve', xres[:, t, h2 * 512:(h2 + 1) * 512], xres[:, t, h2 * 512:(h2 + 1) * 512],
                         p[:, :], ALU.add)
            A.release(m)

        def rwkv(l, yT):
            m = A.mark()
            G = 256
            NG = SEQ // G
            wbuf = [A.alloc([KD, 128], BF16) for _ in range(3)]
            wup = A.alloc([RW], BF16)
            gup = A.alloc([RW], BF16)
            carry = A.alloc([16], F32)
            pf_ = A.alloc([G + 1], F32)
            tmp = [A.alloc([G], F32) for _ in range(12)]
            wdad = A.alloc([G], BF16)
            sg = A.alloc([G], BF16)
            sqb = A.alloc([G], BF16)
            qtT, ktT, btT, kapT, vT, KendT, nBendT, rkrT = [A.alloc([4, G], BF16) for _ in range(8)]
            pc = A.alloc([4, 4], F32)
            Dg = A.alloc([4, 4, 64], BF16)
            tm2 = [[A.alloc([512], BF16) for _ in range(4)] for _ in range(2)]
            Mb = [A.alloc([512], BF16) for _ in range(3)]
            MTb = [A.alloc([512], BF16) for _ in range(3)]
            TTb = [A.alloc([512], BF16) for _ in range(3)]
            A2 = [[A.alloc([512], BF16) for _ in range(5)] for _ in range(2)]
            kaph, qhT, GT = [A.alloc([512], BF16) for _ in range(3)]
            IMb = [A.alloc([512], BF16) for _ in range(2)]
            mrot = [0]
            cpar = [0]
            Sbf = A.alloc([512], BF16)
            ep = [A.alloc([512], F32) for _ in range(3)]
            st8 = [A.alloc([8], F32) for _ in range(4)]
            yo = A.alloc([512], BF16)
            S.wdma(wup[0:64, :], w_up_d[l])
            S.wdma(wup[64:128, :], a_up_d[l])
            S.wdma(gup, g_up_d[l])
            S.memset('dve', carry, 0.0)
            S.memset('dve', Sbf[0:64, :], 0.0)
            o_mu = PV['mu'][0]
            S.ts('dve', omu[:, 0:14], pvec[:, l, o_mu:o_mu + 14], -1.0, 1.0, ALU.mult, ALU.add)
            S.ts('dve', omu[:, 16:20], PVc(l, 'k_a', 0, 4), -1.0, 1.0, ALU.mult, ALU.add)
            mask_s3 = bc(C('mask_s', slice(0, 64)), 1, [64, 8, 64])
            maskT_s3 = bc(C('maskT_s', slice(0, 64)), 1, [64, 8, 64])
            maskT_i3 = bc(cst[0:64, CS['cmask'][0]:CS['cmask'][0] + 64], 1, [64, 8, 64])
            ident3 = bc(cst[0:64, CS['ident'][0]:CS['ident'][0] + 64], 1, [64, 8, 64])
            v3 = lambda ap: ap.rearrange("p (h c) -> p h c", c=64)
            wblk = [(0, 512), (512, 512), (1024, 512), (1536, 256)]
            wcur = [None, None]

            forder = [12, 13] + [blk * 4 + j for j in range(4) for blk in range(3)]
            uses = [fc for _ in range(NG) for fc in forder]
            wptr = [0, 0]

            def issue_w():
                i = wptr[0]
                if i < len(uses):
                    fc = uses[i]
                    S.wdma(wbuf[i % 3], w_in_d[l, :, fc * 128:(fc + 1) * 128].rearrange("(a p) c -> p a c", p=128))
                    wptr[0] += 1

            def next_slot():
                i = wptr[1]
                wptr[1] += 1
                issue_w()
                return i % 3

            issue_w()
            issue_w()

            def proj_mix(fc, g0, out_pm, slot, coff):
                p = PS()
                for kc in range(KD):
                    S.matmul(p[:, 0:G], wbuf[slot][:, kc, coff:coff + 128], hT[:, kc, g0:g0 + G],
                             start=(kc == 0), stop=(kc == KD - 1))
                S.copy('act', pf_[:, 0:1], carry[:, fc:fc + 1])
                S.copy('act', pf_[:, 1:G + 1], p[:, 0:G])
                S.copy('act', carry[:, fc:fc + 1], pf_[:, G:G + 1])
                t0 = tmp[11]
                S.ts('dve', t0, pf_[:, 0:G], PVc(l, 'mu', fc), None, ALU.mult)
                S.stt('dve', out_pm, pf_[:, 1:G + 1], omu[:, fc:fc + 1], t0, ALU.mult, ALU.add)

            for gi in range(NG):
                g0 = gi * G
                pm = tmp[0]
                proj_mix(12, g0, pm, next_slot(), 0)
                S.act(tmp[1][0:64, :], pm[0:64, :], AF.Exp, scale=2.0)
                S.ts('dve', tmp[1][0:64, :], tmp[1][0:64, :], 1.0, None, ALU.add)
                S.recip(tmp[1][0:64, :], tmp[1][0:64, :])
                S.ts('dve', wdad[0:64, :], tmp[1][0:64, :], -2.0, 1.0, ALU.mult, ALU.add)
                S.copy('dve', wdad[64:128, :], pm[64:128, :])
                proj_mix(13, g0, pm, next_slot(), 0)
                sigmoid_(sg, pm, tmp_=tmp[1])
                for j in range(4):
                    rm, km, vm = tmp[0], tmp[1], tmp[2]
                    for (blk, dst) in ((0, rm), (1, km), (2, vm)):
                        proj_mix(blk * 4 + j, g0, dst, next_slot(), 0)
                    jc = slice(j * 128, (j + 1) * 128)
                    pz = PS()
                    S.matmul(pz[:, 0:G], wup[0:64, jc], wdad[0:64, :])
                    lw = tmp[3]
                    sigmoid_(lw, pz[:, 0:G], nbias=NPVc(l, 'w0', j))
                    S.ts('dve', lw, lw, -0.6065306597126334, None, ALU.mult)
                    pa = PS()
                    S.matmul(pa[:, 0:G], wup[64:128, jc], wdad[64:128, :])
                    am = tmp[4]
                    sigmoid_(am, pa[:, 0:G], nbias=NPVc(l, 'a0', j))
                    kkr = tmp[5]
                    S.ts('dve', kkr, km, PVc(l, 'k_k', j), None, ALU.mult)
                    S.act(sqb, kkr, AF.Square)
                    pss = PS()
                    S.matmul(pss[:, 0:G], bones_b, sqb)
                    rn = tmp[6]
                    S.ts('dve', rn, pss[:, 0:G], 1e-24, None, ALU.max)
                    rsqrt_(rn, rn)
                    kk = tmp[5]
                    S.tt('dve', kk, kkr, rn, ALU.mult)
                    t1 = tmp[6]
                    S.ts('dve', t1, am, PVc(l, 'k_a', j), omu[:, 16 + j:17 + j], ALU.mult, ALU.add)
                    kp = tmp[7]
                    S.tt('dve', kp, km, t1, ALU.mult)
                    bb = tmp[8]
                    S.tt('dve', bb, kk, am, ALU.mult)
                    S.stt('dve', rkrT[:, j, :], rm, PVc(l, 'r_k', j), kp, ALU.mult, ALU.mult)
                    S.copy('act', vT[:, j, :], vm)
                    L = tmp[9]
                    S.scan(L, C('chunkmask'), lw, 0.0, ALU.mult, ALU.add)
                    L3 = L.rearrange("p (c t) -> p c t", t=64)
                    LC = L3[:, :, 63:64]
                    S.act(pc[:, j, :], L3[:, :, 63], AF.Exp)
                    eP = tmp[10]
                    S.act(eP, L, AF.Exp)
                    S.tt('dve', qtT[:, j, :], rm, eP, ALU.mult)
                    dK = tmp[10]
                    S.tt('dve', dK, L, lw, ALU.subtract)
                    S.act(dK, dK, AF.Exp)
                    S.tt('dve', kapT[:, j, :], kk, dK, ALU.mult)
                    eN = tmp[3]
                    S.act(eN, L, AF.Exp, scale=-1.0)
                    S.tt('dve', ktT[:, j, :], kp, eN, ALU.mult)
                    S.tt('dve', btT[:, j, :], bb, eN, ALU.mult)
                    eE = tmp[4]
                    S.tt('dve', eE.rearrange("p (c t) -> p c t", t=64), LC.to_broadcast([128, 4, 64]), L3,
                         ALU.subtract)
                    S.act(eE, eE, AF.Exp)
                    S.tt('dve', KendT[:, j, :], kp, eE, ALU.mult)
                    S.stt('dve', nBendT[:, j, :], bb, -1.0, eE, ALU.mult, ALU.mult)
                    S.tt('dve', Dg[:, j, :, :], bc(C('identblk'), 1, [128, 4, 64]),
                         bc(pc[:, j, :], 2, [128, 4, 64]), ALU.mult)
                for c in range(G // 64):
                    cs = slice(c * 64, (c + 1) * 64)
                    tok0 = g0 + c * 64
                    cpar[0] ^= 1
                    tm = tm2[cpar[0]]
                    AkkT, ArkT, nArbT, AV, UV = A2[cpar[0]]
                    for qi, X in enumerate((kapT, vT, KendT, nBendT)):
                        p = PS()
                        for j in range(4):
                            S.matmul(p[0:64, j * 128:(j + 1) * 128], X[:, j, cs], ident_b)
                        evac(tm[qi][0:64, :], p[0:64, :])
                    kap_tm, V_tm, Kend_tm, nBend_tm = [t_[0:64, :] for t_ in tm]
                    hb = lambda h: ((h % 2) * 64, h // 2)
                    HO = (0, 2, 4, 6, 1, 3, 5, 7)
                    pN = PS()
                    pNT = PS()
                    for h in HO:
                        b, j = hb(h)
                        S.matmul(pN[0:64, h * 64:(h + 1) * 64], kapT[b:b + 64, j, cs], btT[b:b + 64, j, cs])
                    for h in HO:
                        b, j = hb(h)
                        S.matmul(pNT[0:64, h * 64:(h + 1) * 64], btT[b:b + 64, j, cs], kapT[b:b + 64, j, cs])
                    mrot[0] = (mrot[0] + 1) % 3
                    cur = mrot[0]
                    M_, MT_, TT_ = Mb[cur][0:64, :], MTb[cur][0:64, :], TTb[cur][0:64, :]
                    S.stt('dve', v3(M_), v3(pN[0:64, :]), -1.0, mask_s3, ALU.mult, ALU.mult)
                    S.stt('dve', v3(MT_), v3(pNT[0:64, :]), -1.0, maskT_s3, ALU.mult, ALU.mult)
                    S.tt('dve', v3(TT_), v3(MT_), ident3, ALU.add)
                    for lev in range(5):
                        mrot[0] = (mrot[0] + 1) % 3
                        nx = mrot[0]
                        Mn, MTn, TTn = Mb[nx][0:64, :], MTb[nx][0:64, :], TTb[nx][0:64, :]
                        pM = PS()
                        for h in range(8):
                            hs = slice(h * 64, (h + 1) * 64)
                            S.matmul(pM[0:64, hs], MT_[:, hs], M_[:, hs])
                        if lev < 4:
                            pMT = PS()
                            for h in range(8):
                                hs = slice(h * 64, (h + 1) * 64)
                                S.matmul(pMT[0:64, hs], M_[:, hs], MT_[:, hs])
                        S.tt('dve', v3(IMb[lev % 2][0:64, :]), v3(pM[0:64, :]), ident3, ALU.add)
                        if lev < 4:
                            S.copy('act', Mn, pM[0:64, :])
                            S.copy('act', MTn, pMT[0:64, :])
                        pT = PS()
                        for h in range(8):
                            hs = slice(h * 64, (h + 1) * 64)
                            S.matmul(pT[0:64, hs], IMb[lev % 2][0:64, hs], TT_[:, hs])
                        evac(TTn, pT[0:64, :])
                        M_, MT_, TT_ = Mn, MTn, TTn
                    TT = TT_
                    pA1, pA2, pA3 = PS(), PS(), PS()
                    for h in HO:
                        b, j = hb(h)
                        hs = slice(h * 64, (h + 1) * 64)
                        S.matmul(pA1[0:64, hs], ktT[b:b + 64, j, cs], kapT[b:b + 64, j, cs])
                        S.matmul(pA2[0:64, hs], ktT[b:b + 64, j, cs], qtT[b:b + 64, j, cs])
                        S.matmul(pA3[0:64, hs], btT[b:b + 64, j, cs], qtT[b:b + 64, j, cs])
                    S.tt('dve', v3(AkkT[0:64, :]), v3(pA1[0:64, :]), maskT_s3, ALU.mult)
                    S.tt('dve', v3(ArkT[0:64, :]), v3(pA2[0:64, :]), maskT_i3, ALU.mult)
                    S.stt('dve', v3(nArbT[0:64, :]), v3(pA3[0:64, :]), -1.0, maskT_i3, ALU.mult, ALU.mult)
                    pAV = PS()
                    for h in range(8):
                        hs = slice(h * 64, (h + 1) * 64)
                        S.matmul(pAV[0:64, hs], AkkT[0:64, hs], V_tm[:, hs])
                    S.copy('act', AV[0:64, :], pAV[0:64, :])
                    pK = PS()
                    for h in range(8):
                        hs = slice(h * 64, (h + 1) * 64)
                        S.matmul(pK[0:64, hs], TT[:, hs], kap_tm[:, hs])
                    S.copy('dve', kaph[0:64, :], pK[0:64, :])
                    pUV = PS()
                    for h in range(8):
                        hs = slice(h * 64, (h + 1) * 64)
                        S.matmul(pUV[0:64, hs], TT[:, hs], AV[0:64, hs])
                    S.copy('act', UV[0:64, :], pUV[0:64, :])
                    pQ = PS()
                    for h in range(8):
                        b, j = hb(h)
                        hs = slice(h * 64, (h + 1) * 64)
                        S.matmul(pQ[0:64, hs], kaph[0:64, hs], nArbT[0:64, hs], start=True, stop=False)
                        S.matmul(pQ[0:64, hs], ident_b[b:b + 64, b:b + 64], qtT[b:b + 64, j, cs],
                                 start=False, stop=True)
                    S.copy('act', qhT[0:64, :], pQ[0:64, :])
                    pG = PS()
                    for h in range(8):
                        b, j = hb(h)
                        hs = slice(h * 64, (h + 1) * 64)
                        S.matmul(pG[0:64, hs], kaph[0:64, hs], nBend_tm[:, hs], start=True, stop=False)
                        S.matmul(pG[0:64, hs], Dg[b:b + 64, j, c, :], ident_b[b:b + 64, b:b + 64],
                                 start=False, stop=True)
                    S.copy('act', GT[0:64, :], pG[0:64, :])
                    pY = PS()
                    for h in range(8):
                        hs = slice(h * 64, (h + 1) * 64)
                        S.matmul(pY[0:64, hs], ArkT[0:64, hs], V_tm[:, hs], start=True, stop=False)
                        S.matmul(pY[0:64, hs], nArbT[0:64, hs], UV[0:64, hs], start=False, stop=False)
                        S.matmul(pY[0:64, hs], qhT[0:64, hs], Sbf[0:64, hs], start=False, stop=True)
                    pS_ = PS()
                    for h in range(8):
                        hs = slice(h * 64, (h + 1) * 64)
                        S.matmul(pS_[0:64, hs], Kend_tm[:, hs], V_tm[:, hs], start=True, stop=False)
                        S.matmul(pS_[0:64, hs], nBend_tm[:, hs], UV[0:64, hs], start=False, stop=False)
                        S.matmul(pS_[0:64, hs], GT[0:64, hs], Sbf[0:64, hs], start=False, stop=True)
                    ysum, var, rstd_, bsum = [s_[0:64, :] for s_ in st8]
                    yc, sq_, bon = [e_[0:64, :] for e_ in ep]
                    S.reduce('dve', ysum, v3(pY[0:64, :]), ALU.add)
                    S.ts('dve', ysum, ysum, 1.0 / 64, None, ALU.mult)
                    S.tt('dve', v3(yc), v3(pY[0:64, :]), bc(ysum, 2, [64, 8, 64]), ALU.subtract)
                    S.copy('act', Sbf[0:64, :], pS_[0:64, :])
                    S.tt('dve', sq_, yc, yc, ALU.mult)
                    S.reduce('dve', var, v3(sq_), ALU.add)
                    S.ts('dve', var, var, 1.0 / 64, 64e-5, ALU.mult, ALU.add)
                    rsqrt_(rstd_, var)
                    S.tt('dve', v3(yc), v3(yc), bc(rstd_, 2, [64, 8, 64]), ALU.mult)
                    S.tt('dve', yc, yc, PRr('ln_g', slice(0, 64)), ALU.mult)
                    S.tt('dve', yc, yc, PRr('ln_b', slice(0, 64)), ALU.add)
                    pB = PS()
                    for j in range(4):
                        S.matmul(pB[0:64, j * 2:(j + 1) * 2], rkrT[:, j, cs], hsel_b)
                    S.copy('act', bsum, pB[0:64, 0:8])
                    S.tt('dve', v3(bon), v3(V_tm), bc(bsum, 2, [64, 8, 64]), ALU.mult)
                    S.tt('dve', yc, yc, bon, ALU.add)
                    pGt = PS()
                    S.matmul(pGt[0:64, :], sg[:, cs], gup)
                    S.tt('dve', yo[0:64, :], yc, pGt[0:64, :], ALU.mult)
                    pYT = PS()
                    for j in range(4):
                        S.matmul(pYT[:, j * 64:(j + 1) * 64], yo[0:64, j * 128:(j + 1) * 128], ident_b[0:64, 0:64])
                    evac(yT[:, :, tok0:tok0 + 64], pYT[:, 0:256].rearrange("p (j t) -> p j t", t=64))
            A.release(m)

        def lru(l, yT):
            m = A.mark()
            wl = A.alloc([KD, D], BF16)
            bd = A.alloc([2, 128], BF16)
            c1 = A.alloc([4], F32)
            xbs = A.alloc([3 + 512], F32)
            xc = A.alloc([512], F32)
            xcb = A.alloc([512], BF16)
            rr = A.alloc([512], F32)
            ii = A.alloc([512], F32)
            aa = A.alloc([512], F32)
            uu = A.alloc([512], F32)
            hh = [A.alloc([512], F32) for _ in range(2)]
            ge = A.alloc([512], F32)
            for h2 in range(2):
                S.wdma(wl[:, :, h2 * 512:(h2 + 1) * 512],
                       w_in_d[l, :, LRU_OFF + h2 * 512:LRU_OFF + (h2 + 1) * 512].rearrange("(a p) c -> p a c", p=128))
            S.act(c1, PVc(l, 'lam', 0, 4), AF.Exp, scale=-1.0)
            S.act(c1, c1, AF.Ln, bias=1.0)
            S.ts('dve', c1, c1, -8.0, None, ALU.mult)
            for fc in range(4):
                S.memset('dve', bd[:, :, :], 0.0)
                for q, wd_ in enumerate((wa_d, wx_d)):
                    S.wdma(bd[0:64, q, 0:64], wd_[l, 2 * fc])
                    S.wdma(bd[64:128, q, 64:128], wd_[l, 2 * fc + 1])
                S.memset('dve', xbs[:, 0:3], 0.0)
                for tc in range(4):
                    ts_ = slice(tc * 512, (tc + 1) * 512)
                    px = PS()
                    for kc in range(KD):
                        S.matmul(px[:, :], wl[:, kc, fc * 128:(fc + 1) * 128], hT[:, kc, ts_],
                                 start=(kc == 0), stop=(kc == KD - 1))
                    pgb = PS()
                    for kc in range(KD):
                        S.matmul(pgb[:, :], wl[:, kc, 512 + fc * 128:512 + (fc + 1) * 128], hT[:, kc, ts_],
                                 start=(kc == 0), stop=(kc == KD - 1))
                    S.copy('act', xbs[:, 3:515], px[:, :])
                    S.copy('act', ge, pgb[:, :])
                    S.tt('dve', rr, ge, ge, ALU.mult)
                    S.ts('dve', rr, rr, 0.044715, 1.0, ALU.mult, ALU.add)
                    S.tt('dve', rr, rr, ge, ALU.mult)
                    sigmoid_(rr, rr, scale=1.5957691216057308)
                    S.tt('dve', ge, ge, rr, ALU.mult)
                    cw = lambda j: PVc(l, 'conv_w', j * 4 + fc)
                    S.ts('dve', xc, xbs[:, 0:512], cw(0), PVc(l, 'conv_b', fc), ALU.mult, ALU.add)
                    for j in range(1, 4):
                        S.stt('dve', xc, xbs[:, j:j + 512], cw(j), xc, ALU.mult, ALU.add)
                    S.copy('act', xbs[:, 0:3], xbs[:, 512:515])
                    S.copy('act', xcb, xc)
                    pr_ = PS()
                    S.matmul(pr_[:, :], bd[:, 0, :], xcb)
                    pi2 = PS()
                    S.matmul(pi2[:, :], bd[:, 1, :], xcb)
                    sigmoid_(rr, pr_[:, :], nbias=NPVc(l, 'ba', fc))
                    sigmoid_(ii, pi2[:, :], nbias=NPVc(l, 'bx', fc))
                    S.act(aa, rr, AF.Exp, scale=c1[:, fc:fc + 1])
                    S.tt('dve', uu, aa, aa, ALU.mult)
                    S.ts('dve', uu, uu, -1.0, 1.0, ALU.mult, ALU.add)
                    S.act(uu, uu, AF.Ln)
                    S.act(uu, uu, AF.Exp, scale=0.5)
                    S.tt('dve', ii, ii, xc, ALU.mult)
                    S.tt('dve', uu, uu, ii, ALU.mult)
                    hcur = hh[tc % 2]
                    init = 0.0 if tc == 0 else hh[(tc - 1) % 2][:, 511:512]
                    S.scan(hcur, aa, uu, init, ALU.mult, ALU.add)
                    S.tt('dve', yT[:, fc, ts_], hcur, ge, ALU.mult)
            A.release(m)

        def mla(l, yT):
            m = A.mark()
            cqnT = A.alloc([2, SEQ], BF16)
            ckvnT = A.alloc([SEQ], BF16)
            krs = A.alloc([NT, 32], F32)
            sskr = A.alloc([NT], F32)
            junk = A.alloc([256], F32)
            st4 = A.alloc([8], F32)
            m1 = A.mark()
            wm = A.alloc([KD, 416], BF16)
            cqn = A.alloc([384], BF16)
            S.wdma(wm, w_in_d[l, :, MLA_OFF:MLA_OFF + 416].rearrange("(a p) c -> p a c", p=128))
            for t in range(NT):
                p = PS()
                for kc in range(KD):
                    S.matmul(p[:, 0:416], hT[:, kc, t * 128:(t + 1) * 128], wm[:, kc, :],
                             start=(kc == 0), stop=(kc == KD - 1))
                S.act(junk[:, 0:256], p[:, 0:256], AF.Square, accum_out=st4[:, 0:1])
                S.act(junk[:, 0:128], p[:, 256:384], AF.Square, accum_out=st4[:, 1:2])
                S.copy('dve', krs[:, t, :], p[:, 384:416])
                S.act(junk[:, 0:32], p[:, 384:416], AF.Square, accum_out=sskr[:, t:t + 1])
                S.ts('dve', st4[:, 0:1], st4[:, 0:1], 1.0 / 256, EPS, ALU.mult, ALU.add)
                S.ts('dve', st4[:, 1:2], st4[:, 1:2], 1.0 / 128, EPS, ALU.mult, ALU.add)
                S.act(st4[:, 2:4], st4[:, 0:2], AF.Ln)
                S.act(st4[:, 4:6], st4[:, 2:4], AF.Exp, scale=-0.5)
                S.ts('dve', cqn[:, 0:256], p[:, 0:256], st4[:, 4:5], None, ALU.mult)
                S.ts('dve', cqn[:, 256:384], p[:, 256:384], st4[:, 5:6], None, ALU.mult)
                p2 = PS()
                for c in range(3):
                    S.matmul(p2[:, c * 128:(c + 1) * 128], cqn[:, c * 128:(c + 1) * 128], ident_b)
                for c in range(2):
                    evac(cqnT[:, c, t * 128:(t + 1) * 128], p2[:, c * 128:(c + 1) * 128], PVc(l, 'q_norm', c))
                evac(ckvnT[:, t * 128:(t + 1) * 128], p2[:, 256:384], PVc(l, 'kv_norm', 0))
            A.release(m1)
            bufs = [(A.alloc([2, 192], BF16), A.alloc([256], BF16), A.alloc([2, SEQ], BF16),
                     A.alloc([2, SEQ], BF16), A.alloc([NT, 2, 128], BF16)) for _ in range(2)]
            st16 = A.alloc([16], F32)
            qf = A.alloc([2, 96], F32)
            kfm = A.alloc([2, 96], F32)
            rt = [A.alloc([2, 16], F32) for _ in range(4)]
            qkb = A.alloc([4, 96], BF16)
            PT = [A.alloc([512], BF16) for _ in range(3)]
            rec = A.alloc([512], F32)
            qg3 = bc(PRr('q_gain'), 1, [128, 2, 96])
            kg3 = bc(PRr('k_gain'), 1, [128, 2, 96])
            scale = 96 ** -0.5
            for b_ in bufs:
                S.memset('dve', b_[4][:, :, :, :], 0.0)

            def tok_gen(pr):
                wq, wkv, QT, KT, Vp = bufs[pr % 2]
                S.wdma(wq, w_uq_d[l, :, pr * 192:(pr + 1) * 192].rearrange("(a p) c -> p a c", p=128))
                S.wdma(wkv, w_ukv_d[l, :, pr * 256:(pr + 1) * 256])
                for t in range(NT):
                    tsl = slice(t * 128, (t + 1) * 128)
                    pq = PS()
                    for c in range(2):
                        S.matmul(pq[:, 0:192], cqnT[:, c, tsl], wq[:, c, :], start=(c == 0), stop=(c == 1))
                    pk = PS()
                    S.matmul(pk[:, 0:256], ckvnT[:, tsl], wkv)
                    for hh_ in range(2):
                        S.act(junk[:, 0:96], pq[:, hh_ * 96:(hh_ + 1) * 96], AF.Square,
                              accum_out=st16[:, hh_:hh_ + 1])
                        S.act(junk[:, 0:64], pk[:, hh_ * 128:hh_ * 128 + 64], AF.Square,
                              accum_out=st16[:, 2 + hh_:3 + hh_])
                    S.ts('dve', st16[:, 2:4], st16[:, 2:4], sskr[:, t:t + 1], None, ALU.add)
                    S.ts('dve', st16[:, 0:4], st16[:, 0:4], 1.0 / 96, EPS, ALU.mult, ALU.add)
                    S.act(st16[:, 4:8], st16[:, 0:4], AF.Ln)
                    S.act(st16[:, 8:12], st16[:, 4:8], AF.Exp, scale=-0.5)
                    S.tt('dve', qf, pq[:, 0:192].rearrange("p (h c) -> p h c", c=96),
                         bc(st16[:, 8:10], 2, [128, 2, 96]), ALU.mult)
                    S.tt('dve', qf, qf, qg3, ALU.mult)
                    S.tt('dve', kfm[:, :, 0:64], pk[:, 0:256].rearrange("p (h c) -> p h c", c=128)[:, :, 0:64],
                         bc(st16[:, 10:12], 2, [128, 2, 64]), ALU.mult)
                    S.tt('dve', kfm[:, :, 64:96], bc(krs[:, t, :], 1, [128, 2, 32]),
                         bc(st16[:, 10:12], 2, [128, 2, 32]), ALU.mult)
                    S.tt('dve', kfm, kfm, kg3, ALU.mult)
                    S.copy('act', Vp[:, t, 0, 0:64], pk[:, 64:128])
                    S.copy('act', Vp[:, t, 1, 64:128], pk[:, 192:256])
                    cosb = bc(cos_t[:, t, :], 1, [128, 2, 16])
                    sinb = bc(sin_t[:, t, :], 1, [128, 2, 16])
                    for (src, o4) in ((qf, 0), (kfm, 2)):
                        x1 = src[:, :, 64:80]
                        x2 = src[:, :, 80:96]
                        S.copy('act', qkb[:, o4:o4 + 2, 0:64], src[:, :, 0:64])
                        S.tt('dve', rt[0], x1, cosb, ALU.mult)
                        S.tt('dve', rt[1], x2, sinb, ALU.mult)
                        S.tt('dve', rt[2], x2, cosb, ALU.mult)
                        S.tt('dve', rt[3], x1, sinb, ALU.mult)
                        S.tt('dve', qkb[:, o4:o4 + 2, 64:80], rt[0], rt[1], ALU.subtract)
                        S.tt('dve', qkb[:, o4:o4 + 2, 80:96], rt[2], rt[3], ALU.add)
                    pt_ = PS()
                    for i4 in range(4):
                        S.matmul(pt_[0:96, i4 * 128:(i4 + 1) * 128], qkb[:, i4, :], ident_b)
                    evac(QT[0:96, :, tsl], pt_[0:96, 0:256].rearrange("p (h t) -> p h t", t=128))
                    evac(KT[0:96, :, tsl], pt_[0:96, 256:512].rearrange("p (h t) -> p h t", t=128))
                    yield

            ibc = [0]

            def attn_gen(pr):
                wq, wkv, QT, KT, Vp = bufs[pr % 2]
                for qc in range(4):
                    po = PSA(0)
                    pd = PSA(1)
                    nkb = 4 * qc + 4
                    first = True
                    for hh_ in range(2):
                        for kb in range(nkb):
                            q0 = max(qc * 512, kb * 128)
                            n = (qc + 1) * 512 - q0
                            off = q0 - qc * 512
                            ps_ = PS()
                            S.matmul(ps_[:, 0:n], KT[0:96, hh_, kb * 128:(kb + 1) * 128], QT[0:96, hh_, q0:q0 + n])
                            pt2 = PT[ibc[0] % 3]
                            ibc[0] += 1
                            S.act(pt2[:, 0:n], ps_[:, 0:n], AF.Exp, scale=scale)
                            if kb * 128 >= qc * 512:
                                S.tt('dve', pt2[:, 0:128], pt2[:, 0:128], cmask_b, ALU.mult)
                            last = (hh_ == 1 and kb == nkb - 1)
                            S.matmul(po[:, off:off + n], Vp[:, kb, hh_, :], pt2[:, 0:n], start=first, stop=last)
                            S.matmul(pd[:, off:off + n], hpad_b[:, hh_, :], pt2[:, 0:n], start=first, stop=last)
                            first = False
                            yield
                    S.act(rec, pd[:, :], AF.Ln)
                    S.act(rec, rec, AF.Exp, scale=-1.0)
                    S.tt('dve', yT[:, pr, qc * 512:(qc + 1) * 512], po[:, :], rec, ALU.mult)
                    yield

            for _ in tok_gen(0):
                pass
            for pr in range(4):
                ga = attn_gen(pr)
                gt = tok_gen(pr + 1) if pr < 3 else None
                done_a = False
                done_t = gt is None
                while not (done_a and done_t):
                    for _ in range(5):
                        if not done_a:
                            try:
                                next(ga)
                            except StopIteration:
                                done_a = True
                    if not done_t:
                        try:
                            next(gt)
                        except StopIteration:
                            done_t = True
            A.release(m)

        def xattn(l):
            m = A.mark()
            rms_to_hT(l, 'g_xa')
            memf = A.alloc([2, D], F32)
            memn = A.alloc([2, D], BF16)
            memT = A.alloc([KD, N_MEM], BF16)
            ss = A.alloc([4], F32)
            junk = A.alloc([D], BF16)
            wkv = A.alloc([KD, D], BF16)
            wq = A.alloc([KD, 512], BF16)
            wo = A.alloc([4, D], BF16)
            KTn = A.alloc([4, N_MEM], BF16)
            V1 = A.alloc([4, 2, 128], BF16)
            sq = A.alloc([512], BF16)
            rs = A.alloc([512], F32)
            QTn = A.alloc([512], BF16)
            PT = [A.alloc([512], BF16) for _ in range(2)]
            oT = A.alloc([4, 512], BF16)
            for mt in range(2):
                S.dma('sp', memf[:, mt, :], mem_d[mt * 128:(mt + 1) * 128, :], 'mem')
            for h2 in range(2):
                S.wdma(wkv[:, :, h2 * 512:(h2 + 1) * 512],
                       xa_wkv_d[l, :, h2 * 512:(h2 + 1) * 512].rearrange("(a p) c -> p a c", p=128))
            S.wdma(wq, xa_wq_d[l].rearrange("(a p) c -> p a c", p=128))
            for h2 in range(2):
                S.wdma(wo[:, :, h2 * 512:(h2 + 1) * 512],
                       xa_wo_d[l, :, h2 * 512:(h2 + 1) * 512].rearrange("(a p) c -> p a c", p=128))
            for mt in range(2):
                S.act(junk, memf[:, mt, :], AF.Square, accum_out=ss[:, mt:mt + 1])
            S.ts('dve', ss[:, 0:2], ss[:, 0:2], 1.0 / D, EPS, ALU.mult, ALU.add)
            S.act(ss[:, 0:2], ss[:, 0:2], AF.Ln)
            S.act(ss[:, 2:4], ss[:, 0:2], AF.Exp, scale=-0.5)
            for mt in range(2):
                S.ts('dve', memn[:, mt, :], memf[:, mt, :], ss[:, 2 + mt:3 + mt], None, ALU.mult)
            for kc in range(KD):
                p = PS()
                for mt in range(2):
                    S.matmul(p[:, mt * 128:(mt + 1) * 128], memn[:, mt, kc * 128:(kc + 1) * 128], ident_b)
                evac(memT[:, kc, :], p[:, 0:256], PVc(l, 'g_mem', kc))

            def headnorm(p, n, gcol, out):
                S.act(sq[:, 0:n], p[:, 0:n], AF.Square)
                p2 = PS()
                S.matmul(p2[:, 0:n], ones_b, sq[:, 0:n])
                S.ts('dve', rs[:, 0:n], p2[:, 0:n], 1.0 / 128, EPS, ALU.mult, ALU.add)
                rsqrt_(rs[:, 0:n], rs[:, 0:n])
                S.stt('dve', out, p[:, 0:n], gcol, rs[:, 0:n], ALU.mult, ALU.mult)

            S.memset('dve', V1[:, :, :, :], 0.0)
            for h in range(4):
                p = PS()
                for kc in range(KD):
                    S.matmul(p[:, 0:N_MEM], wkv[:, kc, h * 256:h * 256 + 128], memT[:, kc, :],
                             start=(kc == 0), stop=(kc == KD - 1))
                headnorm(p, N_MEM, PVc(l, 'xa_kg', 0), KTn[:, h, :])
                for mt in range(2):
                    pv = PS()
                    for kc in range(KD):
                        S.matmul(pv[:, 0:128], memT[:, kc, mt * 128:(mt + 1) * 128],
                                 wkv[:, kc, h * 256 + 128:h * 256 + 256], start=(kc == 0), stop=(kc == KD - 1))
                    evac(V1[:, h, mt, :], pv[:, 0:128])
            scale = 128 ** -0.5
            ib = 0
            for tc in range(4):
                ts_ = slice(tc * 512, (tc + 1) * 512)
                for h in range(4):
                    p = PS()
                    for kc in range(KD):
                        S.matmul(p[:, :], wq[:, kc, h * 128:(h + 1) * 128], hT[:, kc, ts_],
                                 start=(kc == 0), stop=(kc == KD - 1))
                    headnorm(p, 512, PVc(l, 'xa_qg', 0), QTn)
                    po = PSA(0)
                    pd = PSA(1)
                    for mt in range(2):
                        ps_ = PS()
                        S.matmul(ps_[:, :], KTn[:, h, mt * 128:(mt + 1) * 128], QTn)
                        pt2 = PT[ib % 2]
                        ib += 1
                        S.act(pt2, ps_[:, :], AF.Exp, scale=scale)
                        S.matmul(po[:, :], V1[:, h, mt, :], pt2, start=(mt == 0), stop=(mt == 1))
                        S.matmul(pd[:, :], ones_b, pt2, start=(mt == 0), stop=(mt == 1))
                    S.act(rs, pd[:, :], AF.Ln)
                    S.act(rs, rs, AF.Exp, scale=-1.0)
                    S.tt('dve', oT[:, h, :], po[:, :], rs, ALU.mult)
                for j in range(4):
                    t = tc * 4 + j
                    for h2 in range(2):
                        p = PS()
                        for c in range(4):
                            S.matmul(p[:, :], oT[:, c, j * 128:(j + 1) * 128], wo[:, c, h2 * 512:(h2 + 1) * 512],
                                     start=(c == 0), stop=(c == 3))
                        S.tt('dve', xres[:, t, h2 * 512:(h2 + 1) * 512], xres[:, t, h2 * 512:(h2 + 1) * 512],
                             p[:, :], ALU.add)
            A.release(m)

        def ffn(l):
            m = A.mark()
            rms_to_hT(l, 'g_ffn')
            NF = 11
            uT = A.alloc([NF, SEQ], BF16)
            w2g = A.alloc([NF, D], BF16)
            w1b = [A.alloc([KD, 256], BF16) for _ in range(2)]
            w3b = [A.alloc([KD, 256], BF16) for _ in range(2)]
            sl = [A.alloc([512], F32) for _ in range(2)]
            it = 0
            blocks = [(g, b) for g in range(2) for b in range(6)]

            def load_blk(i):
                if i >= len(blocks):
                    return
                g_, b_ = blocks[i]
                nb_ = 2 if b_ < 5 else 1
                c0_ = g_ * NF * 128 + b_ * 256
                S.wdma(w1b[i % 2][:, :, 0:nb_ * 128], w1_d[l, :, c0_:c0_ + nb_ * 128].rearrange("(a p) c -> p a c", p=128))
                S.wdma(w3b[i % 2][:, :, 0:nb_ * 128], w3_d[l, :, c0_:c0_ + nb_ * 128].rearrange("(a p) c -> p a c", p=128))

            load_blk(0)
            for g in range(2):
                f0 = g * NF * 128
                for h2 in range(2):
                    S.wdma(w2g[:, :, h2 * 512:(h2 + 1) * 512],
                           w2_d[l, f0:f0 + NF * 128, h2 * 512:(h2 + 1) * 512].rearrange("(a p) c -> p a c", p=128))
                for b in range(6):
                    nb = 2 if b < 5 else 1
                    s_ = (g * 6 + b) % 2
                    load_blk(g * 6 + b + 1)
                    for f in range(nb):
                        fo = b * 2 + f
                        for tc in range(4):
                            ts_ = slice(tc * 512, (tc + 1) * 512)
                            p1 = PS()
                            for kc in range(KD):
                                S.matmul(p1[:, :], w1b[s_][:, kc, f * 128:(f + 1) * 128], hT[:, kc, ts_],
                                         start=(kc == 0), stop=(kc == KD - 1))
                            p3 = PS()
                            for kc in range(KD):
                                S.matmul(p3[:, :], w3b[s_][:, kc, f * 128:(f + 1) * 128], hT[:, kc, ts_],
                                         start=(kc == 0), stop=(kc == KD - 1))
                            s1 = sl[it % 2]
                            it += 1
                            S.act(s1, p1[:, :], AF.Silu)
                            S.tt('dve', uT[:, fo, ts_], s1, p3[:, :], ALU.mult)
                for t in range(NT):
                    for h2 in range(2):
                        p = PS()
                        for f in range(NF):
                            S.matmul(p[:, :], uT[:, f, t * 128:(t + 1) * 128], w2g[:, f, h2 * 512:(h2 + 1) * 512],
                                     start=(f == 0), stop=(f == NF - 1))
                        S.tt('dve', xres[:, t, h2 * 512:(h2 + 1) * 512], xres[:, t, h2 * 512:(h2 + 1) * 512],
                             p[:, :], ALU.add)
            A.release(m)

        cmask_b = cb[:, 386:514] if False else None
        cb2 = sb("cstb2", [128, 384], BF16)
        cmask_b = cb2[:, 0:128]
        hpad_b = cb2[:, 128:384].rearrange("p (h c) -> p h c", c=128)
        S.copy('dve', cmask_b, C('cmask'))
        S.memset('dve', hpad_b[:, :, :], 0.0)
        S.memset('dve', hpad_b[:, 0, 0:64], 1.0)
        S.memset('dve', hpad_b[:, 1, 64:128], 1.0)

        def tap_tm(name, src_tm):
            pass

        for l in range(layers):
            S.dma('sp', prow[:, :], prow_d[l], 'c3')
            rms_to_hT(l, 'g_mix')
            for n, fn in enumerate((rwkv, lru, mla)):
                m = A.mark()
                yT = A.alloc([4, SEQ], BF16)
                if ('rwkv', 'lru', 'mla')[n] not in phases:
                    A.release(m)
                    continue
                fn(l, yT)
                if ('y%d_%d' % (n, l)) in tap_d:
                    yf = A.alloc([256], F32)
                    for fc in range(4):
                        for q8 in range(8):
                            S.ts('dve', yf, yT[:, fc, q8 * 256:(q8 + 1) * 256], 1e30, -1e30, ALU.min, ALU.max)
                            o = S.dma('sp', tap_d['y%d_%d' % (n, l)][fc * 128:(fc + 1) * 128, q8 * 256:(q8 + 1) * 256], yf, 'tap')
                            outs.append(o)
                if 'merge' in phases:
                    merge(l, n, yT)
                A.release(m)
            if ('x1_%d' % l) in tap_d:
                for t in range(NT):
                    outs.append(S.dma('sp', tap_d['x1_%d' % l][t * 128:(t + 1) * 128, :], xres[:, t, :], 'tap'))
            if 'xattn' in phases:
                xattn(l)
            if ('x2_%d' % l) in tap_d:
                for t in range(NT):
                    outs.append(S.dma('sp', tap_d['x2_%d' % l][t * 128:(t + 1) * 128, :], xres[:, t, :], 'tap'))
            if 'ffn' in phases:
                ffn(l)
        for t in range(NT):
            outs.append(S.dma('sp', out_d[t * 128:(t + 1) * 128, :], xres[:, t, :], force=True))
        S.fence('sp', outs)
        S.emit()
        build.stats = (len(S.ops), A.peak)
    return nc


def _cols(v):
    v = np.ascontiguousarray(v, dtype=np.float32).reshape(-1)
    return v.reshape(-1, 128).T


def _consts():
    c = np.zeros((128, NCS), np.float32)

    def put(name, a):
        o, w = CS[name]
        c[:a.shape[0], o:o + w] = a
    put('ident', np.eye(128, dtype=np.float32))
    put('identblk', np.concatenate([np.eye(64), np.eye(64)], 0).astype(np.float32))
    put('mask_s', np.tril(np.ones((64, 64), np.float32), -1))
    put('maskT_s', np.triu(np.ones((64, 64), np.float32), 1))
    put('cmask', np.triu(np.ones((128, 128), np.float32), 0))
    bo = np.zeros((128, 128), np.float32)
    bo[:64, :64] = 1
    bo[64:, 64:] = 1
    put('blockones', bo)
    hs = np.zeros((128, 2), np.float32)
    hs[:64, 0] = 1
    hs[64:, 1] = 1
    put('headsel', hs)
    cm = np.ones((128, 256), np.float32)
    cm[:, ::64] = 0
    put('chunkmask', cm)
    inv = (10000.0 ** (-np.arange(0, 32, 2, dtype=np.float32) / 32)).astype(np.float32)
    put('invfreq', np.broadcast_to(inv, (128, 16)))
    put('ones', np.ones((128, 128), np.float32))
    return c


def _pack(inputs):
    pv = np.zeros((DEPTH, 128, NPV), np.float32)
    pr = np.zeros((DEPTH, 128, NPR), np.float32)
    src = {'g_mix': 'norm_mix', 'g_xa': 'norm_xattn', 'g_mem': 'norm_mem', 'g_ffn': 'norm_ffn',
           'b_gate': 'b_gate', 'mu': 'rwkv_mu', 'w0': 'rwkv_w0', 'a0': 'rwkv_a0', 'k_k': 'rwkv_k_k',
           'k_a': 'rwkv_k_a', 'r_k': 'rwkv_r_k', 'conv_w': 'lru_conv_w', 'conv_b': 'lru_conv_b',
           'ba': 'lru_ba', 'bx': 'lru_bx', 'lam': 'lru_lambda', 'q_norm': 'mla_q_norm',
           'kv_norm': 'mla_kv_norm', 'xa_qg': 'xa_q_gain', 'xa_kg': 'xa_k_gain'}
    for l in range(DEPTH):
        for k, (o, w) in PV.items():
            pv[l, :, o:o + w] = _cols(inputs[src[k]][l])
        for k, s in (('ln_g', 'rwkv_ln_g'), ('ln_b', 'rwkv_ln_b'), ('q_gain', 'mla_q_gain'), ('k_gain', 'mla_k_gain')):
            o, w = PR[k]
            pr[l, :, o:o + w] = np.broadcast_to(np.asarray(inputs[s][l], np.float32).reshape(1, w), (128, w))
    return pv, pr


_WNAMES = ['w_in', 'rwkv_w_up', 'rwkv_a_up', 'rwkv_g_up', 'lru_wa', 'lru_wx', 'mla_w_uq', 'mla_w_ukv',
           'w_branch', 'w_out', 'xa_w_q', 'xa_w_kv', 'xa_w_o', 'ffn_w1', 'ffn_w3', 'ffn_w2']


def make_in_maps(inputs, cores):
    pv, pr = _pack(inputs)
    cst = _consts()
    shared = {k: np.ascontiguousarray(inputs[k], dtype=np.float32) for k in _WNAMES}
    shared.update({'cst': cst, 'pvec': pv, 'prow': pr})
    maps = []
    for b in cores:
        mm = dict(shared)
        mm['x'] = np.ascontiguousarray(inputs['x'][b], dtype=np.float32)
        mm['mem'] = np.ascontiguousarray(inputs['mem'][b], dtype=np.float32)
        mm['pos'] = np.ascontiguousarray(np.asarray(inputs['positions'][b], np.int32).reshape(NT, 128).T)
        maps.append(mm)
    return maps


def kernel(**inputs):
    inputs = {k: np.asarray(v) for k, v in inputs.items()}
    nc = bass.Bass("TRN2", target_bir_lowering=False)
    build(nc)
    maps = make_in_maps(inputs, list(range(8)))
    res = run_bass_kernel_spmd(nc, maps, core_ids=list(range(8)))
    return np.stack([np.asarray(r["out"], np.float32) for r in res.results], axis=0)
```

```python
import contextlib
import os
import sys
import numpy as np
import concourse.bass as bass
import concourse.mybir as mybir
from concourse.bass_utils import run_bass_kernel_spmd

F32 = mybir.dt.float32
BF16 = mybir.dt.bfloat16
I32 = mybir.dt.int32
AF = mybir.ActivationFunctionType
ALU = mybir.AluOpType
AX = mybir.AxisListType

_ESZ = {F32: 4, BF16: 2, I32: 4}

D = 1024
SEQ = 2048
NT = 16
KD = 8
DEPTH = 2
N_MEM = 256
RW = 512
RWKV_IN = 1792
LRU_OFF = 1792
MLA_OFF = 2816
GATE_OFF = 3232
D_IN = 6304
D_FF = 2816
EPS = 1e-6
TWO_PI = 6.283185307179586

PV = {}
_o = 0
for _n, _w in [('g_mix', 8), ('g_xa', 8), ('g_mem', 8), ('g_ffn', 8), ('b_gate', 24), ('mu', 14),
               ('w0', 4), ('a0', 4), ('k_k', 4), ('k_a', 4), ('r_k', 4), ('conv_w', 16),
               ('conv_b', 4), ('ba', 4), ('bx', 4), ('lam', 4), ('q_norm', 2), ('kv_norm', 1),
               ('xa_qg', 1), ('xa_kg', 1)]:
    PV[_n] = (_o, _w)
    _o += _w
NPV = _o
PR = {'ln_g': (0, 512), 'ln_b': (512, 512), 'q_gain': (1024, 96), 'k_gain': (1120, 96)}
NPR = 1216
CS = {}
_o = 0
for _n, _w in [('ident', 128), ('identblk', 64), ('mask_s', 64), ('maskT_s', 64), ('cmask', 128),
               ('blockones', 128), ('headsel', 2), ('chunkmask', 256), ('invfreq', 16), ('ones', 128)]:
    CS[_n] = (_o, _w)
    _o += _w
NCS = _o


def _box(ap):
    t = ap.tensor
    esz = _ESZ[ap.dtype]
    apl = ap.ap
    off = int(ap.offset) * esz
    sp = str(ap.space)
    if sp == 'PSUM':
        return (t.name, 0, 127, 0, 1 << 20)
    if sp != 'DRAM':
        row = esz
        for s in t.shape[1:]:
            row *= int(s)
        pstep, pcnt = apl[0]
        p0 = off // row
        f0 = off % row
        pn = (pstep * esz) // row if pcnt > 1 else 1
        p1 = p0 + (pcnt - 1) * max(pn, 0)
        ext = 0
        for s, c in apl[1:]:
            ext += (c - 1) * abs(s)
        return (t.name, p0, p1, f0, f0 + (ext + 1) * esz)
    ext = 0
    for s, c in apl:
        ext += (c - 1) * abs(s)
    return (t.name, 0, 0, off, off + (ext + 1) * esz)


def _ovl(a, b):
    return a[1] <= b[2] and b[1] <= a[2] and a[3] < b[4] and b[3] < a[4]


def _covers(a, b):
    return a[1] <= b[1] and a[2] >= b[2] and a[3] <= b[3] and a[4] >= b[4]


class Op:
    __slots__ = ('eng', 'fn', 'deps', 'signal', 'sem', 'val', 'is_dma', 'idx', 'waits', 'key', 'clock', 'line', 'rows', 'odeps', 'cost', 'succ', 'npend')


class Sched:
    def __init__(self, nc):
        self.nc = nc
        self.ops = []
        self.recs = {}
        self.wk = 0
        self.serial_w = True
        self.wprev = None

    def add(self, eng, fn, reads=(), writes=(), dma_key=None, extra_deps=(), force=False, rows=None, cost=None):
        lim = os.environ.get('MAXOPS')
        if lim is not None and len(self.ops) >= int(lim) and not force:
            return None
        extra_deps = [d for d in extra_deps if d is not None]
        op = Op()
        op.line = (sys._getframe(2).f_lineno, sys._getframe(1).f_code.co_name)
        op.eng = eng
        op.rows = rows
        op.fn = fn
        op.is_dma = dma_key is not None
        op.key = dma_key
        op.signal = op.is_dma
        op.idx = len(self.ops)
        deps = {}
        rb = [_box(a) for a in reads]
        wb = [_box(a) for a in writes]
        for b in rb:
            psum = b[4] == (1 << 20)
            for rec in self.recs.get(b[0], ()):
                if _ovl(rec[0], b):
                    if rec[2]:
                        deps[rec[1]] = deps.get(rec[1], 0) | 1
                    elif psum:
                        deps[rec[1]] = deps.get(rec[1], 0) | 4
        for b in wb:
            for rec in self.recs.get(b[0], ()):
                if _ovl(rec[0], b):
                    deps[rec[1]] = deps.get(rec[1], 0) | 2
        for d in extra_deps:
            deps[d.idx] = 1
        need = []
        op.odeps = [self.ops[di] for di, kind in deps.items() if kind != 4]
        if cost is None:
            fd = 1
            if writes:
                for s_ in writes[0].shape[1:]:
                    fd *= int(s_)
            if op.is_dma:
                cost = 2000 + fd * 128 * 4 / 250.0
            elif eng == 'pe':
                cost = 70 + max(fd, 64) * 0.45
            elif eng == 'act':
                cost = 200 + fd * 0.85
            else:
                cost = 80 + fd * 1.05
        op.cost = cost
        for di, kind in deps.items():
            d = self.ops[di]
            if not d.is_dma and not op.is_dma and d.eng == eng:
                if kind == 4:
                    continue
                if eng == 'pe':
                    if not ((kind & 2) and rows is not None and d.rows is not None and
                            (rows[1] <= d.rows[0] or d.rows[1] <= rows[0])):
                        continue
            d.signal = True
            need.append(d)
        op.deps = need
        have = {d.idx for d in op.odeps}
        for d in need:
            if d.idx not in have:
                op.odeps.append(d)
        self.ops.append(op)
        for b in wb:
            lst = self.recs.setdefault(b[0], [])
            lst[:] = [r for r in lst if not _covers(b, r[0])]
            lst.append((b, op.idx, True))
        for b in rb:
            lst = self.recs.setdefault(b[0], [])
            lst.append((b, op.idx, False))
        return op

    def matmul(self, out, lhsT, rhs, start=True, stop=True):
        b0 = _box(lhsT)
        return self.add('pe', lambda e: e.matmul(out, lhsT, rhs, start=start, stop=stop),
                        reads=[lhsT, rhs], writes=[out], rows=(b0[1], b0[2] + 1))

    def act(self, out, in_, func, bias=None, scale=None, accum_out=None):
        kw = {}
        reads = [in_]
        writes = [out]
        if bias is not None:
            kw['bias'] = bias
            if not isinstance(bias, (int, float)):
                reads.append(bias)
        if scale is not None:
            kw['scale'] = scale
            if not isinstance(scale, (int, float)):
                reads.append(scale)
        if accum_out is not None:
            kw['accum_out'] = accum_out
            writes.append(accum_out)
        return self.add('act', lambda e: e.activation(out, in_, func, **kw), reads, writes)

    def tt(self, eng, out, in0, in1, op):
        return self.add(eng, lambda e: e.tensor_tensor(out, in0, in1, op), [in0, in1], [out])

    def ts(self, eng, out, in0, s1, s2, op0, op1=None):
        reads = [in0]
        for s in (s1, s2):
            if s is not None and not isinstance(s, (int, float)):
                reads.append(s)
        if op1 is None:
            return self.add(eng, lambda e: e.tensor_scalar(out, in0, s1, None, op0), reads, [out])
        return self.add(eng, lambda e: e.tensor_scalar(out, in0, s1, s2, op0, op1), reads, [out])

    def stt(self, eng, out, in0, scalar, in1, op0, op1):
        reads = [in0, in1]
        if not isinstance(scalar, (int, float)):
            reads.append(scalar)
        return self.add(eng, lambda e: e.scalar_tensor_tensor(out, in0, scalar, in1, op0, op1),
                        reads, [out])

    def copy(self, eng, out, in_):
        if eng == 'act':
            return self.add('act', lambda e: e.copy(out, in_), [in_], [out])
        return self.add(eng, lambda e: e.tensor_copy(out, in_), [in_], [out])

    def reduce(self, eng, out, in_, op):
        return self.add(eng, lambda e: e.tensor_reduce(out, in_, AX.X, op), [in_], [out])

    def recip(self, out, in_):
        return self.add('dve', lambda e: e.reciprocal(out, in_), [in_], [out])

    def memset(self, eng, out, val):
        return self.add(eng, lambda e: e.memset(out, val), [], [out])

    def scan(self, out, d0, d1, initial, op0, op1):
        reads = [d0, d1]
        if not isinstance(initial, (int, float)):
            reads.append(initial)
        return self.add('dve', lambda e: e.tensor_tensor_scan(out, d0, d1, initial, op0, op1),
                        reads, [out])

    def dma(self, eng, out, in_, key=None, force=False):
        NQ = 8
        self.dk = getattr(self, 'dk', 0) + 1
        k = '%s%d' % (eng, self.dk % NQ)
        if not hasattr(self, 'dprev'):
            self.dprev = {}
        prev = [self.dprev[k]] if k in self.dprev else []
        op = self.add(eng, lambda e: e.dma_start(out=out, in_=in_), [in_], [out], dma_key=k,
                      extra_deps=prev, force=force)
        if op is not None:
            self.dprev[k] = op
        return op

    def wdma(self, out, in_):
        NW = int(os.environ.get('NW', '2'))
        self.wk = (self.wk + 1) % NW
        k = 'w%d' % self.wk
        if not hasattr(self, 'wprevs'):
            self.wprevs = {}
        prev = [self.wprevs[k]] if k in self.wprevs else []
        op = self.add('pool', lambda e: e.dma_start(out=out, in_=in_), [in_], [out],
                      dma_key=k, extra_deps=prev)
        if op is not None:
            self.wprevs[k] = op
        return op

    def fence(self, eng, deps):
        return self.add(eng, None, extra_deps=deps, force=True)

    def schedule(self):
        import heapq
        ops = self.ops
        for op in ops:
            op.succ = []
        for op in ops:
            uniq = {d.idx: d for d in op.odeps}
            op.odeps = list(uniq.values())
            op.npend = len(op.odeps)
            for d in op.odeps:
                d.succ.append(op)
        fin = [0.0] * len(ops)
        engs = ['pe', 'act', 'dve', 'pool', 'sp']
        free = {e: 0.0 for e in engs}
        wait_h = {e: [] for e in engs}
        rdy_h = {e: [] for e in engs}
        WIN = int(os.environ.get('SCHED_WIN', '1000'))
        LAT_S = float(os.environ.get('SCHED_LAT_S', '0'))
        LAT_X = float(os.environ.get('SCHED_LAT_X', '50'))

        def push(op):
            rt = 0.0
            for d in op.odeps:
                lat = LAT_S if (d.eng == op.eng and not d.is_dma) else LAT_X
                rt = max(rt, fin[d.idx] + lat)
            heapq.heappush(wait_h[op.eng], (rt, op.idx))
        for op in ops:
            if op.npend == 0:
                push(op)
        order = []
        nsched = 0
        lowest = 0
        done = [False] * len(ops)
        while nsched < len(ops):
            best = None
            for e in engs:
                wh, rh = wait_h[e], rdy_h[e]
                while wh and wh[0][0] <= free[e]:
                    heapq.heappush(rh, heapq.heappop(wh)[1])
                if rh:
                    cand = (free[e], rh[0], e, True)
                elif wh:
                    cand = (wh[0][0], wh[0][1], e, False)
                else:
                    continue
                if cand[1] > lowest + WIN:
                    cand = (cand[0] + 1e9, cand[1], e, cand[3])
                if best is None or cand[:2] < best[:2]:
                    best = cand
            st_, idx, e, from_rdy = best
            if st_ >= 1e9:
                st_ -= 1e9
            if from_rdy:
                heapq.heappop(rdy_h[e])
            else:
                heapq.heappop(wait_h[e])
            op = ops[idx]
            start = max(st_, free[e])
            fin[idx] = start + op.cost
            free[e] = fin[idx] if not op.is_dma else start + 60.0
            order.append(op)
            done[idx] = True
            nsched += 1
            while lowest < len(ops) and done[lowest]:
                lowest += 1
            for s2 in op.succ:
                s2.npend -= 1
                if s2.npend == 0:
                    push(s2)
        self.est_time = max(fin) if fin else 0.0
        return order

    def emit(self):
        nc = self.nc
        ops = self.schedule() if os.environ.get('NOSCHED') is None else self.ops
        engs = ['pe', 'act', 'dve', 'pool', 'sp']
        keys = []
        for op in ops:
            if op.is_dma and op.signal and op.key not in keys:
                keys.append(op.key)
        with contextlib.ExitStack() as st:
            sems = {}
            for e in engs:
                sems[e] = st.enter_context(nc.semaphore('s_' + e))
            for k in keys:
                sems[('dma', k)] = st.enter_context(nc.semaphore('d_' + str(k)))
            cnt = {}
            clock = {e: {} for e in engs}
            for op in ops:
                ck = clock[op.eng]
                need = {}
                for d in op.deps:
                    if ck.get(d.sem, 0) < d.val:
                        need[d.sem] = max(need.get(d.sem, 0), d.val)
                for d in op.deps:
                    for s, v in d.clock.items():
                        if ck.get(s, 0) < v:
                            ck[s] = v
                    if ck.get(d.sem, 0) < d.val:
                        ck[d.sem] = d.val
                op.waits = list(need.items())
                if op.signal:
                    if op.is_dma:
                        sk = ('dma', op.key)
                        cnt[sk] = cnt.get(sk, 0) + 16
                    else:
                        sk = op.eng
                        cnt[sk] = cnt.get(sk, 0) + 1
                    op.sem = sk
                    op.val = cnt[sk]
                    op.clock = dict(ck)
                else:
                    op.sem = None
                    op.val = 0
                    op.clock = None
            per = {e: [o for o in ops if o.eng == e] for e in engs}

            def run(e_name):
                def body(eng):
                    for op in per[e_name]:
                        for s, v in op.waits:
                            eng.wait_ge(sems[s], v)
                        if op.fn is None:
                            continue
                        ins = op.fn(eng)
                        if op.signal:
                            ins.then_inc(sems[op.sem], 16 if op.is_dma else 1)
                return body

            with nc.Block() as block:
                block.tensor(run('pe'))
                block.scalar(run('act'))
                block.vector(run('dve'))
                block.gpsimd(run('pool'))
                block.sync(run('sp'))


class Arena:
    def __init__(self, t, words):
        self.t = t
        self.off = 0
        self.words = words
        self.peak = 0

    def mark(self):
        return self.off

    def release(self, m):
        self.off = m

    def alloc(self, shape, dtype):
        n = 1
        for s in shape:
            n *= s
        esz = _ESZ[dtype]
        w = (n * esz + 3) // 4
        w = (w + 7) // 8 * 8
        assert self.off + w <= self.words, ('arena overflow', self.off, w, self.words)
        v = self.t[:, self.off:self.off + w]
        self.off += w
        self.peak = max(self.peak, self.off)
        if dtype != F32:
            v = v.bitcast(dtype)
        v = v[:, 0:n]
        if len(shape) == 2:
            v = v.rearrange("p (a b) -> p a b", b=shape[1])
        elif len(shape) == 3:
            v = v.rearrange("p (a b c) -> p a b c", b=shape[1], c=shape[2])
        return v


def bc(ap, axis, shape):
    return ap.unsqueeze(axis).to_broadcast(shape)


def build(nc, layers=DEPTH, taps=(), phases=('rwkv', 'lru', 'mla', 'merge', 'xattn', 'ffn')):
    dr = lambda n, s, d=F32, k="ExternalInput": nc.dram_tensor(n, s, d, kind=k).ap()
    x_d = dr("x", [SEQ, D])
    mem_d = dr("mem", [N_MEM, D])
    pos_d = dr("pos", [128, NT], I32)
    cst_d = dr("cst", [128, NCS])
    pvec_d = dr("pvec", [DEPTH, 128, NPV])
    prow_d = dr("prow", [DEPTH, 128, NPR])
    w_in_d = dr("w_in", [DEPTH, D, D_IN])
    w_up_d = dr("rwkv_w_up", [DEPTH, 64, RW])
    a_up_d = dr("rwkv_a_up", [DEPTH, 64, RW])
    g_up_d = dr("rwkv_g_up", [DEPTH, 128, RW])
    wa_d = dr("lru_wa", [DEPTH, 8, 64, 64])
    wx_d = dr("lru_wx", [DEPTH, 8, 64, 64])
    w_uq_d = dr("mla_w_uq", [DEPTH, 256, 768])
    w_ukv_d = dr("mla_w_ukv", [DEPTH, 128, 1024])
    w_br_d = dr("w_branch", [DEPTH, 3, 512, D])
    w_out_d = dr("w_out", [DEPTH, D, D])
    xa_wq_d = dr("xa_w_q", [DEPTH, D, 512])
    xa_wkv_d = dr("xa_w_kv", [DEPTH, D, D])
    xa_wo_d = dr("xa_w_o", [DEPTH, 512, D])
    w1_d = dr("ffn_w1", [DEPTH, D, D_FF])
    w3_d = dr("ffn_w3", [DEPTH, D, D_FF])
    w2_d = dr("ffn_w2", [DEPTH, D_FF, D])
    out_d = dr("out", [SEQ, D], F32, "ExternalOutput")
    tap_d = {}
    for name, shape in taps:
        tap_d[name] = dr("tap_" + name, shape, F32, "ExternalOutput")

    ARW = 24800
    with contextlib.ExitStack() as st:
        sb = lambda n, s, d: st.enter_context(nc.sbuf_tensor(n, s, d))
        xres = sb("xres", [128, NT, D], F32)
        hT = sb("hT", [128, KD, SEQ], BF16)
        cst = sb("cstf", [128, NCS], F32)
        cb = sb("cstb", [128, 512], BF16)
        pvec = sb("pvec_s", [128, DEPTH, NPV], F32)
        prow = sb("prow_s", [128, NPR], F32)
        omu = sb("omu", [128, 32], F32)
        npv = sb("npv", [128, DEPTH, NPV], F32)
        cos_t = sb("cos_t", [128, NT, 16], F32)
        sin_t = sb("sin_t", [128, NT, 16], F32)
        art = sb("arena", [128, ARW], F32)
        pst = [st.enter_context(nc.psum_tensor("ps%d" % i, [128, 512], F32)) for i in range(8)]
        S = Sched(nc)
        A = Arena(art, ARW)
        psi = [0]

        def PS():
            psi[0] = (psi[0] + 1) % 6
            return pst[psi[0]]

        def PSA(i):
            return pst[6 + i]

        def C(name, rows=slice(0, 128)):
            o, w = CS[name]
            return cst[rows, o:o + w]

        def PVc(l, name, j=0, n=1, rows=slice(0, 128)):
            o, w = PV[name]
            return pvec[rows, l, o + j:o + j + n]

        def NPVc(l, name, j=0, n=1):
            o, w = PV[name]
            return npv[:, l, o + j:o + j + n]

        def PRr(name, rows=slice(0, 128)):
            o, w = PR[name]
            return prow[rows, o:o + w]

        ident_b = cb[:, 0:128]
        bones_b = cb[:, 128:256]
        ones_b = cb[:, 256:384]
        hsel_b = cb[:, 384:386]
        tapn = [0]

        def tap(name, dst_ap, src_ap):
            if name not in tap_d:
                return
            tapn[0] += 1
            outs.append(S.dma('sp', dst_ap, src_ap, 'tap'))

        outs = []
        evn = [0]

        def evac(out, in_, scale_ap=None):
            evn[0] += 1
            if evn[0] % 2 == 0:
                if scale_ap is None:
                    S.copy('act', out, in_)
                else:
                    S.act(out, in_, AF.Copy, scale=scale_ap)
            else:
                if scale_ap is None:
                    S.copy('dve', out, in_)
                else:
                    S.ts('dve', out, in_, scale_ap, None, ALU.mult)

        S.dma('sp', cst[:, :], cst_d, 'c0')
        S.dma('sp', pvec[:, :, :], pvec_d.rearrange("l p c -> p l c"), 'c1')
        for t in range(NT):
            S.dma('sp', xres[:, t, :], x_d[t * 128:(t + 1) * 128, :], 'x%d' % (t % 4))
        S.ts('dve', npv[:, :, :], pvec[:, :, :], -1.0, None, ALU.mult)
        S.copy('dve', ident_b, C('ident'))
        S.copy('dve', bones_b, C('blockones'))
        S.copy('dve', ones_b, C('ones'))
        S.copy('dve', hsel_b, C('headsel'))
        m0 = A.mark()
        pi_ = A.alloc([NT], I32)
        pf = A.alloc([NT], F32)
        ang = A.alloc([NT, 16], F32)
        kf = A.alloc([NT, 16], F32)
        ki = A.alloc([NT, 16], I32)
        S.dma('sp', pi_, pos_d, 'c2')
        S.copy('dve', pf, pi_)
        S.tt('dve', ang, bc(pf, 2, [128, NT, 16]), bc(C('invfreq'), 1, [128, NT, 16]), ALU.mult)
        for (dst, shift) in ((sin_t, 0.0), (cos_t, TWO_PI / 4)):
            a2 = ang
            if shift != 0.0:
                a2 = A.alloc([NT, 16], F32)
                S.ts('dve', a2, ang, shift, None, ALU.add)
            S.ts('dve', kf, a2, 1.0 / TWO_PI, None, ALU.mult)
            S.copy('dve', ki, kf)
            S.copy('dve', kf, ki)
            S.stt('dve', kf, kf, -TWO_PI, a2, ALU.mult, ALU.add)
            S.ts('dve', kf, kf, 3.14159, -3.14159, ALU.min, ALU.max)
            S.act(dst[:, :, :], kf, AF.Sin)
        A.release(m0)

        def rsqrt_(out, in_):
            S.act(out, in_, AF.Ln)
            S.act(out, out, AF.Exp, scale=-0.5)

        def sigmoid_(out, in_, nbias=None, scale=1.0, tmp_=None):
            t_ = out if tmp_ is None else tmp_
            if nbias is None:
                S.act(t_, in_, AF.Exp, scale=-scale)
            else:
                S.act(t_, in_, AF.Exp, scale=-scale, bias=nbias)
            S.act(t_, t_, AF.Ln, bias=1.0)
            S.act(out, t_, AF.Exp, scale=-1.0)

        def rms_to_hT(l, gname):
            m = A.mark()
            junk = A.alloc([D], BF16)
            ss = A.alloc([NT], F32)
            rstd = A.alloc([NT], F32)
            xn = A.alloc([4, D], BF16)
            for t in range(NT):
                S.act(junk, xres[:, t, :], AF.Square, accum_out=ss[:, t:t + 1])
            S.ts('dve', rstd, ss, 1.0 / D, EPS, ALU.mult, ALU.add)
            rsqrt_(rstd, rstd)
            for tg in range(4):
                for j in range(4):
                    t = tg * 4 + j
                    S.ts('dve', xn[:, j, :], xres[:, t, :], rstd[:, t:t + 1], None, ALU.mult)
                for kc in range(KD):
                    p = PS()
                    for j in range(4):
                        S.matmul(p[:, j * 128:(j + 1) * 128], xn[:, j, kc * 128:(kc + 1) * 128], ident_b)
                    evac(hT[:, kc, tg * 512:(tg + 1) * 512], p[:, :], PVc(l, gname, kc))
            A.release(m)

        def merge(l, n, yT):
            m = A.mark()
            gp = A.alloc([KD, SEQ], BF16)
            wout = A.alloc([KD, D], BF16)
            wg = [A.alloc([KD, 512], BF16) for _ in range(2)]
            wb_ = [A.alloc([4, 512], BF16) for _ in range(2)]
            gt = [A.alloc([512], F32) for _ in range(2)]
            for fq in range(2):
                c0 = GATE_OFF + n * D + fq * 512
                S.wdma(wg[fq], w_in_d[l, :, c0:c0 + 512].rearrange("(a p) c -> p a c", p=128))
                S.wdma(wb_[fq], w_br_d[l, n, :, fq * 512:(fq + 1) * 512].rearrange("(a p) c -> p a c", p=128))
            for h2 in range(2):
                S.wdma(wout[:, :, h2 * 512:(h2 + 1) * 512],
                       w_out_d[l, :, h2 * 512:(h2 + 1) * 512].rearrange("(a p) c -> p a c", p=128))
            it = 0
            for fq in range(2):
                for f in range(4):
                    fo = fq * 4 + f
                    for tc in range(4):
                        ts_ = slice(tc * 512, (tc + 1) * 512)
                        pg = PS()
                        for kc in range(KD):
                            S.matmul(pg[:, :], wg[fq][:, kc, f * 128:(f + 1) * 128], hT[:, kc, ts_],
                                     start=(kc == 0), stop=(kc == KD - 1))
                        pp = PS()
                        for kc in range(4):
                            S.matmul(pp[:, :], wb_[fq][:, kc, f * 128:(f + 1) * 128], yT[:, kc, ts_],
                                     start=(kc == 0), stop=(kc == 3))
                        g = gt[it % 2]
                        it += 1
                        S.act(g, pg[:, :], AF.Sigmoid, bias=PVc(l, 'b_gate', n * 8 + fo))
                        S.tt('dve', gp[:, fo, ts_], g, pp[:, :], ALU.mult)
            for t in range(NT):
                for h2 in range(2):
                    p = PS()
                    for f in range(KD):
                        S.matmul(p[:, :], gp[:, f, t * 128:(t + 1) * 128], wout[:, f, h2 * 512:(h2 + 1) * 512],
                                 start=(f == 0), stop=(f == KD - 1))
                    S.tt('dve', xres[:, t, h2 * 512:(h2 + 1) * 512], xres[:, t, h2 * 512:(h2 + 1) * 512],
                         p[:, :], ALU.add)
            A.release(m)

        def rwkv(l, yT):
            m = A.mark()
            G = 256
            NG = SEQ // G
            wbuf = [A.alloc([KD, 128], BF16) for _ in range(3)]
            wup = A.alloc([RW], BF16)
            gup = A.alloc([RW], BF16)
            carry = A.alloc([16], F32)
            pf_ = A.alloc([G + 1], F32)
            tmp = [A.alloc([G], F32) for _ in range(12)]
            wdad = A.alloc([G], BF16)
            sg = A.alloc([G], BF16)
            sqb = A.alloc([G], BF16)
            qtT, ktT, btT, kapT, vT, KendT, nBendT, rkrT = [A.alloc([4, G], BF16) for _ in range(8)]
            pc = A.alloc([4, 4], F32)
            Dg = A.alloc([4, 4, 64], BF16)
            tm2 = [[A.alloc([512], BF16) for _ in range(4)] for _ in range(2)]
            Mb = [A.alloc([512], BF16) for _ in range(3)]
            MTb = [A.alloc([512], BF16) for _ in range(3)]
            TTb = [A.alloc([512], BF16) for _ in range(3)]
            A2 = [[A.alloc([512], BF16) for _ in range(5)] for _ in range(2)]
            kaph, qhT, GT = [A.alloc([512], BF16) for _ in range(3)]
            IMb = [A.alloc([512], BF16) for _ in range(2)]
            mrot = [0]
            cpar = [0]
            Sbf = A.alloc([512], BF16)
            ep = [A.alloc([512], F32) for _ in range(3)]
            st8 = [A.alloc([8], F32) for _ in range(4)]
            yo = A.alloc([512], BF16)
            S.wdma(wup[0:64, :], w_up_d[l])
            S.wdma(wup[64:128, :], a_up_d[l])
            S.wdma(gup, g_up_d[l])
            S.memset('dve', carry, 0.0)
            S.memset('dve', Sbf[0:64, :], 0.0)
            o_mu = PV['mu'][0]
            S.ts('dve', omu[:, 0:14], pvec[:, l, o_mu:o_mu + 14], -1.0, 1.0, ALU.mult, ALU.add)
            S.ts('dve', omu[:, 16:20], PVc(l, 'k_a', 0, 4), -1.0, 1.0, ALU.mult, ALU.add)
            mask_s3 = bc(C('mask_s', slice(0, 64)), 1, [64, 8, 64])
            maskT_s3 = bc(C('maskT_s', slice(0, 64)), 1, [64, 8, 64])
            maskT_i3 = bc(cst[0:64, CS['cmask'][0]:CS['cmask'][0] + 64], 1, [64, 8, 64])
            ident3 = bc(cst[0:64, CS['ident'][0]:CS['ident'][0] + 64], 1, [64, 8, 64])
            v3 = lambda ap: ap.rearrange("p (h c) -> p h c", c=64)
            wblk = [(0, 512), (512, 512), (1024, 512), (1536, 256)]
            wcur = [None, None]

            forder = [12, 13] + [blk * 4 + j for j in range(4) for blk in range(3)]
            uses = [fc for _ in range(NG) for fc in forder]
            wptr = [0, 0]

            def issue_w():
                i = wptr[0]
                if i < len(uses):
                    fc = uses[i]
                    S.wdma(wbuf[i % 3], w_in_d[l, :, fc * 128:(fc + 1) * 128].rearrange("(a p) c -> p a c", p=128))
                    wptr[0] += 1

            def next_slot():
                i = wptr[1]
                wptr[1] += 1
                issue_w()
                return i % 3

            issue_w()
            issue_w()

            def proj_mix(fc, g0, out_pm, slot, coff):
                p = PS()
                for kc in range(KD):
                    S.matmul(p[:, 0:G], wbuf[slot][:, kc, coff:coff + 128], hT[:, kc, g0:g0 + G],
                             start=(kc == 0), stop=(kc == KD - 1))
                S.copy('act', pf_[:, 0:1], carry[:, fc:fc + 1])
                S.copy('act', pf_[:, 1:G + 1], p[:, 0:G])
                S.copy('act', carry[:, fc:fc + 1], pf_[:, G:G + 1])
                t0 = tmp[11]
                S.ts('dve', t0, pf_[:, 0:G], PVc(l, 'mu', fc), None, ALU.mult)
                S.stt('dve', out_pm, pf_[:, 1:G + 1], omu[:, fc:fc + 1], t0, ALU.mult, ALU.add)

            for gi in range(NG):
                g0 = gi * G
                pm = tmp[0]
                proj_mix(12, g0, pm, next_slot(), 0)
                S.act(tmp[1][0:64, :], pm[0:64, :], AF.Exp, scale=2.0)
                S.ts('dve', tmp[1][0:64, :], tmp[1][0:64, :], 1.0, None, ALU.add)
                S.recip(tmp[1][0:64, :], tmp[1][0:64, :])
                S.ts('dve', wdad[0:64, :], tmp[1][0:64, :], -2.0, 1.0, ALU.mult, ALU.add)
                S.copy('dve', wdad[64:128, :], pm[64:128, :])
                proj_mix(13, g0, pm, next_slot(), 0)
                sigmoid_(sg, pm, tmp_=tmp[1])
                for j in range(4):
                    rm, km, vm = tmp[0], tmp[1], tmp[2]
                    for (blk, dst) in ((0, rm), (1, km), (2, vm)):
                        proj_mix(blk * 4 + j, g0, dst, next_slot(), 0)
                    jc = slice(j * 128, (j + 1) * 128)
                    pz = PS()
                    S.matmul(pz[:, 0:G], wup[0:64, jc], wdad[0:64, :])
                    lw = tmp[3]
                    sigmoid_(lw, pz[:, 0:G], nbias=NPVc(l, 'w0', j))
                    S.ts('dve', lw, lw, -0.6065306597126334, None, ALU.mult)
                    pa = PS()
                    S.matmul(pa[:, 0:G], wup[64:128, jc], wdad[64:128, :])
                    am = tmp[4]
                    sigmoid_(am, pa[:, 0:G], nbias=NPVc(l, 'a0', j))
                    kkr = tmp[5]
                    S.ts('dve', kkr, km, PVc(l, 'k_k', j), None, ALU.mult)
                    S.act(sqb, kkr, AF.Square)
                    pss = PS()
                    S.matmul(pss[:, 0:G], bones_b, sqb)
                    rn = tmp[6]
                    S.ts('dve', rn, pss[:, 0:G], 1e-24, None, ALU.max)
                    rsqrt_(rn, rn)
                    kk = tmp[5]
                    S.tt('dve', kk, kkr, rn, ALU.mult)
                    t1 = tmp[6]
                    S.ts('dve', t1, am, PVc(l, 'k_a', j), omu[:, 16 + j:17 + j], ALU.mult, ALU.add)
                    kp = tmp[7]
                    S.tt('dve', kp, km, t1, ALU.mult)
                    bb = tmp[8]
                    S.tt('dve', bb, kk, am, ALU.mult)
                    S.stt('dve', rkrT[:, j, :], rm, PVc(l, 'r_k', j), kp, ALU.mult, ALU.mult)
                    S.copy('act', vT[:, j, :], vm)
                    L = tmp[9]
                    S.scan(L, C('chunkmask'), lw, 0.0, ALU.mult, ALU.add)
                    L3 = L.rearrange("p (c t) -> p c t", t=64)
                    LC = L3[:, :, 63:64]
                    S.act(pc[:, j, :], L3[:, :, 63], AF.Exp)
                    eP = tmp[10]
                    S.act(eP, L, AF.Exp)
                    S.tt('dve', qtT[:, j, :], rm, eP, ALU.mult)
                    dK = tmp[10]
                    S.tt('dve', dK, L, lw, ALU.subtract)
                    S.act(dK, dK, AF.Exp)
                    S.tt('dve', kapT[:, j, :], kk, dK, ALU.mult)
                    eN = tmp[3]
                    S.act(eN, L, AF.Exp, scale=-1.0)
                    S.tt('dve', ktT[:, j, :], kp, eN, ALU.mult)
                    S.tt('dve', btT[:, j, :], bb, eN, ALU.mult)
                    eE = tmp[4]
                    S.tt('dve', eE.rearrange("p (c t) -> p c t", t=64), LC.to_broadcast([128, 4, 64]), L3,
                         ALU.subtract)
                    S.act(eE, eE, AF.Exp)
                    S.tt('dve', KendT[:, j, :], kp, eE, ALU.mult)
                    S.stt('dve', nBendT[:, j, :], bb, -1.0, eE, ALU.mult, ALU.mult)
                    S.tt('dve', Dg[:, j, :, :], bc(C('identblk'), 1, [128, 4, 64]),
                         bc(pc[:, j, :], 2, [128, 4, 64]), ALU.mult)
                for c in range(G // 64):
                    cs = slice(c * 64, (c + 1) * 64)
                    tok0 = g0 + c * 64
                    cpar[0] ^= 1
                    tm = tm2[cpar[0]]
                    AkkT, ArkT, nArbT, AV, UV = A2[cpar[0]]
                    for qi, X in enumerate((kapT, vT, KendT, nBendT)):
                        p = PS()
                        for j in range(4):
                            S.matmul(p[0:64, j * 128:(j + 1) * 128], X[:, j, cs], ident_b)
                        evac(tm[qi][0:64, :], p[0:64, :])
                    kap_tm, V_tm, Kend_tm, nBend_tm = [t_[0:64, :] for t_ in tm]
                    hb = lambda h: ((h % 2) * 64, h // 2)
                    HO = (0, 2, 4, 6, 1, 3, 5, 7)
                    pN = PS()
                    pNT = PS()
                    for h in HO:
                        b, j = hb(h)
                        S.matmul(pN[0:64, h * 64:(h + 1) * 64], kapT[b:b + 64, j, cs], btT[b:b + 64, j, cs])
                    for h in HO:
                        b, j = hb(h)
                        S.matmul(pNT[0:64, h * 64:(h + 1) * 64], btT[b:b + 64, j, cs], kapT[b:b + 64, j, cs])
                    mrot[0] = (mrot[0] + 1) % 3
                    cur = mrot[0]
                    M_, MT_, TT_ = Mb[cur][0:64, :], MTb[cur][0:64, :], TTb[cur][0:64, :]
                    S.stt('dve', v3(M_), v3(pN[0:64, :]), -1.0, mask_s3, ALU.mult, ALU.mult)
                    S.stt('dve', v3(MT_), v3(pNT[0:64, :]), -1.0, maskT_s3, ALU.mult, ALU.mult)
                    S.tt('dve', v3(TT_), v3(MT_), ident3, ALU.add)
                    for lev in range(5):
                        mrot[0] = (mrot[0] + 1) % 3
                        nx = mrot[0]
                        Mn, MTn, TTn = Mb[nx][0:64, :], MTb[nx][0:64, :], TTb[nx][0:64, :]
                        pM = PS()
                        for h in range(8):
                            hs = slice(h * 64, (h + 1) * 64)
                            S.matmul(pM[0:64, hs], MT_[:, hs], M_[:, hs])
                        if lev < 4:
                            pMT = PS()
                            for h in range(8):
                                hs = slice(h * 64, (h + 1) * 64)
                                S.matmul(pMT[0:64, hs], M_[:, hs], MT_[:, hs])
                        S.tt('dve', v3(IMb[lev % 2][0:64, :]), v3(pM[0:64, :]), ident3, ALU.add)
                        if lev < 4:
                            S.copy('act', Mn, pM[0:64, :])
                            S.copy('act', MTn, pMT[0:64, :])
                        pT = PS()
                        for h in range(8):
                            hs = slice(h * 64, (h + 1) * 64)
                            S.matmul(pT[0:64, hs], IMb[lev % 2][0:64, hs], TT_[:, hs])
                        evac(TTn, pT[0:64, :])
                        M_, MT_, TT_ = Mn, MTn, TTn
                    TT = TT_
                    pA1, pA2, pA3 = PS(), PS(), PS()
                    for h in HO:
                        b, j = hb(h)
                        hs = slice(h * 64, (h + 1) * 64)
                        S.matmul(pA1[0:64, hs], ktT[b:b + 64, j, cs], kapT[b:b + 64, j, cs])
                        S.matmul(pA2[0:64, hs], ktT[b:b + 64, j, cs], qtT[b:b + 64, j, cs])
                        S.matmul(pA3[0:64, hs], btT[b:b + 64, j, cs], qtT[b:b + 64, j, cs])
                    S.tt('dve', v3(AkkT[0:64, :]), v3(pA1[0:64, :]), maskT_s3, ALU.mult)
                    S.tt('dve', v3(ArkT[0:64, :]), v3(pA2[0:64, :]), maskT_i3, ALU.mult)
                    S.stt('dve', v3(nArbT[0:64, :]), v3(pA3[0:64, :]), -1.0, maskT_i3, ALU.mult, ALU.mult)
                    pAV = PS()
                    for h in range(8):
                        hs = slice(h * 64, (h + 1) * 64)
                        S.matmul(pAV[0:64, hs], AkkT[0:64, hs], V_tm[:, hs])
                    S.copy('act', AV[0:64, :], pAV[0:64, :])
                    pK = PS()
                    for h in range(8):
                        hs = slice(h * 64, (h + 1) * 64)
                        S.matmul(pK[0:64, hs], TT[:, hs], kap_tm[:, hs])
                    S.copy('dve', kaph[0:64, :], pK[0:64, :])
                    pUV = PS()
                    for h in range(8):
                        hs = slice(h * 64, (h + 1) * 64)
                        S.matmul(pUV[0:64, hs], TT[:, hs], AV[0:64, hs])
                    S.copy('act', UV[0:64, :], pUV[0:64, :])
                    pQ = PS()
                    for h in range(8):
                        b, j = hb(h)
                        hs = slice(h * 64, (h + 1) * 64)
                        S.matmul(pQ[0:64, hs], kaph[0:64, hs], nArbT[0:64, hs], start=True, stop=False)
                        S.matmul(pQ[0:64, hs], ident_b[b:b + 64, b:b + 64], qtT[b:b + 64, j, cs],
                                 start=False, stop=True)
                    S.copy('act', qhT[0:64, :], pQ[0:64, :])
                    pG = PS()
                    for h in range(8):
                        b, j = hb(h)
                        hs = slice(h * 64, (h + 1) * 64)
                        S.matmul(pG[0:64, hs], kaph[0:64, hs], nBend_tm[:, hs], start=True, stop=False)
                        S.matmul(pG[0:64, hs], Dg[b:b + 64, j, c, :], ident_b[b:b + 64, b:b + 64],
                                 start=False, stop=True)
                    S.copy('act', GT[0:64, :], pG[0:64, :])
                    pY = PS()
                    for h in range(8):
                        hs = slice(h * 64, (h + 1) * 64)
                        S.matmul(pY[0:64, hs], ArkT[0:64, hs], V_tm[:, hs], start=True, stop=False)
                        S.matmul(pY[0:64, hs], nArbT[0:64, hs], UV[0:64, hs], start=False, stop=False)
                        S.matmul(pY[0:64, hs], qhT[0:64, hs], Sbf[0:64, hs], start=False, stop=True)
                    pS_ = PS()
                    for h in range(8):
                        hs = slice(h * 64, (h + 1) * 64)
                        S.matmul(pS_[0:64, hs], Kend_tm[:, hs], V_tm[:, hs], start=True, stop=False)
                        S.matmul(pS_[0:64, hs], nBend_tm[:, hs], UV[0:64, hs], start=False, stop=False)
                        S.matmul(pS_[0:64, hs], GT[0:64, hs], Sbf[0:64, hs], start=False, stop=True)
                    ysum, var, rstd_, bsum = [s_[0:64, :] for s_ in st8]
                    yc, sq_, bon = [e_[0:64, :] for e_ in ep]
                    S.reduce('dve', ysum, v3(pY[0:64, :]), ALU.add)
                    S.ts('dve', ysum, ysum, 1.0 / 64, None, ALU.mult)
                    S.tt('dve', v3(yc), v3(pY[0:64, :]), bc(ysum, 2, [64, 8, 64]), ALU.subtract)
                    S.copy('act', Sbf[0:64, :], pS_[0:64, :])
                    S.tt('dve', sq_, yc, yc, ALU.mult)
                    S.reduce('dve', var, v3(sq_), ALU.add)
                    S.ts('dve', var, var, 1.0 / 64, 64e-5, ALU.mult, ALU.add)
                    rsqrt_(rstd_, var)
                    S.tt('dve', v3(yc), v3(yc), bc(rstd_, 2, [64, 8, 64]), ALU.mult)
                    S.tt('dve', yc, yc, PRr('ln_g', slice(0, 64)), ALU.mult)
                    S.tt('dve', yc, yc, PRr('ln_b', slice(0, 64)), ALU.add)
                    pB = PS()
                    for j in range(4):
                        S.matmul(pB[0:64, j * 2:(j + 1) * 2], rkrT[:, j, cs], hsel_b)
                    S.copy('act', bsum, pB[0:64, 0:8])
                    S.tt('dve', v3(bon), v3(V_tm), bc(bsum, 2, [64, 8, 64]), ALU.mult)
                    S.tt('dve', yc, yc, bon, ALU.add)
                    pGt = PS()
                    S.matmul(pGt[0:64, :], sg[:, cs], gup)
                    S.tt('dve', yo[0:64, :], yc, pGt[0:64, :], ALU.mult)
                    pYT = PS()
                    for j in range(4):
                        S.matmul(pYT[:, j * 64:(j + 1) * 64], yo[0:64, j * 128:(j + 1) * 128], ident_b[0:64, 0:64])
                    evac(yT[:, :, tok0:tok0 + 64], pYT[:, 0:256].rearrange("p (j t) -> p j t", t=64))
            A.release(m)

        def lru(l, yT):
            m = A.mark()
            wl = A.alloc([KD, D], BF16)
            bd = A.alloc([2, 128], BF16)
            c1 = A.alloc([4], F32)
            xbs = A.alloc([3 + 512], F32)
            xc = A.alloc([512], F32)
            xcb = A.alloc([512], BF16)
            rr = A.alloc([512], F32)
            ii = A.alloc([512], F32)
            aa = A.alloc([512], F32)
            uu = A.alloc([512], F32)
            hh = [A.alloc([512], F32) for _ in range(2)]
            ge = A.alloc([512], F32)
            for h2 in range(2):
                S.wdma(wl[:, :, h2 * 512:(h2 + 1) * 512],
                       w_in_d[l, :, LRU_OFF + h2 * 512:LRU_OFF + (h2 + 1) * 512].rearrange("(a p) c -> p a c", p=128))
            S.act(c1, PVc(l, 'lam', 0, 4), AF.Exp, scale=-1.0)
            S.act(c1, c1, AF.Ln, bias=1.0)
            S.ts('dve', c1, c1, -8.0, None, ALU.mult)
            for fc in range(4):
                S.memset('dve', bd[:, :, :], 0.0)
                for q, wd_ in enumerate((wa_d, wx_d)):
                    S.wdma(bd[0:64, q, 0:64], wd_[l, 2 * fc])
                    S.wdma(bd[64:128, q, 64:128], wd_[l, 2 * fc + 1])
                S.memset('dve', xbs[:, 0:3], 0.0)
                for tc in range(4):
                    ts_ = slice(tc * 512, (tc + 1) * 512)
                    px = PS()
                    for kc in range(KD):
                        S.matmul(px[:, :], wl[:, kc, fc * 128:(fc + 1) * 128], hT[:, kc, ts_],
                                 start=(kc == 0), stop=(kc == KD - 1))
                    pgb = PS()
                    for kc in range(KD):
                        S.matmul(pgb[:, :], wl[:, kc, 512 + fc * 128:512 + (fc + 1) * 128], hT[:, kc, ts_],
                                 start=(kc == 0), stop=(kc == KD - 1))
                    S.copy('act', xbs[:, 3:515], px[:, :])
                    S.copy('act', ge, pgb[:, :])
                    S.tt('dve', rr, ge, ge, ALU.mult)
                    S.ts('dve', rr, rr, 0.044715, 1.0, ALU.mult, ALU.add)
                    S.tt('dve', rr, rr, ge, ALU.mult)
                    sigmoid_(rr, rr, scale=1.5957691216057308)
                    S.tt('dve', ge, ge, rr, ALU.mult)
                    cw = lambda j: PVc(l, 'conv_w', j * 4 + fc)
                    S.ts('dve', xc, xbs[:, 0:512], cw(0), PVc(l, 'conv_b', fc), ALU.mult, ALU.add)
                    for j in range(1, 4):
                        S.stt('dve', xc, xbs[:, j:j + 512], cw(j), xc, ALU.mult, ALU.add)
                    S.copy('act', xbs[:, 0:3], xbs[:, 512:515])
                    S.copy('act', xcb, xc)
                    pr_ = PS()
                    S.matmul(pr_[:, :], bd[:, 0, :], xcb)
                    pi2 = PS()
                    S.matmul(pi2[:, :], bd[:, 1, :], xcb)
                    sigmoid_(rr, pr_[:, :], nbias=NPVc(l, 'ba', fc))
                    sigmoid_(ii, pi2[:, :], nbias=NPVc(l, 'bx', fc))
                    S.act(aa, rr, AF.Exp, scale=c1[:, fc:fc + 1])
                    S.tt('dve', uu, aa, aa, ALU.mult)
                    S.ts('dve', uu, uu, -1.0, 1.0, ALU.mult, ALU.add)
                    S.act(uu, uu, AF.Ln)
                    S.act(uu, uu, AF.Exp, scale=0.5)
                    S.tt('dve', ii, ii, xc, ALU.mult)
                    S.tt('dve', uu, uu, ii, ALU.mult)
                    hcur = hh[tc % 2]
                    init = 0.0 if tc == 0 else hh[(tc - 1) % 2][:, 511:512]
                    S.scan(hcur, aa, uu, init, ALU.mult, ALU.add)
                    S.tt('dve', yT[:, fc, ts_], hcur, ge, ALU.mult)
            A.release(m)

        def mla(l, yT):
            m = A.mark()
            cqnT = A.alloc([2, SEQ], BF16)
            ckvnT = A.alloc([SEQ], BF16)
            krs = A.alloc([NT, 32], F32)
            sskr = A.alloc([NT], F32)
            junk = A.alloc([256], F32)
            st4 = A.alloc([8], F32)
            m1 = A.mark()
            wm = A.alloc([KD, 416], BF16)
            cqn = A.alloc([384], BF16)
            S.wdma(wm, w_in_d[l, :, MLA_OFF:MLA_OFF + 416].rearrange("(a p) c -> p a c", p=128))
            for t in range(NT):
                p = PS()
                for kc in range(KD):
                    S.matmul(p[:, 0:416], hT[:, kc, t * 128:(t + 1) * 128], wm[:, kc, :],
                             start=(kc == 0), stop=(kc == KD - 1))
                S.act(junk[:, 0:256], p[:, 0:256], AF.Square, accum_out=st4[:, 0:1])
                S.act(junk[:, 0:128], p[:, 256:384], AF.Square, accum_out=st4[:, 1:2])
                S.copy('dve', krs[:, t, :], p[:, 384:416])
                S.act(junk[:, 0:32], p[:, 384:416], AF.Square, accum_out=sskr[:, t:t + 1])
                S.ts('dve', st4[:, 0:1], st4[:, 0:1], 1.0 / 256, EPS, ALU.mult, ALU.add)
                S.ts('dve', st4[:, 1:2], st4[:, 1:2], 1.0 / 128, EPS, ALU.mult, ALU.add)
                S.act(st4[:, 2:4], st4[:, 0:2], AF.Ln)
                S.act(st4[:, 4:6], st4[:, 2:4], AF.Exp, scale=-0.5)
                S.ts('dve', cqn[:, 0:256], p[:, 0:256], st4[:, 4:5], None, ALU.mult)
                S.ts('dve', cqn[:, 256:384], p[:, 256:384], st4[:, 5:6], None, ALU.mult)
                p2 = PS()
                for c in range(3):
                    S.matmul(p2[:, c * 128:(c + 1) * 128], cqn[:, c * 128:(c + 1) * 128], ident_b)
                for c in range(2):
                    evac(cqnT[:, c, t * 128:(t + 1) * 128], p2[:, c * 128:(c + 1) * 128], PVc(l, 'q_norm', c))
                evac(ckvnT[:, t * 128:(t + 1) * 128], p2[:, 256:384], PVc(l, 'kv_norm', 0))
            A.release(m1)
            bufs = [(A.alloc([2, 192], BF16), A.alloc([256], BF16), A.alloc([2, SEQ], BF16),
                     A.alloc([2, SEQ], BF16), A.alloc([NT, 2, 128], BF16)) for _ in range(2)]
            st16 = A.alloc([16], F32)
            qf = A.alloc([2, 96], F32)
            kfm = A.alloc([2, 96], F32)
            rt = [A.alloc([2, 16], F32) for _ in range(4)]
            qkb = A.alloc([4, 96], BF16)
            PT = [A.alloc([512], BF16) for _ in range(3)]
            rec = A.alloc([512], F32)
            qg3 = bc(PRr('q_gain'), 1, [128, 2, 96])
            kg3 = bc(PRr('k_gain'), 1, [128, 2, 96])
            scale = 96 ** -0.5
            for b_ in bufs:
                S.memset('dve', b_[4][:, :, :, :], 0.0)

            def tok_gen(pr):
                wq, wkv, QT, KT, Vp = bufs[pr % 2]
                S.wdma(wq, w_uq_d[l, :, pr * 192:(pr + 1) * 192].rearrange("(a p) c -> p a c", p=128))
                S.wdma(wkv, w_ukv_d[l, :, pr * 256:(pr + 1) * 256])
                for t in range(NT):
                    tsl = slice(t * 128, (t + 1) * 128)
                    pq = PS()
                    for c in range(2):
                        S.matmul(pq[:, 0:192], cqnT[:, c, tsl], wq[:, c, :], start=(c == 0), stop=(c == 1))
                    pk = PS()
                    S.matmul(pk[:, 0:256], ckvnT[:, tsl], wkv)
                    for hh_ in range(2):
                        S.act(junk[:, 0:96], pq[:, hh_ * 96:(hh_ + 1) * 96], AF.Square,
                              accum_out=st16[:, hh_:hh_ + 1])
                        S.act(junk[:, 0:64], pk[:, hh_ * 128:hh_ * 128 + 64], AF.Square,
                              accum_out=st16[:, 2 + hh_:3 + hh_])
                    S.ts('dve', st16[:, 2:4], st16[:, 2:4], sskr[:, t:t + 1], None, ALU.add)
                    S.ts('dve', st16[:, 0:4], st16[:, 0:4], 1.0 / 96, EPS, ALU.mult, ALU.add)
                    S.act(st16[:, 4:8], st16[:, 0:4], AF.Ln)
                    S.act(st16[:, 8:12], st16[:, 4:8], AF.Exp, scale=-0.5)
                    S.tt('dve', qf, pq[:, 0:192].rearrange("p (h c) -> p h c", c=96),
                         bc(st16[:, 8:10], 2, [128, 2, 96]), ALU.mult)
                    S.tt('dve', qf, qf, qg3, ALU.mult)
                    S.tt('dve', kfm[:, :, 0:64], pk[:, 0:256].rearrange("p (h c) -> p h c", c=128)[:, :, 0:64],
                         bc(st16[:, 10:12], 2, [128, 2, 64]), ALU.mult)
                    S.tt('dve', kfm[:, :, 64:96], bc(krs[:, t, :], 1, [128, 2, 32]),
                         bc(st16[:, 10:12], 2, [128, 2, 32]), ALU.mult)
                    S.tt('dve', kfm, kfm, kg3, ALU.mult)
                    S.copy('act', Vp[:, t, 0, 0:64], pk[:, 64:128])
                    S.copy('act', Vp[:, t, 1, 64:128], pk[:, 192:256])
                    cosb = bc(cos_t[:, t, :], 1, [128, 2, 16])
                    sinb = bc(sin_t[:, t, :], 1, [128, 2, 16])
                    for (src, o4) in ((qf, 0), (kfm, 2)):
                        x1 = src[:, :, 64:80]
                        x2 = src[:, :, 80:96]
                        S.copy('act', qkb[:, o4:o4 + 2, 0:64], src[:, :, 0:64])
                        S.tt('dve', rt[0], x1, cosb, ALU.mult)
                        S.tt('dve', rt[1], x2, sinb, ALU.mult)
                        S.tt('dve', rt[2], x2, cosb, ALU.mult)
                        S.tt('dve', rt[3], x1, sinb, ALU.mult)
                        S.tt('dve', qkb[:, o4:o4 + 2, 64:80], rt[0], rt[1], ALU.subtract)
                        S.tt('dve', qkb[:, o4:o4 + 2, 80:96], rt[2], rt[3], ALU.add)
                    pt_ = PS()
                    for i4 in range(4):
                        S.matmul(pt_[0:96, i4 * 128:(i4 + 1) * 128], qkb[:, i4, :], ident_b)
                    evac(QT[0:96, :, tsl], pt_[0:96, 0:256].rearrange("p (h t) -> p h t", t=128))
                    evac(KT[0:96, :, tsl], pt_[0:96, 256:512].rearrange("p (h t) -> p h t", t=128))
                    yield

            ibc = [0]

            def attn_gen(pr):
                wq, wkv, QT, KT, Vp = bufs[pr % 2]
                for qc in range(4):
                    po = PSA(0)
                    pd = PSA(1)
                    nkb = 4 * qc + 4
                    first = True
                    for hh_ in range(2):
                        for kb in range(nkb):
                            q0 = max(qc * 512, kb * 128)
                            n = (qc + 1) * 512 - q0
                            off = q0 - qc * 512
                            ps_ = PS()
                            S.matmul(ps_[:, 0:n], KT[0:96, hh_, kb * 128:(kb + 1) * 128], QT[0:96, hh_, q0:q0 + n])
                            pt2 = PT[ibc[0] % 3]
                            ibc[0] += 1
                            S.act(pt2[:, 0:n], ps_[:, 0:n], AF.Exp, scale=scale)
                            if kb * 128 >= qc * 512:
                                S.tt('dve', pt2[:, 0:128], pt2[:, 0:128], cmask_b, ALU.mult)
                            last = (hh_ == 1 and kb == nkb - 1)
                            S.matmul(po[:, off:off + n], Vp[:, kb, hh_, :], pt2[:, 0:n], start=first, stop=last)
                            S.matmul(pd[:, off:off + n], hpad_b[:, hh_, :], pt2[:, 0:n], start=first, stop=last)
                            first = False
                            yield
                    S.act(rec, pd[:, :], AF.Ln)
                    S.act(rec, rec, AF.Exp, scale=-1.0)
                    S.tt('dve', yT[:, pr, qc * 512:(qc + 1) * 512], po[:, :], rec, ALU.mult)
                    yield

            for _ in tok_gen(0):
                pass
            for pr in range(4):
                ga = attn_gen(pr)
                gt = tok_gen(pr + 1) if pr < 3 else None
                done_a = False
                done_t = gt is None
                while not (done_a and done_t):
                    for _ in range(5):
                        if not done_a:
                            try:
                                next(ga)
                            except StopIteration:
                                done_a = True
                    if not done_t:
                        try:
                            next(gt)
                        except StopIteration:
                            done_t = True
            A.release(m)

        def xattn(l):
            m = A.mark()
            rms_to_hT(l, 'g_xa')
            memf = A.alloc([2, D], F32)
            memn = A.alloc([2, D], BF16)
            memT = A.alloc([KD, N_MEM], BF16)
            ss = A.alloc([4], F32)
            junk = A.alloc([D], BF16)
            wkv = A.alloc([KD, D], BF16)
            wq = A.alloc([KD, 512], BF16)
            wo = A.alloc([4, D], BF16)
            KTn = A.alloc([4, N_MEM], BF16)
            V1 = A.alloc([4, 2, 128], BF16)
            sq = A.alloc([512], BF16)
            rs = A.alloc([512], F32)
            QTn = A.alloc([512], BF16)
            PT = [A.alloc([512], BF16) for _ in range(2)]
            oT = A.alloc([4, 512], BF16)
            for mt in range(2):
                S.dma('sp', memf[:, mt, :], mem_d[mt * 128:(mt + 1) * 128, :], 'mem')
            for h2 in range(2):
                S.wdma(wkv[:, :, h2 * 512:(h2 + 1) * 512],
                       xa_wkv_d[l, :, h2 * 512:(h2 + 1) * 512].rearrange("(a p) c -> p a c", p=128))
            S.wdma(wq, xa_wq_d[l].rearrange("(a p) c -> p a c", p=128))
            for h2 in range(2):
                S.wdma(wo[:, :, h2 * 512:(h2 + 1) * 512],
                       xa_wo_d[l, :, h2 * 512:(h2 + 1) * 512].rearrange("(a p) c -> p a c", p=128))
            for mt in range(2):
                S.act(junk, memf[:, mt, :], AF.Square, accum_out=ss[:, mt:mt + 1])
            S.ts('dve', ss[:, 0:2], ss[:, 0:2], 1.0 / D, EPS, ALU.mult, ALU.add)
            S.act(ss[:, 0:2], ss[:, 0:2], AF.Ln)
            S.act(ss[:, 2:4], ss[:, 0:2], AF.Exp, scale=-0.5)
            for mt in range(2):
                S.ts('dve', memn[:, mt, :], memf[:, mt, :], ss[:, 2 + mt:3 + mt], None, ALU.mult)
            for kc in range(KD):
                p = PS()
                for mt in range(2):
                    S.matmul(p[:, mt * 128:(mt + 1) * 128], memn[:, mt, kc * 128:(kc + 1) * 128], ident_b)
                evac(memT[:, kc, :], p[:, 0:256], PVc(l, 'g_mem', kc))

            def headnorm(p, n, gcol, out):
                S.act(sq[:, 0:n], p[:, 0:n], AF.Square)
                p2 = PS()
                S.matmul(p2[:, 0:n], ones_b, sq[:, 0:n])
                S.ts('dve', rs[:, 0:n], p2[:, 0:n], 1.0 / 128, EPS, ALU.mult, ALU.add)
                rsqrt_(rs[:, 0:n], rs[:, 0:n])
                S.stt('dve', out, p[:, 0:n], gcol, rs[:, 0:n], ALU.mult, ALU.mult)

            S.memset('dve', V1[:, :, :, :], 0.0)
            for h in range(4):
                p = PS()
                for kc in range(KD):
                    S.matmul(p[:, 0:N_MEM], wkv[:, kc, h * 256:h * 256 + 128], memT[:, kc, :],
                             start=(kc == 0), stop=(kc == KD - 1))
                headnorm(p, N_MEM, PVc(l, 'xa_kg', 0), KTn[:, h, :])
                for mt in range(2):
                    pv = PS()
                    for kc in range(KD):
                        S.matmul(pv[:, 0:128], memT[:, kc, mt * 128:(mt + 1) * 128],
                                 wkv[:, kc, h * 256 + 128:h * 256 + 256], start=(kc == 0), stop=(kc == KD - 1))
                    evac(V1[:, h, mt, :], pv[:, 0:128])
            scale = 128 ** -0.5
            ib = 0
            for tc in range(4):
                ts_ = slice(tc * 512, (tc + 1) * 512)
                for h in range(4):
                    p = PS()
                    for kc in range(KD):
                        S.matmul(p[:, :], wq[:, kc, h * 128:(h + 1) * 128], hT[:, kc, ts_],
                                 start=(kc == 0), stop=(kc == KD - 1))
                    headnorm(p, 512, PVc(l, 'xa_qg', 0), QTn)
                    po = PSA(0)
                    pd = PSA(1)
                    for mt in range(2):
                        ps_ = PS()
                        S.matmul(ps_[:, :], KTn[:, h, mt * 128:(mt + 1) * 128], QTn)
                        pt2 = PT[ib % 2]
                        ib += 1
                        S.act(pt2, ps_[:, :], AF.Exp, scale=scale)
                        S.matmul(po[:, :], V1[:, h, mt, :], pt2, start=(mt == 0), stop=(mt == 1))
                        S.matmul(pd[:, :], ones_b, pt2, start=(mt == 0), stop=(mt == 1))
                    S.act(rs, pd[:, :], AF.Ln)
                    S.act(rs, rs, AF.Exp, scale=-1.0)
                    S.tt('dve', oT[:, h, :], po[:, :], rs, ALU.mult)
                for j in range(4):
                    t = tc * 4 + j
                    for h2 in range(2):
                        p = PS()
                        for c in range(4):
                            S.matmul(p[:, :], oT[:, c, j * 128:(j + 1) * 128], wo[:, c, h2 * 512:(h2 + 1) * 512],
                                     start=(c == 0), stop=(c == 3))
                        S.tt('dve', xres[:, t, h2 * 512:(h2 + 1) * 512], xres[:, t, h2 * 512:(h2 + 1) * 512],
                             p[:, :], ALU.add)
            A.release(m)

        def ffn(l):
            m = A.mark()
            rms_to_hT(l, 'g_ffn')
            NF = 11
            uT = A.alloc([NF, SEQ], BF16)
            w2g = A.alloc([NF, D], BF16)
            w1b = [A.alloc([KD, 256], BF16) for _ in range(2)]
            w3b = [A.alloc([KD, 256], BF16) for _ in range(2)]
            sl = [A.alloc([512], F32) for _ in range(2)]
            it = 0
            blocks = [(g, b) for g in range(2) for b in range(6)]

            def load_blk(i):
                if i >= len(blocks):
                    return
                g_, b_ = blocks[i]
                nb_ = 2 if b_ < 5 else 1
                c0_ = g_ * NF * 128 + b_ * 256
                S.wdma(w1b[i % 2][:, :, 0:nb_ * 128], w1_d[l, :, c0_:c0_ + nb_ * 128].rearrange("(a p) c -> p a c", p=128))
                S.wdma(w3b[i % 2][:, :, 0:nb_ * 128], w3_d[l, :, c0_:c0_ + nb_ * 128].rearrange("(a p) c -> p a c", p=128))

            load_blk(0)
            for g in range(2):
                f0 = g * NF * 128
                for h2 in range(2):
                    S.wdma(w2g[:, :, h2 * 512:(h2 + 1) * 512],
                           w2_d[l, f0:f0 + NF * 128, h2 * 512:(h2 + 1) * 512].rearrange("(a p) c -> p a c", p=128))
                for b in range(6):
                    nb = 2 if b < 5 else 1
                    s_ = (g * 6 + b) % 2
                    load_blk(g * 6 + b + 1)
                    for f in range(nb):
                        fo = b * 2 + f
                        for tc in range(4):
                            ts_ = slice(tc * 512, (tc + 1) * 512)
                            p1 = PS()
                            for kc in range(KD):
                                S.matmul(p1[:, :], w1b[s_][:, kc, f * 128:(f + 1) * 128], hT[:, kc, ts_],
                                         start=(kc == 0), stop=(kc == KD - 1))
                            p3 = PS()
                            for kc in range(KD):
                                S.matmul(p3[:, :], w3b[s_][:, kc, f * 128:(f + 1) * 128], hT[:, kc, ts_],
                                         start=(kc == 0), stop=(kc == KD - 1))
                            s1 = sl[it % 2]
                            it += 1
                            S.act(s1, p1[:, :], AF.Silu)
                            S.tt('dve', uT[:, fo, ts_], s1, p3[:, :], ALU.mult)
                for t in range(NT):
                    for h2 in range(2):
                        p = PS()
                        for f in range(NF):
                            S.matmul(p[:, :], uT[:, f, t * 128:(t + 1) * 128], w2g[:, f, h2 * 512:(h2 + 1) * 512],
                                     start=(f == 0), stop=(f == NF - 1))
                        S.tt('dve', xres[:, t, h2 * 512:(h2 + 1) * 512], xres[:, t, h2 * 512:(h2 + 1) * 512],
                             p[:, :], ALU.add)
            A.release(m)

        cmask_b = cb[:, 386:514] if False else None
        cb2 = sb("cstb2", [128, 384], BF16)
        cmask_b = cb2[:, 0:128]
        hpad_b = cb2[:, 128:384].rearrange("p (h c) -> p h c", c=128)
        S.copy('dve', cmask_b, C('cmask'))
        S.memset('dve', hpad_b[:, :, :], 0.0)
        S.memset('dve', hpad_b[:, 0, 0:64], 1.0)
        S.memset('dve', hpad_b[:, 1, 64:128], 1.0)

        def tap_tm(name, src_tm):
            pass

        for l in range(layers):
            S.dma('sp', prow[:, :], prow_d[l], 'c3')
            rms_to_hT(l, 'g_mix')
            for n, fn in enumerate((rwkv, lru, mla)):
                m = A.mark()
                yT = A.alloc([4, SEQ], BF16)
                if ('rwkv', 'lru', 'mla')[n] not in phases:
                    A.release(m)
                    continue
                fn(l, yT)
                if ('y%d_%d' % (n, l)) in tap_d:
                    yf = A.alloc([256], F32)
                    for fc in range(4):
                        for q8 in range(8):
                            S.ts('dve', yf, yT[:, fc, q8 * 256:(q8 + 1) * 256], 1e30, -1e30, ALU.min, ALU.max)
                            o = S.dma('sp', tap_d['y%d_%d' % (n, l)][fc * 128:(fc + 1) * 128, q8 * 256:(q8 + 1) * 256], yf, 'tap')
                            outs.append(o)
                if 'merge' in phases:
                    merge(l, n, yT)
                A.release(m)
            if ('x1_%d' % l) in tap_d:
                for t in range(NT):
                    outs.append(S.dma('sp', tap_d['x1_%d' % l][t * 128:(t + 1) * 128, :], xres[:, t, :], 'tap'))
            if 'xattn' in phases:
                xattn(l)
            if ('x2_%d' % l) in tap_d:
                for t in range(NT):
                    outs.append(S.dma('sp', tap_d['x2_%d' % l][t * 128:(t + 1) * 128, :], xres[:, t, :], 'tap'))
            if 'ffn' in phases:
                ffn(l)
        for t in range(NT):
            outs.append(S.dma('sp', out_d[t * 128:(t + 1) * 128, :], xres[:, t, :], force=True))
        S.fence('sp', outs)
        S.emit()
        build.stats = (len(S.ops), A.peak)
    return nc


def _cols(v):
    v = np.ascontiguousarray(v, dtype=np.float32).reshape(-1)
    return v.reshape(-1, 128).T


def _consts():
    c = np.zeros((128, NCS), np.float32)

    def put(name, a):
        o, w = CS[name]
        c[:a.shape[0], o:o + w] = a
    put('ident', np.eye(128, dtype=np.float32))
    put('identblk', np.concatenate([np.eye(64), np.eye(64)], 0).astype(np.float32))
    put('mask_s', np.tril(np.ones((64, 64), np.float32), -1))
    put('maskT_s', np.triu(np.ones((64, 64), np.float32), 1))
    put('cmask', np.triu(np.ones((128, 128), np.float32), 0))
    bo = np.zeros((128, 128), np.float32)
    bo[:64, :64] = 1
    bo[64:, 64:] = 1
    put('blockones', bo)
    hs = np.zeros((128, 2), np.float32)
    hs[:64, 0] = 1
    hs[64:, 1] = 1
    put('headsel', hs)
    cm = np.ones((128, 256), np.float32)
    cm[:, ::64] = 0
    put('chunkmask', cm)
    inv = (10000.0 ** (-np.arange(0, 32, 2, dtype=np.float32) / 32)).astype(np.float32)
    put('invfreq', np.broadcast_to(inv, (128, 16)))
    put('ones', np.ones((128, 128), np.float32))
    return c


def _pack(inputs):
    pv = np.zeros((DEPTH, 128, NPV), np.float32)
    pr = np.zeros((DEPTH, 128, NPR), np.float32)
    src = {'g_mix': 'norm_mix', 'g_xa': 'norm_xattn', 'g_mem': 'norm_mem', 'g_ffn': 'norm_ffn',
           'b_gate': 'b_gate', 'mu': 'rwkv_mu', 'w0': 'rwkv_w0', 'a0': 'rwkv_a0', 'k_k': 'rwkv_k_k',
           'k_a': 'rwkv_k_a', 'r_k': 'rwkv_r_k', 'conv_w': 'lru_conv_w', 'conv_b': 'lru_conv_b',
           'ba': 'lru_ba', 'bx': 'lru_bx', 'lam': 'lru_lambda', 'q_norm': 'mla_q_norm',
           'kv_norm': 'mla_kv_norm', 'xa_qg': 'xa_q_gain', 'xa_kg': 'xa_k_gain'}
    for l in range(DEPTH):
        for k, (o, w) in PV.items():
            pv[l, :, o:o + w] = _cols(inputs[src[k]][l])
        for k, s in (('ln_g', 'rwkv_ln_g'), ('ln_b', 'rwkv_ln_b'), ('q_gain', 'mla_q_gain'), ('k_gain', 'mla_k_gain')):
            o, w = PR[k]
            pr[l, :, o:o + w] = np.broadcast_to(np.asarray(inputs[s][l], np.float32).reshape(1, w), (128, w))
    return pv, pr


_WNAMES = ['w_in', 'rwkv_w_up', 'rwkv_a_up', 'rwkv_g_up', 'lru_wa', 'lru_wx', 'mla_w_uq', 'mla_w_ukv',
           'w_branch', 'w_out', 'xa_w_q', 'xa_w_kv', 'xa_w_o', 'ffn_w1', 'ffn_w3', 'ffn_w2']


def make_in_maps(inputs, cores):
    pv, pr = _pack(inputs)
    cst = _consts()
    shared = {k: np.ascontiguousarray(inputs[k], dtype=np.float32) for k in _WNAMES}
    shared.update({'cst': cst, 'pvec': pv, 'prow': pr})
    maps = []
    for b in cores:
        mm = dict(shared)
        mm['x'] = np.ascontiguousarray(inputs['x'][b], dtype=np.float32)
        mm['mem'] = np.ascontiguousarray(inputs['mem'][b], dtype=np.float32)
        mm['pos'] = np.ascontiguousarray(np.asarray(inputs['positions'][b], np.int32).reshape(NT, 128).T)
        maps.append(mm)
    return maps


def kernel(**inputs):
    inputs = {k: np.asarray(v) for k, v in inputs.items()}
    nc = bass.Bass("TRN2", target_bir_lowering=False)
    build(nc)
    maps = make_in_maps(inputs, list(range(8)))
    res = run_bass_kernel_spmd(nc, maps, core_ids=list(range(8)))
    return np.stack([np.asarray(r["out"], np.float32) for r in res.results], axis=0)
```

```python
import contextlib
import os
import sys
import numpy as np
import concourse.bass as bass
import concourse.mybir as mybir
from concourse.bass_utils import run_bass_kernel_spmd

F32 = mybir.dt.float32
BF16 = mybir.dt.bfloat16
I32 = mybir.dt.int32
AF = mybir.ActivationFunctionType
ALU = mybir.AluOpType
AX = mybir.AxisListType

_ESZ = {F32: 4, BF16: 2, I32: 4}

D = 1024
SEQ = 2048
NT = 16
KD = 8
DEPTH = 2
N_MEM = 256
RW = 512
RWKV_IN = 1792
LRU_OFF = 1792
MLA_OFF = 2816
GATE_OFF = 3232
D_IN = 6304
D_FF = 2816
EPS = 1e-6
TWO_PI = 6.283185307179586

PV = {}
_o = 0
for _n, _w in [('g_mix', 8), ('g_xa', 8), ('g_mem', 8), ('g_ffn', 8), ('b_gate', 24), ('mu', 14),
               ('w0', 4), ('a0', 4), ('k_k', 4), ('k_a', 4), ('r_k', 4), ('conv_w', 16),
               ('conv_b', 4), ('ba', 4), ('bx', 4), ('lam', 4), ('q_norm', 2), ('kv_norm', 1),
               ('xa_qg', 1), ('xa_kg', 1)]:
    PV[_n] = (_o, _w)
    _o += _w
NPV = _o
PR = {'ln_g': (0, 512), 'ln_b': (512, 512), 'q_gain': (1024, 96), 'k_gain': (1120, 96)}
NPR = 1216
CS = {}
_o = 0
for _n, _w in [('ident', 128), ('identblk', 64), ('mask_s', 64), ('maskT_s', 64), ('cmask', 128),
               ('blockones', 128), ('headsel', 2), ('chunkmask', 256), ('invfreq', 16), ('ones', 128)]:
    CS[_n] = (_o, _w)
    _o += _w
NCS = _o


def _box(ap):
    t = ap.tensor
    esz = _ESZ[ap.dtype]
    apl = ap.ap
    off = int(ap.offset) * esz
    sp = str(ap.space)
    if sp == 'PSUM':
        return (t.name, 0, 127, 0, 1 << 20)
    if sp != 'DRAM':
        row = esz
        for s in t.shape[1:]:
            row *= int(s)
        pstep, pcnt = apl[0]
        p0 = off // row
        f0 = off % row
        pn = (pstep * esz) // row if pcnt > 1 else 1
        p1 = p0 + (pcnt - 1) * max(pn, 0)
        ext = 0
        for s, c in apl[1:]:
            ext += (c - 1) * abs(s)
        return (t.name, p0, p1, f0, f0 + (ext + 1) * esz)
    ext = 0
    for s, c in apl:
        ext += (c - 1) * abs(s)
    return (t.name, 0, 0, off, off + (ext + 1) * esz)


def _ovl(a, b):
    return a[1] <= b[2] and b[1] <= a[2] and a[3] < b[4] and b[3] < a[4]


def _covers(a, b):
    return a[1] <= b[1] and a[2] >= b[2] and a[3] <= b[3] and a[4] >= b[4]


class Op:
    __slots__ = ('eng', 'fn', 'deps', 'signal', 'sem', 'val', 'is_dma', 'idx', 'waits', 'key', 'clock', 'line', 'rows', 'odeps', 'cost', 'succ', 'npend')


class Sched:
    def __init__(self, nc):
        self.nc = nc
        self.ops = []
        self.recs = {}
        self.wk = 0
        self.serial_w = True
        self.wprev = None

    def add(self, eng, fn, reads=(), writes=(), dma_key=None, extra_deps=(), force=False, rows=None, cost=None):
        lim = os.environ.get('MAXOPS')
        if lim is not None and len(self.ops) >= int(lim) and not force:
            return None
        extra_deps = [d for d in extra_deps if d is not None]
        op = Op()
        op.line = (sys._getframe(2).f_lineno, sys._getframe(1).f_code.co_name)
        op.eng = eng
        op.rows = rows
        op.fn = fn
        op.is_dma = dma_key is not None
        op.key = dma_key
        op.signal = op.is_dma
        op.idx = len(self.ops)
        deps = {}
        rb = [_box(a) for a in reads]
        wb = [_box(a) for a in writes]
        for b in rb:
            psum = b[4] == (1 << 20)
            for rec in self.recs.get(b[0], ()):
                if _ovl(rec[0], b):
                    if rec[2]:
                        deps[rec[1]] = deps.get(rec[1], 0) | 1
                    elif psum:
                        deps[rec[1]] = deps.get(rec[1], 0) | 4
        for b in wb:
            for rec in self.recs.get(b[0], ()):
                if _ovl(rec[0], b):
                    deps[rec[1]] = deps.get(rec[1], 0) | 2
        for d in extra_deps:
            deps[d.idx] = 1
        need = []
        op.odeps = [self.ops[di] for di, kind in deps.items() if kind != 4]
        if cost is None:
            fd = 1
            if writes:
                for s_ in writes[0].shape[1:]:
                    fd *= int(s_)
            if op.is_dma:
                cost = 2000 + fd * 128 * 4 / 250.0
            elif eng == 'pe':
                cost = 70 + max(fd, 64) * 0.45
            elif eng == 'act':
                cost = 200 + fd * 0.85
            else:
                cost = 80 + fd * 1.05
        op.cost = cost
        for di, kind in deps.items():
            d = self.ops[di]
            if not d.is_dma and not op.is_dma and d.eng == eng:
                if kind == 4:
                    continue
                if eng == 'pe':
                    if not ((kind & 2) and rows is not None and d.rows is not None and
                            (rows[1] <= d.rows[0] or d.rows[1] <= rows[0])):
                        continue
            d.signal = True
            need.append(d)
        op.deps = need
        have = {d.idx for d in op.odeps}
        for d in need:
            if d.idx not in have:
                op.odeps.append(d)
        self.ops.append(op)
        for b in wb:
            lst = self.recs.setdefault(b[0], [])
            lst[:] = [r for r in lst if not _covers(b, r[0])]
            lst.append((b, op.idx, True))
        for b in rb:
            lst = self.recs.setdefault(b[0], [])
            lst.append((b, op.idx, False))
        return op

    def matmul(self, out, lhsT, rhs, start=True, stop=True):
        b0 = _box(lhsT)
        return self.add('pe', lambda e: e.matmul(out, lhsT, rhs, start=start, stop=stop),
                        reads=[lhsT, rhs], writes=[out], rows=(b0[1], b0[2] + 1))

    def act(self, out, in_, func, bias=None, scale=None, accum_out=None):
        kw = {}
        reads = [in_]
        writes = [out]
        if bias is not None:
            kw['bias'] = bias
            if not isinstance(bias, (int, float)):
                reads.append(bias)
        if scale is not None:
            kw['scale'] = scale
            if not isinstance(scale, (int, float)):
                reads.append(scale)
        if accum_out is not None:
            kw['accum_out'] = accum_out
            writes.append(accum_out)
        return self.add('act', lambda e: e.activation(out, in_, func, **kw), reads, writes)

    def tt(self, eng, out, in0, in1, op):
        return self.add(eng, lambda e: e.tensor_tensor(out, in0, in1, op), [in0, in1], [out])

    def ts(self, eng, out, in0, s1, s2, op0, op1=None):
        reads = [in0]
        for s in (s1, s2):
            if s is not None and not isinstance(s, (int, float)):
                reads.append(s)
        if op1 is None:
            return self.add(eng, lambda e: e.tensor_scalar(out, in0, s1, None, op0), reads, [out])
        return self.add(eng, lambda e: e.tensor_scalar(out, in0, s1, s2, op0, op1), reads, [out])

    def stt(self, eng, out, in0, scalar, in1, op0, op1):
        reads = [in0, in1]
        if not isinstance(scalar, (int, float)):
            reads.append(scalar)
        return self.add(eng, lambda e: e.scalar_tensor_tensor(out, in0, scalar, in1, op0, op1),
                        reads, [out])

    def copy(self, eng, out, in_):
        if eng == 'act':
            return self.add('act', lambda e: e.copy(out, in_), [in_], [out])
        return self.add(eng, lambda e: e.tensor_copy(out, in_), [in_], [out])

    def reduce(self, eng, out, in_, op):
        return self.add(eng, lambda e: e.tensor_reduce(out, in_, AX.X, op), [in_], [out])

    def recip(self, out, in_):
        return self.add('dve', lambda e: e.reciprocal(out, in_), [in_], [out])

    def memset(self, eng, out, val):
        return self.add(eng, lambda e: e.memset(out, val), [], [out])

    def scan(self, out, d0, d1, initial, op0, op1):
        reads = [d0, d1]
        if not isinstance(initial, (int, float)):
            reads.append(initial)
        return self.add('dve', lambda e: e.tensor_tensor_scan(out, d0, d1, initial, op0, op1),
                        reads, [out])

    def dma(self, eng, out, in_, key=None, force=False):
        NQ = 8
        self.dk = getattr(self, 'dk', 0) + 1
        k = '%s%d' % (eng, self.dk % NQ)
        if not hasattr(self, 'dprev'):
            self.dprev = {}
        prev = [self.dprev[k]] if k in self.dprev else []
        op = self.add(eng, lambda e: e.dma_start(out=out, in_=in_), [in_], [out], dma_key=k,
                      extra_deps=prev, force=force)
        if op is not None:
            self.dprev[k] = op
        return op

    def wdma(self, out, in_):
        NW = int(os.environ.get('NW', '3'))
        self.wk = (self.wk + 1) % NW
        k = 'w%d' % self.wk
        if not hasattr(self, 'wprevs'):
            self.wprevs = {}
        prev = [self.wprevs[k]] if k in self.wprevs else []
        op = self.add('pool', lambda e: e.dma_start(out=out, in_=in_), [in_], [out],
                      dma_key=k, extra_deps=prev)
        if op is not None:
            self.wprevs[k] = op
        return op

    def fence(self, eng, deps):
        return self.add(eng, None, extra_deps=deps, force=True)

    def schedule(self):
        import heapq
        ops = self.ops
        for op in ops:
            op.succ = []
        for op in ops:
            uniq = {d.idx: d for d in op.odeps}
            op.odeps = list(uniq.values())
            op.npend = len(op.odeps)
            for d in op.odeps:
                d.succ.append(op)
        fin = [0.0] * len(ops)
        engs = ['pe', 'act', 'dve', 'pool', 'sp']
        free = {e: 0.0 for e in engs}
        wait_h = {e: [] for e in engs}
        rdy_h = {e: [] for e in engs}
        WIN = int(os.environ.get('SCHED_WIN', '1000'))
        LAT_S = float(os.environ.get('SCHED_LAT_S', '0'))
        LAT_X = float(os.environ.get('SCHED_LAT_X', '50'))

        def push(op):
            rt = 0.0
            for d in op.odeps:
                lat = LAT_S if (d.eng == op.eng and not d.is_dma) else LAT_X
                rt = max(rt, fin[d.idx] + lat)
            heapq.heappush(wait_h[op.eng], (rt, op.idx))
        for op in ops:
            if op.npend == 0:
                push(op)
        order = []
        nsched = 0
        lowest = 0
        done = [False] * len(ops)
        while nsched < len(ops):
            best = None
            for e in engs:
                wh, rh = wait_h[e], rdy_h[e]
                while wh and wh[0][0] <= free[e]:
                    heapq.heappush(rh, heapq.heappop(wh)[1])
                if rh:
                    cand = (free[e], rh[0], e, True)
                elif wh:
                    cand = (wh[0][0], wh[0][1], e, False)
                else:
                    continue
                if cand[1] > lowest + WIN:
                    cand = (cand[0] + 1e9, cand[1], e, cand[3])
                if best is None or cand[:2] < best[:2]:
                    best = cand
            st_, idx, e, from_rdy = best
            if st_ >= 1e9:
                st_ -= 1e9
            if from_rdy:
                heapq.heappop(rdy_h[e])
            else:
                heapq.heappop(wait_h[e])
            op = ops[idx]
            start = max(st_, free[e])
            fin[idx] = start + op.cost
            free[e] = fin[idx] if not op.is_dma else start + 60.0
            order.append(op)
            done[idx] = True
            nsched += 1
            while lowest < len(ops) and done[lowest]:
                lowest += 1
            for s2 in op.succ:
                s2.npend -= 1
                if s2.npend == 0:
                    push(s2)
        self.est_time = max(fin) if fin else 0.0
        return order

    def emit(self):
        nc = self.nc
        ops = self.schedule() if os.environ.get('NOSCHED') is None else self.ops
        engs = ['pe', 'act', 'dve', 'pool', 'sp']
        keys = []
        for op in ops:
            if op.is_dma and op.signal and op.key not in keys:
                keys.append(op.key)
        with contextlib.ExitStack() as st:
            sems = {}
            for e in engs:
                sems[e] = st.enter_context(nc.semaphore('s_' + e))
            for k in keys:
                sems[('dma', k)] = st.enter_context(nc.semaphore('d_' + str(k)))
            cnt = {}
            clock = {e: {} for e in engs}
            for op in ops:
                ck = clock[op.eng]
                need = {}
                for d in op.deps:
                    if ck.get(d.sem, 0) < d.val:
                        need[d.sem] = max(need.get(d.sem, 0), d.val)
                for d in op.deps:
                    for s, v in d.clock.items():
                        if ck.get(s, 0) < v:
                            ck[s] = v
                    if ck.get(d.sem, 0) < d.val:
                        ck[d.sem] = d.val
                op.waits = list(need.items())
                if op.signal:
                    if op.is_dma:
                        sk = ('dma', op.key)
                        cnt[sk] = cnt.get(sk, 0) + 16
                    else:
                        sk = op.eng
                        cnt[sk] = cnt.get(sk, 0) + 1
                    op.sem = sk
                    op.val = cnt[sk]
                    op.clock = dict(ck)
                else:
                    op.sem = None
                    op.val = 0
                    op.clock = None
            per = {e: [o for o in ops if o.eng == e] for e in engs}

            def run(e_name):
                def body(eng):
                    for op in per[e_name]:
                        for s, v in op.waits:
                            eng.wait_ge(sems[s], v)
                        if op.fn is None:
                            continue
                        ins = op.fn(eng)
                        if op.signal:
                            ins.then_inc(sems[op.sem], 16 if op.is_dma else 1)
                return body

            with nc.Block() as block:
                block.tensor(run('pe'))
                block.scalar(run('act'))
                block.vector(run('dve'))
                block.gpsimd(run('pool'))
                block.sync(run('sp'))


class Arena:
    def __init__(self, t, words):
        self.t = t
        self.off = 0
        self.words = words
        self.peak = 0

    def mark(self):
        return self.off

    def release(self, m):
        self.off = m

    def alloc(self, shape, dtype):
        n = 1
        for s in shape:
            n *= s
        esz = _ESZ[dtype]
        w = (n * esz + 3) // 4
        w = (w + 7) // 8 * 8
        assert self.off + w <= self.words, ('arena overflow', self.off, w, self.words)
        v = self.t[:, self.off:self.off + w]
        self.off += w
        self.peak = max(self.peak, self.off)
        if dtype != F32:
            v = v.bitcast(dtype)
        v = v[:, 0:n]
        if len(shape) == 2:
            v = v.rearrange("p (a b) -> p a b", b=shape[1])
        elif len(shape) == 3:
            v = v.rearrange("p (a b c) -> p a b c", b=shape[1], c=shape[2])
        return v


def bc(ap, axis, shape):
    return ap.unsqueeze(axis).to_broadcast(shape)


def build(nc, layers=DEPTH, taps=(), phases=('rwkv', 'lru', 'mla', 'merge', 'xattn', 'ffn')):
    dr = lambda n, s, d=F32, k="ExternalInput": nc.dram_tensor(n, s, d, kind=k).ap()
    x_d = dr("x", [SEQ, D])
    mem_d = dr("mem", [N_MEM, D])
    pos_d = dr("pos", [128, NT], I32)
    cst_d = dr("cst", [128, NCS])
    pvec_d = dr("pvec", [DEPTH, 128, NPV])
    prow_d = dr("prow", [DEPTH, 128, NPR])
    w_in_d = dr("w_in", [DEPTH, D, D_IN])
    w_up_d = dr("rwkv_w_up", [DEPTH, 64, RW])
    a_up_d = dr("rwkv_a_up", [DEPTH, 64, RW])
    g_up_d = dr("rwkv_g_up", [DEPTH, 128, RW])
    wa_d = dr("lru_wa", [DEPTH, 8, 64, 64])
    wx_d = dr("lru_wx", [DEPTH, 8, 64, 64])
    w_uq_d = dr("mla_w_uq", [DEPTH, 256, 768])
    w_ukv_d = dr("mla_w_ukv", [DEPTH, 128, 1024])
    w_br_d = dr("w_branch", [DEPTH, 3, 512, D])
    w_out_d = dr("w_out", [DEPTH, D, D])
    xa_wq_d = dr("xa_w_q", [DEPTH, D, 512])
    xa_wkv_d = dr("xa_w_kv", [DEPTH, D, D])
    xa_wo_d = dr("xa_w_o", [DEPTH, 512, D])
    w1_d = dr("ffn_w1", [DEPTH, D, D_FF])
    w3_d = dr("ffn_w3", [DEPTH, D, D_FF])
    w2_d = dr("ffn_w2", [DEPTH, D_FF, D])
    out_d = dr("out", [SEQ, D], F32, "ExternalOutput")
    tap_d = {}
    for name, shape in taps:
        tap_d[name] = dr("tap_" + name, shape, F32, "ExternalOutput")

    ARW = 24800
    with contextlib.ExitStack() as st:
        sb = lambda n, s, d: st.enter_context(nc.sbuf_tensor(n, s, d))
        xres = sb("xres", [128, NT, D], F32)
        hT = sb("hT", [128, KD, SEQ], BF16)
        cst = sb("cstf", [128, NCS], F32)
        cb = sb("cstb", [128, 512], BF16)
        pvec = sb("pvec_s", [128, DEPTH, NPV], F32)
        prow = sb("prow_s", [128, NPR], F32)
        omu = sb("omu", [128, 32], F32)
        npv = sb("npv", [128, DEPTH, NPV], F32)
        cos_t = sb("cos_t", [128, NT, 16], F32)
        sin_t = sb("sin_t", [128, NT, 16], F32)
        art = sb("arena", [128, ARW], F32)
        pst = [st.enter_context(nc.psum_tensor("ps%d" % i, [128, 512], F32)) for i in range(8)]
        S = Sched(nc)
        A = Arena(art, ARW)
        psi = [0]

        def PS():
            psi[0] = (psi[0] + 1) % 6
            return pst[psi[0]]

        def PSA(i):
            return pst[6 + i]

        def C(name, rows=slice(0, 128)):
            o, w = CS[name]
            return cst[rows, o:o + w]

        def PVc(l, name, j=0, n=1, rows=slice(0, 128)):
            o, w = PV[name]
            return pvec[rows, l, o + j:o + j + n]

        def NPVc(l, name, j=0, n=1):
            o, w = PV[name]
            return npv[:, l, o + j:o + j + n]

        def PRr(name, rows=slice(0, 128)):
            o, w = PR[name]
            return prow[rows, o:o + w]

        ident_b = cb[:, 0:128]
        bones_b = cb[:, 128:256]
        ones_b = cb[:, 256:384]
        hsel_b = cb[:, 384:386]
        tapn = [0]

        def tap(name, dst_ap, src_ap):
            if name not in tap_d:
                return
            tapn[0] += 1
            outs.append(S.dma('sp', dst_ap, src_ap, 'tap'))

        outs = []
        evn = [0]

        def evac(out, in_, scale_ap=None):
            evn[0] += 1
            if evn[0] % 2 == 0:
                if scale_ap is None:
                    S.copy('act', out, in_)
                else:
                    S.act(out, in_, AF.Copy, scale=scale_ap)
            else:
                if scale_ap is None:
                    S.copy('dve', out, in_)
                else:
                    S.ts('dve', out, in_, scale_ap, None, ALU.mult)

        S.dma('sp', cst[:, :], cst_d, 'c0')
        S.dma('sp', pvec[:, :, :], pvec_d.rearrange("l p c -> p l c"), 'c1')
        for t in range(NT):
            S.dma('sp', xres[:, t, :], x_d[t * 128:(t + 1) * 128, :], 'x%d' % (t % 4))
        S.ts('dve', npv[:, :, :], pvec[:, :, :], -1.0, None, ALU.mult)
        S.copy('dve', ident_b, C('ident'))
        S.copy('dve', bones_b, C('blockones'))
        S.copy('dve', ones_b, C('ones'))
        S.copy('dve', hsel_b, C('headsel'))
        m0 = A.mark()
        pi_ = A.alloc([NT], I32)
        pf = A.alloc([NT], F32)
        ang = A.alloc([NT, 16], F32)
        kf = A.alloc([NT, 16], F32)
        ki = A.alloc([NT, 16], I32)
        S.dma('sp', pi_, pos_d, 'c2')
        S.copy('dve', pf, pi_)
        S.tt('dve', ang, bc(pf, 2, [128, NT, 16]), bc(C('invfreq'), 1, [128, NT, 16]), ALU.mult)
        for (dst, shift) in ((sin_t, 0.0), (cos_t, TWO_PI / 4)):
            a2 = ang
            if shift != 0.0:
                a2 = A.alloc([NT, 16], F32)
                S.ts('dve', a2, ang, shift, None, ALU.add)
            S.ts('dve', kf, a2, 1.0 / TWO_PI, None, ALU.mult)
            S.copy('dve', ki, kf)
            S.copy('dve', kf, ki)
            S.stt('dve', kf, kf, -TWO_PI, a2, ALU.mult, ALU.add)
            S.ts('dve', kf, kf, 3.14159, -3.14159, ALU.min, ALU.max)
            S.act(dst[:, :, :], kf, AF.Sin)
        A.release(m0)

        def rsqrt_(out, in_):
            S.act(out, in_, AF.Ln)
            S.act(out, out, AF.Exp, scale=-0.5)

        def sigmoid_(out, in_, nbias=None, scale=1.0, tmp_=None):
            t_ = out if tmp_ is None else tmp_
            if nbias is None:
                S.act(t_, in_, AF.Exp, scale=-scale)
            else:
                S.act(t_, in_, AF.Exp, scale=-scale, bias=nbias)
            S.act(t_, t_, AF.Ln, bias=1.0)
            S.act(out, t_, AF.Exp, scale=-1.0)

        def rms_to_hT(l, gname):
            m = A.mark()
            junk = A.alloc([D], BF16)
            ss = A.alloc([NT], F32)
            rstd = A.alloc([NT], F32)
            xn = A.alloc([4, D], BF16)
            for t in range(NT):
                S.act(junk, xres[:, t, :], AF.Square, accum_out=ss[:, t:t + 1])
            S.ts('dve', rstd, ss, 1.0 / D, EPS, ALU.mult, ALU.add)
            rsqrt_(rstd, rstd)
            for tg in range(4):
                for j in range(4):
                    t = tg * 4 + j
                    S.ts('dve', xn[:, j, :], xres[:, t, :], rstd[:, t:t + 1], None, ALU.mult)
                for kc in range(KD):
                    p = PS()
                    for j in range(4):
                        S.matmul(p[:, j * 128:(j + 1) * 128], xn[:, j, kc * 128:(kc + 1) * 128], ident_b)
                    evac(hT[:, kc, tg * 512:(tg + 1) * 512], p[:, :], PVc(l, gname, kc))
            A.release(m)

        def merge(l, n, yT):
            m = A.mark()
            gp = A.alloc([KD, SEQ], BF16)
            wout = A.alloc([KD, D], BF16)
            wg = [A.alloc([KD, 512], BF16) for _ in range(2)]
            wb_ = [A.alloc([4, 512], BF16) for _ in range(2)]
            gt = [A.alloc([512], F32) for _ in range(2)]
            for fq in range(2):
                c0 = GATE_OFF + n * D + fq * 512
                S.wdma(wg[fq], w_in_d[l, :, c0:c0 + 512].rearrange("(a p) c -> p a c", p=128))
                S.wdma(wb_[fq], w_br_d[l, n, :, fq * 512:(fq + 1) * 512].rearrange("(a p) c -> p a c", p=128))
            for h2 in range(2):
                S.wdma(wout[:, :, h2 * 512:(h2 + 1) * 512],
                       w_out_d[l, :, h2 * 512:(h2 + 1) * 512].rearrange("(a p) c -> p a c", p=128))
            it = 0
            for fq in range(2):
                for f in range(4):
                    fo = fq * 4 + f
                    for tc in range(4):
                        ts_ = slice(tc * 512, (tc + 1) * 512)
                        pg = PS()
                        for kc in range(KD):
                            S.matmul(pg[:, :], wg[fq][:, kc, f * 128:(f + 1) * 128], hT[:, kc, ts_],
                                     start=(kc == 0), stop=(kc == KD - 1))
                        pp = PS()
                        for kc in range(4):
                            S.matmul(pp[:, :], wb_[fq][:, kc, f * 128:(f + 1) * 128], yT[:, kc, ts_],
                                     start=(kc == 0), stop=(kc == 3))
                        g = gt[it % 2]
                        it += 1
                        S.act(g, pg[:, :], AF.Sigmoid, bias=PVc(l, 'b_gate', n * 8 + fo))
                        S.tt('dve', gp[:, fo, ts_], g, pp[:, :], ALU.mult)
            for t in range(NT):
                for h2 in range(2):
                    p = PS()
                    for f in range(KD):
                        S.matmul(p[:, :], gp[:, f, t * 128:(t + 1) * 128], wout[:, f, h2 * 512:(h2 + 1) * 512],
                                 start=(f == 0), stop=(f == KD - 1))
                    S.tt('dve', xres[:, t, h2 * 512:(h2 + 1) * 512], xres[:, t, h2 * 512:(h2 + 1) * 512],
                         p[:, :], ALU.add)
            A.release(m)

        def rwkv(l, yT):
            m = A.mark()
            G = 256
            NG = SEQ // G
            wbuf = [A.alloc([KD, 128], BF16) for _ in range(3)]
            wup = A.alloc([RW], BF16)
            gup = A.alloc([RW], BF16)
            carry = A.alloc([16], F32)
            pf_ = A.alloc([G + 1], F32)
            tmp = [A.alloc([G], F32) for _ in range(12)]
            wdad = A.alloc([G], BF16)
            sg = A.alloc([G], BF16)
            sqb = A.alloc([G], BF16)
            qtT, ktT, btT, kapT, vT, KendT, nBendT, rkrT = [A.alloc([4, G], BF16) for _ in range(8)]
            pc = A.alloc([4, 4], F32)
            Dg = A.alloc([4, 4, 64], BF16)
            tm2 = [[A.alloc([512], BF16) for _ in range(4)] for _ in range(2)]
            Mb = [A.alloc([512], BF16) for _ in range(3)]
            MTb = [A.alloc([512], BF16) for _ in range(3)]
            TTb = [A.alloc([512], BF16) for _ in range(3)]
            A2 = [[A.alloc([512], BF16) for _ in range(5)] for _ in range(2)]
            kaph, qhT, GT = [A.alloc([512], BF16) for _ in range(3)]
            IMb = [A.alloc([512], BF16) for _ in range(2)]
            mrot = [0]
            cpar = [0]
            Sbf = A.alloc([512], BF16)
            ep = [A.alloc([512], F32) for _ in range(3)]
            st8 = [A.alloc([8], F32) for _ in range(4)]
            yo = A.alloc([512], BF16)
            S.wdma(wup[0:64, :], w_up_d[l])
            S.wdma(wup[64:128, :], a_up_d[l])
            S.wdma(gup, g_up_d[l])
            S.memset('dve', carry, 0.0)
            S.memset('dve', Sbf[0:64, :], 0.0)
            o_mu = PV['mu'][0]
            S.ts('dve', omu[:, 0:14], pvec[:, l, o_mu:o_mu + 14], -1.0, 1.0, ALU.mult, ALU.add)
            S.ts('dve', omu[:, 16:20], PVc(l, 'k_a', 0, 4), -1.0, 1.0, ALU.mult, ALU.add)
            mask_s3 = bc(C('mask_s', slice(0, 64)), 1, [64, 8, 64])
            maskT_s3 = bc(C('maskT_s', slice(0, 64)), 1, [64, 8, 64])
            maskT_i3 = bc(cst[0:64, CS['cmask'][0]:CS['cmask'][0] + 64], 1, [64, 8, 64])
            ident3 = bc(cst[0:64, CS['ident'][0]:CS['ident'][0] + 64], 1, [64, 8, 64])
            v3 = lambda ap: ap.rearrange("p (h c) -> p h c", c=64)
            wblk = [(0, 512), (512, 512), (1024, 512), (1536, 256)]
            wcur = [None, None]

            forder = [12, 13] + [blk * 4 + j for j in range(4) for blk in range(3)]
            uses = [fc for _ in range(NG) for fc in forder]
            wptr = [0, 0]

            def issue_w():
                i = wptr[0]
                if i < len(uses):
                    fc = uses[i]
                    S.wdma(wbuf[i % 3], w_in_d[l, :, fc * 128:(fc + 1) * 128].rearrange("(a p) c -> p a c", p=128))
                    wptr[0] += 1

            def next_slot():
                i = wptr[1]
                wptr[1] += 1
                issue_w()
                return i % 3

            issue_w()
            issue_w()

            def proj_mix(fc, g0, out_pm, slot, coff):
                p = PS()
                for kc in range(KD):
                    S.matmul(p[:, 0:G], wbuf[slot][:, kc, coff:coff + 128], hT[:, kc, g0:g0 + G],
                             start=(kc == 0), stop=(kc == KD - 1))
                S.copy('act', pf_[:, 0:1], carry[:, fc:fc + 1])
                S.copy('act', pf_[:, 1:G + 1], p[:, 0:G])
                S.copy('act', carry[:, fc:fc + 1], pf_[:, G:G + 1])
                t0 = tmp[11]
                S.ts('dve', t0, pf_[:, 0:G], PVc(l, 'mu', fc), None, ALU.mult)
                S.stt('dve', out_pm, pf_[:, 1:G + 1], omu[:, fc:fc + 1], t0, ALU.mult, ALU.add)

            for gi in range(NG):
                g0 = gi * G
                pm = tmp[0]
                proj_mix(12, g0, pm, next_slot(), 0)
                S.act(tmp[1][0:64, :], pm[0:64, :], AF.Exp, scale=2.0)
                S.ts('dve', tmp[1][0:64, :], tmp[1][0:64, :], 1.0, None, ALU.add)
                S.recip(tmp[1][0:64, :], tmp[1][0:64, :])
                S.ts('dve', wdad[0:64, :], tmp[1][0:64, :], -2.0, 1.0, ALU.mult, ALU.add)
                S.copy('dve', wdad[64:128, :], pm[64:128, :])
                proj_mix(13, g0, pm, next_slot(), 0)
                sigmoid_(sg, pm, tmp_=tmp[1])
                for j in range(4):
                    rm, km, vm = tmp[0], tmp[1], tmp[2]
                    for (blk, dst) in ((0, rm), (1, km), (2, vm)):
                        proj_mix(blk * 4 + j, g0, dst, next_slot(), 0)
                    jc = slice(j * 128, (j + 1) * 128)
                    pz = PS()
                    S.matmul(pz[:, 0:G], wup[0:64, jc], wdad[0:64, :])
                    lw = tmp[3]
                    sigmoid_(lw, pz[:, 0:G], nbias=NPVc(l, 'w0', j))
                    S.ts('dve', lw, lw, -0.6065306597126334, None, ALU.mult)
                    pa = PS()
                    S.matmul(pa[:, 0:G], wup[64:128, jc], wdad[64:128, :])
                    am = tmp[4]
                    sigmoid_(am, pa[:, 0:G], nbias=NPVc(l, 'a0', j))
                    kkr = tmp[5]
                    S.ts('dve', kkr, km, PVc(l, 'k_k', j), None, ALU.mult)
                    S.act(sqb, kkr, AF.Square)
                    pss = PS()
                    S.matmul(pss[:, 0:G], bones_b, sqb)
                    rn = tmp[6]
                    S.ts('dve', rn, pss[:, 0:G], 1e-24, None, ALU.max)
                    rsqrt_(rn, rn)
                    kk = tmp[5]
                    S.tt('dve', kk, kkr, rn, ALU.mult)
                    t1 = tmp[6]
                    S.ts('dve', t1, am, PVc(l, 'k_a', j), omu[:, 16 + j:17 + j], ALU.mult, ALU.add)
                    kp = tmp[7]
                    S.tt('dve', kp, km, t1, ALU.mult)
                    bb = tmp[8]
                    S.tt('dve', bb, kk, am, ALU.mult)
                    S.stt('dve', rkrT[:, j, :], rm, PVc(l, 'r_k', j), kp, ALU.mult, ALU.mult)
                    S.copy('act', vT[:, j, :], vm)
                    L = tmp[9]
                    S.scan(L, C('chunkmask'), lw, 0.0, ALU.mult, ALU.add)
                    L3 = L.rearrange("p (c t) -> p c t", t=64)
                    LC = L3[:, :, 63:64]
                    S.act(pc[:, j, :], L3[:, :, 63], AF.Exp)
                    eP = tmp[10]
                    S.act(eP, L, AF.Exp)
                    S.tt('dve', qtT[:, j, :], rm, eP, ALU.mult)
                    dK = tmp[10]
                    S.tt('dve', dK, L, lw, ALU.subtract)
                    S.act(dK, dK, AF.Exp)
                    S.tt('dve', kapT[:, j, :], kk, dK, ALU.mult)
                    eN = tmp[3]
                    S.act(eN, L, AF.Exp, scale=-1.0)
                    S.tt('dve', ktT[:, j, :], kp, eN, ALU.mult)
                    S.tt('dve', btT[:, j, :], bb, eN, ALU.mult)
                    eE = tmp[4]
                    S.tt('dve', eE.rearrange("p (c t) -> p c t", t=64), LC.to_broadcast([128, 4, 64]), L3,
                         ALU.subtract)
                    S.act(eE, eE, AF.Exp)
                    S.tt('dve', KendT[:, j, :], kp, eE, ALU.mult)
                    S.stt('dve', nBendT[:, j, :], bb, -1.0, eE, ALU.mult, ALU.mult)
                    S.tt('dve', Dg[:, j, :, :], bc(C('identblk'), 1, [128, 4, 64]),
                         bc(pc[:, j, :], 2, [128, 4, 64]), ALU.mult)
                for c in range(G // 64):
                    cs = slice(c * 64, (c + 1) * 64)
                    tok0 = g0 + c * 64
                    cpar[0] ^= 1
                    tm = tm2[cpar[0]]
                    AkkT, ArkT, nArbT, AV, UV = A2[cpar[0]]
                    for qi, X in enumerate((kapT, vT, KendT, nBendT)):
                        p = PS()
                        for j in range(4):
                            S.matmul(p[0:64, j * 128:(j + 1) * 128], X[:, j, cs], ident_b)
                        evac(tm[qi][0:64, :], p[0:64, :])
                    kap_tm, V_tm, Kend_tm, nBend_tm = [t_[0:64, :] for t_ in tm]
                    hb = lambda h: ((h % 2) * 64, h // 2)
                    HO = (0, 2, 4, 6, 1, 3, 5, 7)
                    pN = PS()
                    pNT = PS()
                    for h in HO:
                        b, j = hb(h)
                        S.matmul(pN[0:64, h * 64:(h + 1) * 64], kapT[b:b + 64, j, cs], btT[b:b + 64, j, cs])
                    for h in HO:
                        b, j = hb(h)
                        S.matmul(pNT[0:64, h * 64:(h + 1) * 64], btT[b:b + 64, j, cs], kapT[b:b + 64, j, cs])
                    mrot[0] = (mrot[0] + 1) % 3
                    cur = mrot[0]
                    M_, MT_, TT_ = Mb[cur][0:64, :], MTb[cur][0:64, :], TTb[cur][0:64, :]
                    S.stt('dve', v3(M_), v3(pN[0:64, :]), -1.0, mask_s3, ALU.mult, ALU.mult)
                    S.stt('dve', v3(MT_), v3(pNT[0:64, :]), -1.0, maskT_s3, ALU.mult, ALU.mult)
                    S.tt('dve', v3(TT_), v3(MT_), ident3, ALU.add)
                    for lev in range(5):
                        mrot[0] = (mrot[0] + 1) % 3
                        nx = mrot[0]
                        Mn, MTn, TTn = Mb[nx][0:64, :], MTb[nx][0:64, :], TTb[nx][0:64, :]
                        pM = PS()
                        for h in range(8):
                            hs = slice(h * 64, (h + 1) * 64)
                            S.matmul(pM[0:64, hs], MT_[:, hs], M_[:, hs])
                        if lev < 4:
                            pMT = PS()
                            for h in range(8):
                                hs = slice(h * 64, (h + 1) * 64)
                                S.matmul(pMT[0:64, hs], M_[:, hs], MT_[:, hs])
                        S.tt('dve', v3(IMb[lev % 2][0:64, :]), v3(pM[0:64, :]), ident3, ALU.add)
                        if lev < 4:
                            S.copy('act', Mn, pM[0:64, :])
                            S.copy('act', MTn, pMT[0:64, :])
                        pT = PS()
                        for h in range(8):
                            hs = slice(h * 64, (h + 1) * 64)
                            S.matmul(pT[0:64, hs], IMb[lev % 2][0:64, hs], TT_[:, hs])
                        evac(TTn, pT[0:64, :])
                        M_, MT_, TT_ = Mn, MTn, TTn
                    TT = TT_
                    pA1, pA2, pA3 = PS(), PS(), PS()
                    for h in HO:
                        b, j = hb(h)
                        hs = slice(h * 64, (h + 1) * 64)
                        S.matmul(pA1[0:64, hs], ktT[b:b + 64, j, cs], kapT[b:b + 64, j, cs])
                        S.matmul(pA2[0:64, hs], ktT[b:b + 64, j, cs], qtT[b:b + 64, j, cs])
                        S.matmul(pA3[0:64, hs], btT[b:b + 64, j, cs], qtT[b:b + 64, j, cs])
                    S.tt('dve', v3(AkkT[0:64, :]), v3(pA1[0:64, :]), maskT_s3, ALU.mult)
                    S.tt('dve', v3(ArkT[0:64, :]), v3(pA2[0:64, :]), maskT_i3, ALU.mult)
                    S.stt('dve', v3(nArbT[0:64, :]), v3(pA3[0:64, :]), -1.0, maskT_i3, ALU.mult, ALU.mult)
                    pAV = PS()
                    for h in range(8):
                        hs = slice(h * 64, (h + 1) * 64)
                        S.matmul(pAV[0:64, hs], AkkT[0:64, hs], V_tm[:, hs])
                    S.copy('act', AV[0:64, :], pAV[0:64, :])
                    pK = PS()
                    for h in range(8):
                        hs = slice(h * 64, (h + 1) * 64)
                        S.matmul(pK[0:64, hs], TT[:, hs], kap_tm[:, hs])
                    S.copy('dve', kaph[0:64, :], pK[0:64, :])
                    pUV = PS()
                    for h in range(8):
                        hs = slice(h * 64, (h + 1) * 64)
                        S.matmul(pUV[0:64, hs], TT[:, hs], AV[0:64, hs])
                    S.copy('act', UV[0:64, :], pUV[0:64, :])
                    pQ = PS()
                    for h in range(8):
                        b, j = hb(h)
                        hs = slice(h * 64, (h + 1) * 64)
                        S.matmul(pQ[0:64, hs], kaph[0:64, hs], nArbT[0:64, hs], start=True, stop=False)
                        S.matmul(pQ[0:64, hs], ident_b[b:b + 64, b:b + 64], qtT[b:b + 64, j, cs],
                                 start=False, stop=True)
                    S.copy('act', qhT[0:64, :], pQ[0:64, :])
                    pG = PS()
                    for h in range(8):
                        b, j = hb(h)
                        hs = slice(h * 64, (h + 1) * 64)
                        S.matmul(pG[0:64, hs], kaph[0:64, hs], nBend_tm[:, hs], start=True, stop=False)
                        S.matmul(pG[0:64, hs], Dg[b:b + 64, j, c, :], ident_b[b:b + 64, b:b + 64],
                                 start=False, stop=True)
                    S.copy('act', GT[0:64, :], pG[0:64, :])
                    pY = PS()
                    for h in range(8):
                        hs = slice(h * 64, (h + 1) * 64)
                        S.matmul(pY[0:64, hs], ArkT[0:64, hs], V_tm[:, hs], start=True, stop=False)
                        S.matmul(pY[0:64, hs], nArbT[0:64, hs], UV[0:64, hs], start=False, stop=False)
                        S.matmul(pY[0:64, hs], qhT[0:64, hs], Sbf[0:64, hs], start=False, stop=True)
                    pS_ = PS()
                    for h in range(8):
                        hs = slice(h * 64, (h + 1) * 64)
                        S.matmul(pS_[0:64, hs], Kend_tm[:, hs], V_tm[:, hs], start=True, stop=False)
                        S.matmul(pS_[0:64, hs], nBend_tm[:, hs], UV[0:64, hs], start=False, stop=False)
                        S.matmul(pS_[0:64, hs], GT[0:64, hs], Sbf[0:64, hs], start=False, stop=True)
                    ysum, var, rstd_, bsum = [s_[0:64, :] for s_ in st8]
                    yc, sq_, bon = [e_[0:64, :] for e_ in ep]
                    S.reduce('dve', ysum, v3(pY[0:64, :]), ALU.add)
                    S.ts('dve', ysum, ysum, 1.0 / 64, None, ALU.mult)
                    S.tt('dve', v3(yc), v3(pY[0:64, :]), bc(ysum, 2, [64, 8, 64]), ALU.subtract)
                    S.copy('act', Sbf[0:64, :], pS_[0:64, :])
                    S.tt('dve', sq_, yc, yc, ALU.mult)
                    S.reduce('dve', var, v3(sq_), ALU.add)
                    S.ts('dve', var, var, 1.0 / 64, 64e-5, ALU.mult, ALU.add)
                    rsqrt_(rstd_, var)
                    S.tt('dve', v3(yc), v3(yc), bc(rstd_, 2, [64, 8, 64]), ALU.mult)
                    S.tt('dve', yc, yc, PRr('ln_g', slice(0, 64)), ALU.mult)
                    S.tt('dve', yc, yc, PRr('ln_b', slice(0, 64)), ALU.add)
                    pB = PS()
                    for j in range(4):
                        S.matmul(pB[0:64, j * 2:(j + 1) * 2], rkrT[:, j, cs], hsel_b)
                    S.copy('act', bsum, pB[0:64, 0:8])
                    S.tt('dve', v3(bon), v3(V_tm), bc(bsum, 2, [64, 8, 64]), ALU.mult)
                    S.tt('dve', yc, yc, bon, ALU.add)
                    pGt = PS()
                    S.matmul(pGt[0:64, :], sg[:, cs], gup)
                    S.tt('dve', yo[0:64, :], yc, pGt[0:64, :], ALU.mult)
                    pYT = PS()
                    for j in range(4):
                        S.matmul(pYT[:, j * 64:(j + 1) * 64], yo[0:64, j * 128:(j + 1) * 128], ident_b[0:64, 0:64])
                    evac(yT[:, :, tok0:tok0 + 64], pYT[:, 0:256].rearrange("p (j t) -> p j t", t=64))
            A.release(m)

        def lru(l, yT):
            m = A.mark()
            wl = A.alloc([KD, D], BF16)
            bd = A.alloc([2, 128], BF16)
            c1 = A.alloc([4], F32)
            xbs = A.alloc([3 + 512], F32)
            xc = A.alloc([512], F32)
            xcb = A.alloc([512], BF16)
            rr = A.alloc([512], F32)
            ii = A.alloc([512], F32)
            aa = A.alloc([512], F32)
            uu = A.alloc([512], F32)
            hh = [A.alloc([512], F32) for _ in range(2)]
            ge = A.alloc([512], F32)
            for h2 in range(2):
                S.wdma(wl[:, :, h2 * 512:(h2 + 1) * 512],
                       w_in_d[l, :, LRU_OFF + h2 * 512:LRU_OFF + (h2 + 1) * 512].rearrange("(a p) c -> p a c", p=128))
            S.act(c1, PVc(l, 'lam', 0, 4), AF.Exp, scale=-1.0)
            S.act(c1, c1, AF.Ln, bias=1.0)
            S.ts('dve', c1, c1, -8.0, None, ALU.mult)
            for fc in range(4):
                S.memset('dve', bd[:, :, :], 0.0)
                for q, wd_ in enumerate((wa_d, wx_d)):
                    S.wdma(bd[0:64, q, 0:64], wd_[l, 2 * fc])
                    S.wdma(bd[64:128, q, 64:128], wd_[l, 2 * fc + 1])
                S.memset('dve', xbs[:, 0:3], 0.0)
                for tc in range(4):
                    ts_ = slice(tc * 512, (tc + 1) * 512)
                    px = PS()
                    for kc in range(KD):
                        S.matmul(px[:, :], wl[:, kc, fc * 128:(fc + 1) * 128], hT[:, kc, ts_],
                                 start=(kc == 0), stop=(kc == KD - 1))
                    pgb = PS()
                    for kc in range(KD):
                        S.matmul(pgb[:, :], wl[:, kc, 512 + fc * 128:512 + (fc + 1) * 128], hT[:, kc, ts_],
                                 start=(kc == 0), stop=(kc == KD - 1))
                    S.copy('act', xbs[:, 3:515], px[:, :])
                    S.copy('act', ge, pgb[:, :])
                    S.tt('dve', rr, ge, ge, ALU.mult)
                    S.ts('dve', rr, rr, 0.044715, 1.0, ALU.mult, ALU.add)
                    S.tt('dve', rr, rr, ge, ALU.mult)
                    sigmoid_(rr, rr, scale=1.5957691216057308)
                    S.tt('dve', ge, ge, rr, ALU.mult)
                    cw = lambda j: PVc(l, 'conv_w', j * 4 + fc)
                    S.ts('dve', xc, xbs[:, 0:512], cw(0), PVc(l, 'conv_b', fc), ALU.mult, ALU.add)
                    for j in range(1, 4):
                        S.stt('dve', xc, xbs[:, j:j + 512], cw(j), xc, ALU.mult, ALU.add)
                    S.copy('act', xbs[:, 0:3], xbs[:, 512:515])
                    S.copy('act', xcb, xc)
                    pr_ = PS()
                    S.matmul(pr_[:, :], bd[:, 0, :], xcb)
                    pi2 = PS()
                    S.matmul(pi2[:, :], bd[:, 1, :], xcb)
                    sigmoid_(rr, pr_[:, :], nbias=NPVc(l, 'ba', fc))
                    sigmoid_(ii, pi2[:, :], nbias=NPVc(l, 'bx', fc))
                    S.act(aa, rr, AF.Exp, scale=c1[:, fc:fc + 1])
                    S.tt('dve', uu, aa, aa, ALU.mult)
                    S.ts('dve', uu, uu, -1.0, 1.0, ALU.mult, ALU.add)
                    S.act(uu, uu, AF.Ln)
                    S.act(uu, uu, AF.Exp, scale=0.5)
                    S.tt('dve', ii, ii, xc, ALU.mult)
                    S.tt('dve', uu, uu, ii, ALU.mult)
                    hcur = hh[tc % 2]
                    init = 0.0 if tc == 0 else hh[(tc - 1) % 2][:, 511:512]
                    S.scan(hcur, aa, uu, init, ALU.mult, ALU.add)
                    S.tt('dve', yT[:, fc, ts_], hcur, ge, ALU.mult)
            A.release(m)

        def mla(l, yT):
            m = A.mark()
            cqnT = A.alloc([2, SEQ], BF16)
            ckvnT = A.alloc([SEQ], BF16)
            krs = A.alloc([NT, 32], F32)
            sskr = A.alloc([NT], F32)
            junk = A.alloc([256], F32)
            st4 = A.alloc([8], F32)
            m1 = A.mark()
            wm = A.alloc([KD, 416], BF16)
            cqn = A.alloc([384], BF16)
            S.wdma(wm, w_in_d[l, :, MLA_OFF:MLA_OFF + 416].rearrange("(a p) c -> p a c", p=128))
            for t in range(NT):
                p = PS()
                for kc in range(KD):
                    S.matmul(p[:, 0:416], hT[:, kc, t * 128:(t + 1) * 128], wm[:, kc, :],
                             start=(kc == 0), stop=(kc == KD - 1))
                S.act(junk[:, 0:256], p[:, 0:256], AF.Square, accum_out=st4[:, 0:1])
                S.act(junk[:, 0:128], p[:, 256:384], AF.Square, accum_out=st4[:, 1:2])
                S.copy('dve', krs[:, t, :], p[:, 384:416])
                S.act(junk[:, 0:32], p[:, 384:416], AF.Square, accum_out=sskr[:, t:t + 1])
                S.ts('dve', st4[:, 0:1], st4[:, 0:1], 1.0 / 256, EPS, ALU.mult, ALU.add)
                S.ts('dve', st4[:, 1:2], st4[:, 1:2], 1.0 / 128, EPS, ALU.mult, ALU.add)
                S.act(st4[:, 2:4], st4[:, 0:2], AF.Ln)
                S.act(st4[:, 4:6], st4[:, 2:4], AF.Exp, scale=-0.5)
                S.ts('dve', cqn[:, 0:256], p[:, 0:256], st4[:, 4:5], None, ALU.mult)
                S.ts('dve', cqn[:, 256:384], p[:, 256:384], st4[:, 5:6], None, ALU.mult)
                p2 = PS()
                for c in range(3):
                    S.matmul(p2[:, c * 128:(c + 1) * 128], cqn[:, c * 128:(c + 1) * 128], ident_b)
                for c in range(2):
                    evac(cqnT[:, c, t * 128:(t + 1) * 128], p2[:, c * 128:(c + 1) * 128], PVc(l, 'q_norm', c))
                evac(ckvnT[:, t * 128:(t + 1) * 128], p2[:, 256:384], PVc(l, 'kv_norm', 0))
            A.release(m1)
            bufs = [(A.alloc([2, 192], BF16), A.alloc([256], BF16), A.alloc([2, SEQ], BF16),
                     A.alloc([2, SEQ], BF16), A.alloc([NT, 2, 128], BF16)) for _ in range(2)]
            st16 = A.alloc([16], F32)
            qf = A.alloc([2, 96], F32)
            kfm = A.alloc([2, 96], F32)
            rt = [A.alloc([2, 16], F32) for _ in range(4)]
            qkb = A.alloc([4, 96], BF16)
            PT = [A.alloc([512], BF16) for _ in range(3)]
            rec = A.alloc([512], F32)
            qg3 = bc(PRr('q_gain'), 1, [128, 2, 96])
            kg3 = bc(PRr('k_gain'), 1, [128, 2, 96])
            scale = 96 ** -0.5
            for b_ in bufs:
                S.memset('dve', b_[4][:, :, :, :], 0.0)

            def tok_gen(pr):
                wq, wkv, QT, KT, Vp = bufs[pr % 2]
                S.wdma(wq, w_uq_d[l, :, pr * 192:(pr + 1) * 192].rearrange("(a p) c -> p a c", p=128))
                S.wdma(wkv, w_ukv_d[l, :, pr * 256:(pr + 1) * 256])
                for t in range(NT):
                    tsl = slice(t * 128, (t + 1) * 128)
                    pq = PS()
                    for c in range(2):
                        S.matmul(pq[:, 0:192], cqnT[:, c, tsl], wq[:, c, :], start=(c == 0), stop=(c == 1))
                    pk = PS()
                    S.matmul(pk[:, 0:256], ckvnT[:, tsl], wkv)
                    for hh_ in range(2):
                        S.act(junk[:, 0:96], pq[:, hh_ * 96:(hh_ + 1) * 96], AF.Square,
                              accum_out=st16[:, hh_:hh_ + 1])
                        S.act(junk[:, 0:64], pk[:, hh_ * 128:hh_ * 128 + 64], AF.Square,
                              accum_out=st16[:, 2 + hh_:3 + hh_])
                    S.ts('dve', st16[:, 2:4], st16[:, 2:4], sskr[:, t:t + 1], None, ALU.add)
                    S.ts('dve', st16[:, 0:4], st16[:, 0:4], 1.0 / 96, EPS, ALU.mult, ALU.add)
                    S.act(st16[:, 4:8], st16[:, 0:4], AF.Ln)
                    S.act(st16[:, 8:12], st16[:, 4:8], AF.Exp, scale=-0.5)
                    S.tt('dve', qf, pq[:, 0:192].rearrange("p (h c) -> p h c", c=96),
                         bc(st16[:, 8:10], 2, [128, 2, 96]), ALU.mult)
                    S.tt('dve', qf, qf, qg3, ALU.mult)
                    S.tt('dve', kfm[:, :, 0:64], pk[:, 0:256].rearrange("p (h c) -> p h c", c=128)[:, :, 0:64],
                         bc(st16[:, 10:12], 2, [128, 2, 64]), ALU.mult)
                    S.tt('dve', kfm[:, :, 64:96], bc(krs[:, t, :], 1, [128, 2, 32]),
                         bc(st16[:, 10:12], 2, [128, 2, 32]), ALU.mult)
                    S.tt('dve', kfm, kfm, kg3, ALU.mult)
                    S.copy('act', Vp[:, t, 0, 0:64], pk[:, 64:128])
                    S.copy('act', Vp[:, t, 1, 64:128], pk[:, 192:256])
                    cosb = bc(cos_t[:, t, :], 1, [128, 2, 16])
                    sinb = bc(sin_t[:, t, :], 1, [128, 2, 16])
                    for (src, o4) in ((qf, 0), (kfm, 2)):
                        x1 = src[:, :, 64:80]
                        x2 = src[:, :, 80:96]
                        S.copy('act', qkb[:, o4:o4 + 2, 0:64], src[:, :, 0:64])
                        S.tt('dve', rt[0], x1, cosb, ALU.mult)
                        S.tt('dve', rt[1], x2, sinb, ALU.mult)
                        S.tt('dve', rt[2], x2, cosb, ALU.mult)
                        S.tt('dve', rt[3], x1, sinb, ALU.mult)
                        S.tt('dve', qkb[:, o4:o4 + 2, 64:80], rt[0], rt[1], ALU.subtract)
                        S.tt('dve', qkb[:, o4:o4 + 2, 80:96], rt[2], rt[3], ALU.add)
                    pt_ = PS()
                    for i4 in range(4):
                        S.matmul(pt_[0:96, i4 * 128:(i4 + 1) * 128], qkb[:, i4, :], ident_b)
                    evac(QT[0:96, :, tsl], pt_[0:96, 0:256].rearrange("p (h t) -> p h t", t=128))
                    evac(KT[0:96, :, tsl], pt_[0:96, 256:512].rearrange("p (h t) -> p h t", t=128))
                    yield

            ibc = [0]

            def attn_gen(pr):
                wq, wkv, QT, KT, Vp = bufs[pr % 2]
                for qc in range(4):
                    po = PSA(0)
                    pd = PSA(1)
                    nkb = 4 * qc + 4
                    first = True
                    for hh_ in range(2):
                        for kb in range(nkb):
                            q0 = max(qc * 512, kb * 128)
                            n = (qc + 1) * 512 - q0
                            off = q0 - qc * 512
                            ps_ = PS()
                            S.matmul(ps_[:, 0:n], KT[0:96, hh_, kb * 128:(kb + 1) * 128], QT[0:96, hh_, q0:q0 + n])
                            pt2 = PT[ibc[0] % 3]
                            ibc[0] += 1
                            S.act(pt2[:, 0:n], ps_[:, 0:n], AF.Exp, scale=scale)
                            if kb * 128 >= qc * 512:
                                S.tt('dve', pt2[:, 0:128], pt2[:, 0:128], cmask_b, ALU.mult)
                            last = (hh_ == 1 and kb == nkb - 1)
                            S.matmul(po[:, off:off + n], Vp[:, kb, hh_, :], pt2[:, 0:n], start=first, stop=last)
                            S.matmul(pd[:, off:off + n], hpad_b[:, hh_, :], pt2[:, 0:n], start=first, stop=last)
                            first = False
                            yield
                    S.act(rec, pd[:, :], AF.Ln)
                    S.act(rec, rec, AF.Exp, scale=-1.0)
                    S.tt('dve', yT[:, pr, qc * 512:(qc + 1) * 512], po[:, :], rec, ALU.mult)
                    yield

            for _ in tok_gen(0):
                pass
            for pr in range(4):
                ga = attn_gen(pr)
                gt = tok_gen(pr + 1) if pr < 3 else None
                done_a = False
                done_t = gt is None
                while not (done_a and done_t):
                    for _ in range(5):
                        if not done_a:
                            try:
                                next(ga)
                            except StopIteration:
                                done_a = True
                    if not done_t:
                        try:
                            next(gt)
                        except StopIteration:
                            done_t = True
            A.release(m)

        def xattn(l):
            m = A.mark()
            rms_to_hT(l, 'g_xa')
            memf = A.alloc([2, D], F32)
            memn = A.alloc([2, D], BF16)
            memT = A.alloc([KD, N_MEM], BF16)
            ss = A.alloc([4], F32)
            junk = A.alloc([D], BF16)
            wkv = A.alloc([KD, D], BF16)
            wq = A.alloc([KD, 512], BF16)
            wo = A.alloc([4, D], BF16)
            KTn = A.alloc([4, N_MEM], BF16)
            V1 = A.alloc([4, 2, 128], BF16)
            sq = A.alloc([512], BF16)
            rs = A.alloc([512], F32)
            QTn = A.alloc([512], BF16)
            PT = [A.alloc([512], BF16) for _ in range(2)]
            oT = A.alloc([4, 512], BF16)
            for mt in range(2):
                S.dma('sp', memf[:, mt, :], mem_d[mt * 128:(mt + 1) * 128, :], 'mem')
            for h2 in range(2):
                S.wdma(wkv[:, :, h2 * 512:(h2 + 1) * 512],
                       xa_wkv_d[l, :, h2 * 512:(h2 + 1) * 512].rearrange("(a p) c -> p a c", p=128))
            S.wdma(wq, xa_wq_d[l].rearrange("(a p) c -> p a c", p=128))
            for h2 in range(2):
                S.wdma(wo[:, :, h2 * 512:(h2 + 1) * 512],
                       xa_wo_d[l, :, h2 * 512:(h2 + 1) * 512].rearrange("(a p) c -> p a c", p=128))
            for mt in range(2):
                S.act(junk, memf[:, mt, :], AF.Square, accum_out=ss[:, mt:mt + 1])
            S.ts('dve', ss[:, 0:2], ss[:, 0:2], 1.0 / D, EPS, ALU.mult, ALU.add)
            S.act(ss[:, 0:2], ss[:, 0:2], AF.Ln)
            S.act(ss[:, 2:4], ss[:, 0:2], AF.Exp, scale=-0.5)
            for mt in range(2):
                S.ts('dve', memn[:, mt, :], memf[:, mt, :], ss[:, 2 + mt:3 + mt], None, ALU.mult)
            for kc in range(KD):
                p = PS()
                for mt in range(2):
                    S.matmul(p[:, mt * 128:(mt + 1) * 128], memn[:, mt, kc * 128:(kc + 1) * 128], ident_b)
                evac(memT[:, kc, :], p[:, 0:256], PVc(l, 'g_mem', kc))

            def headnorm(p, n, gcol, out):
                S.act(sq[:, 0:n], p[:, 0:n], AF.Square)
                p2 = PS()
                S.matmul(p2[:, 0:n], ones_b, sq[:, 0:n])
                S.ts('dve', rs[:, 0:n], p2[:, 0:n], 1.0 / 128, EPS, ALU.mult, ALU.add)
                rsqrt_(rs[:, 0:n], rs[:, 0:n])
                S.stt('dve', out, p[:, 0:n], gcol, rs[:, 0:n], ALU.mult, ALU.mult)

            S.memset('dve', V1[:, :, :, :], 0.0)
            for h in range(4):
                p = PS()
                for kc in range(KD):
                    S.matmul(p[:, 0:N_MEM], wkv[:, kc, h * 256:h * 256 + 128], memT[:, kc, :],
                             start=(kc == 0), stop=(kc == KD - 1))
                headnorm(p, N_MEM, PVc(l, 'xa_kg', 0), KTn[:, h, :])
                for mt in range(2):
                    pv = PS()
                    for kc in range(KD):
                        S.matmul(pv[:, 0:128], memT[:, kc, mt * 128:(mt + 1) * 128],
                                 wkv[:, kc, h * 256 + 128:h * 256 + 256], start=(kc == 0), stop=(kc == KD - 1))
                    evac(V1[:, h, mt, :], pv[:, 0:128])
            scale = 128 ** -0.5
            ib = 0
            for tc in range(4):
                ts_ = slice(tc * 512, (tc + 1) * 512)
                for h in range(4):
                    p = PS()
                    for kc in range(KD):
                        S.matmul(p[:, :], wq[:, kc, h * 128:(h + 1) * 128], hT[:, kc, ts_],
                                 start=(kc == 0), stop=(kc == KD - 1))
                    headnorm(p, 512, PVc(l, 'xa_qg', 0), QTn)
                    po = PSA(0)
                    pd = PSA(1)
                    for mt in range(2):
                        ps_ = PS()
                        S.matmul(ps_[:, :], KTn[:, h, mt * 128:(mt + 1) * 128], QTn)
                        pt2 = PT[ib % 2]
                        ib += 1
                        S.act(pt2, ps_[:, :], AF.Exp, scale=scale)
                        S.matmul(po[:, :], V1[:, h, mt, :], pt2, start=(mt == 0), stop=(mt == 1))
                        S.matmul(pd[:, :], ones_b, pt2, start=(mt == 0), stop=(mt == 1))
                    S.act(rs, pd[:, :], AF.Ln)
                    S.act(rs, rs, AF.Exp, scale=-1.0)
                    S.tt('dve', oT[:, h, :], po[:, :], rs, ALU.mult)
                for j in range(4):
                    t = tc * 4 + j
                    for h2 in range(2):
                        p = PS()
                        for c in range(4):
                            S.matmul(p[:, :], oT[:, c, j * 128:(j + 1) * 128], wo[:, c, h2 * 512:(h2 + 1) * 512],
                                     start=(c == 0), stop=(c == 3))
                        S.tt('dve', xres[:, t, h2 * 512:(h2 + 1) * 512], xres[:, t, h2 * 512:(h2 + 1) * 512],
                             p[:, :], ALU.add)
            A.release(m)

        def ffn(l):
            m = A.mark()
            rms_to_hT(l, 'g_ffn')
            NF = 11
            uT = A.alloc([NF, SEQ], BF16)
            w2g = A.alloc([NF, D], BF16)
            w1b = [A.alloc([KD, 256], BF16) for _ in range(2)]
            w3b = [A.alloc([KD, 256], BF16) for _ in range(2)]
            sl = [A.alloc([512], F32) for _ in range(2)]
            it = 0
            blocks = [(g, b) for g in range(2) for b in range(6)]

            def load_blk(i):
                if i >= len(blocks):
                    return
                g_, b_ = blocks[i]
                nb_ = 2 if b_ < 5 else 1
                c0_ = g_ * NF * 128 + b_ * 256
                S.wdma(w1b[i % 2][:, :, 0:nb_ * 128], w1_d[l, :, c0_:c0_ + nb_ * 128].rearrange("(a p) c -> p a c", p=128))
                S.wdma(w3b[i % 2][:, :, 0:nb_ * 128], w3_d[l, :, c0_:c0_ + nb_ * 128].rearrange("(a p) c -> p a c", p=128))

            load_blk(0)
            for g in range(2):
                f0 = g * NF * 128
                for h2 in range(2):
                    S.wdma(w2g[:, :, h2 * 512:(h2 + 1) * 512],
                           w2_d[l, f0:f0 + NF * 128, h2 * 512:(h2 + 1) * 512].rearrange("(a p) c -> p a c", p=128))
                for b in range(6):
                    nb = 2 if b < 5 else 1
                    s_ = (g * 6 + b) % 2
                    load_blk(g * 6 + b + 1)
                    for f in range(nb):
                        fo = b * 2 + f
                        for tc in range(4):
                            ts_ = slice(tc * 512, (tc + 1) * 512)
                            p1 = PS()
                            for kc in range(KD):
                                S.matmul(p1[:, :], w1b[s_][:, kc, f * 128:(f + 1) * 128], hT[:, kc, ts_],
                                         start=(kc == 0), stop=(kc == KD - 1))
                            p3 = PS()
                            for kc in range(KD):
                                S.matmul(p3[:, :], w3b[s_][:, kc, f * 128:(f + 1) * 128], hT[:, kc, ts_],
                                         start=(kc == 0), stop=(kc == KD - 1))
                            s1 = sl[it % 2]
                            it += 1
                            S.act(s1, p1[:, :], AF.Silu)
                            S.tt('dve', uT[:, fo, ts_], s1, p3[:, :], ALU.mult)
                for t in range(NT):
                    for h2 in range(2):
                        p = PS()
                        for f in range(NF):
                            S.matmul(p[:, :], uT[:, f, t * 128:(t + 1) * 128], w2g[:, f, h2 * 512:(h2 + 1) * 512],
                                     start=(f == 0), stop=(f == NF - 1))
                        S.tt('dve', xres[:, t, h2 * 512:(h2 + 1) * 512], xres[:, t, h2 * 512:(h2 + 1) * 512],
                             p[:, :], ALU.add)
            A.release(m)

        cmask_b = cb[:, 386:514] if False else None
        cb2 = sb("cstb2", [128, 384], BF16)
        cmask_b = cb2[:, 0:128]
        hpad_b = cb2[:, 128:384].rearrange("p (h c) -> p h c", c=128)
        S.copy('dve', cmask_b, C('cmask'))
        S.memset('dve', hpad_b[:, :, :], 0.0)
        S.memset('dve', hpad_b[:, 0, 0:64], 1.0)
        S.memset('dve', hpad_b[:, 1, 64:128], 1.0)

        def tap_tm(name, src_tm):
            pass

        for l in range(layers):
            S.dma('sp', prow[:, :], prow_d[l], 'c3')
            rms_to_hT(l, 'g_mix')
            for n, fn in enumerate((rwkv, lru, mla)):
                m = A.mark()
                yT = A.alloc([4, SEQ], BF16)
                if ('rwkv', 'lru', 'mla')[n] not in phases:
                    A.release(m)
                    continue
                fn(l, yT)
                if ('y%d_%d' % (n, l)) in tap_d:
                    yf = A.alloc([256], F32)
                    for fc in range(4):
                        for q8 in range(8):
                            S.ts('dve', yf, yT[:, fc, q8 * 256:(q8 + 1) * 256], 1e30, -1e30, ALU.min, ALU.max)
                            o = S.dma('sp', tap_d['y%d_%d' % (n, l)][fc * 128:(fc + 1) * 128, q8 * 256:(q8 + 1) * 256], yf, 'tap')
                            outs.append(o)
                if 'merge' in phases:
                    merge(l, n, yT)
                A.release(m)
            if ('x1_%d' % l) in tap_d:
                for t in range(NT):
                    outs.append(S.dma('sp', tap_d['x1_%d' % l][t * 128:(t + 1) * 128, :], xres[:, t, :], 'tap'))
            if 'xattn' in phases:
                xattn(l)
            if ('x2_%d' % l) in tap_d:
                for t in range(NT):
                    outs.append(S.dma('sp', tap_d['x2_%d' % l][t * 128:(t + 1) * 128, :], xres[:, t, :], 'tap'))
            if 'ffn' in phases:
                ffn(l)
        for t in range(NT):
            outs.append(S.dma('sp', out_d[t * 128:(t + 1) * 128, :], xres[:, t, :], force=True))
        S.fence('sp', outs)
        S.emit()
        build.stats = (len(S.ops), A.peak)
    return nc


def _cols(v):
    v = np.ascontiguousarray(v, dtype=np.float32).reshape(-1)
    return v.reshape(-1, 128).T


def _consts():
    c = np.zeros((128, NCS), np.float32)

    def put(name, a):
        o, w = CS[name]
        c[:a.shape[0], o:o + w] = a
    put('ident', np.eye(128, dtype=np.float32))
    put('identblk', np.concatenate([np.eye(64), np.eye(64)], 0).astype(np.float32))
    put('mask_s', np.tril(np.ones((64, 64), np.float32), -1))
    put('maskT_s', np.triu(np.ones((64, 64), np.float32), 1))
    put('cmask', np.triu(np.ones((128, 128), np.float32), 0))
    bo = np.zeros((128, 128), np.float32)
    bo[:64, :64] = 1
    bo[64:, 64:] = 1
    put('blockones', bo)
    hs = np.zeros((128, 2), np.float32)
    hs[:64, 0] = 1
    hs[64:, 1] = 1
    put('headsel', hs)
    cm = np.ones((128, 256), np.float32)
    cm[:, ::64] = 0
    put('chunkmask', cm)
    inv = (10000.0 ** (-np.arange(0, 32, 2, dtype=np.float32) / 32)).astype(np.float32)
    put('invfreq', np.broadcast_to(inv, (128, 16)))
    put('ones', np.ones((128, 128), np.float32))
    return c


def _pack(inputs):
    pv = np.zeros((DEPTH, 128, NPV), np.float32)
    pr = np.zeros((DEPTH, 128, NPR), np.float32)
    src = {'g_mix': 'norm_mix', 'g_xa': 'norm_xattn', 'g_mem': 'norm_mem', 'g_ffn': 'norm_ffn',
           'b_gate': 'b_gate', 'mu': 'rwkv_mu', 'w0': 'rwkv_w0', 'a0': 'rwkv_a0', 'k_k': 'rwkv_k_k',
           'k_a': 'rwkv_k_a', 'r_k': 'rwkv_r_k', 'conv_w': 'lru_conv_w', 'conv_b': 'lru_conv_b',
           'ba': 'lru_ba', 'bx': 'lru_bx', 'lam': 'lru_lambda', 'q_norm': 'mla_q_norm',
           'kv_norm': 'mla_kv_norm', 'xa_qg': 'xa_q_gain', 'xa_kg': 'xa_k_gain'}
    for l in range(DEPTH):
        for k, (o, w) in PV.items():
            pv[l, :, o:o + w] = _cols(inputs[src[k]][l])
        for k, s in (('ln_g', 'rwkv_ln_g'), ('ln_b', 'rwkv_ln_b'), ('q_gain', 'mla_q_gain'), ('k_gain', 'mla_k_gain')):
            o, w = PR[k]
            pr[l, :, o:o + w] = np.broadcast_to(np.asarray(inputs[s][l], np.float32).reshape(1, w), (128, w))
    return pv, pr


_WNAMES = ['w_in', 'rwkv_w_up', 'rwkv_a_up', 'rwkv_g_up', 'lru_wa', 'lru_wx', 'mla_w_uq', 'mla_w_ukv',
           'w_branch', 'w_out', 'xa_w_q', 'xa_w_kv', 'xa_w_o', 'ffn_w1', 'ffn_w3', 'ffn_w2']


def make_in_maps(inputs, cores):
    pv, pr = _pack(inputs)
    cst = _consts()
    shared = {k: np.ascontiguousarray(inputs[k], dtype=np.float32) for k in _WNAMES}
    shared.update({'cst': cst, 'pvec': pv, 'prow': pr})
    maps = []
    for b in cores:
        mm = dict(shared)
        mm['x'] = np.ascontiguousarray(inputs['x'][b], dtype=np.float32)
        mm['mem'] = np.ascontiguousarray(inputs['mem'][b], dtype=np.float32)
        mm['pos'] = np.ascontiguousarray(np.asarray(inputs['positions'][b], np.int32).reshape(NT, 128).T)
        maps.append(mm)
    return maps


def kernel(**inputs):
    inputs = {k: np.asarray(v) for k, v in inputs.items()}
    nc = bass.Bass("TRN2", target_bir_lowering=False)
    build(nc)
    maps = make_in_maps(inputs, list(range(8)))
    res = run_bass_kernel_spmd(nc, maps, core_ids=list(range(8)))
    return np.stack([np.asarray(r["out"], np.float32) for r in res.results], axis=0)
```
